# Optimizing a Trainium2 kernel written in Bass

```python
import jax, jax.numpy as jnp
from jax import lax
import numpy as np

D_MODEL = 1024
BATCH = 8
SEQ = 2048
DEPTH = 1
DEC_BATCH = 16
DEC_SEQ = 32
PAST_LEN = 2048

CHUNK = 64
D_MIX = D_MODEL
D_CONV = D_MIX // 2
CONV_WIDTH = 31
CONV_BUF = CONV_WIDTH - 1
D_HGRN = D_MIX - D_CONV
HGRN_HEADS = 4
HGRN_DK = D_HGRN // HGRN_HEADS
HGRN_DV = D_HGRN // HGRN_HEADS
D_IN = 2 * D_CONV + 4 * D_HGRN
SPLITS = (D_CONV, 2 * D_CONV, 2 * D_CONV + D_HGRN, 2 * D_CONV + 2 * D_HGRN, 2 * D_CONV + 3 * D_HGRN)
N_MEM = 256
MEM_HEADS = 4
MEM_HEAD_DIM = D_MODEL // MEM_HEADS
D_FF = 2816
EPS = 1e-6

kernel_name = 'hymba_conformer_hgrn2_streaming_step'


def rms_norm(x, g):
    xf = x.astype(jnp.float32)
    y = xf * lax.rsqrt(jnp.mean(xf * xf, axis=-1, keepdims=True) + EPS)
    return (y * g.astype(jnp.float32)).astype(x.dtype)


def layer_norm(x, g, b):
    xf = x.astype(jnp.float32)
    mu = jnp.mean(xf, axis=-1, keepdims=True)
    var = jnp.mean(jnp.square(xf - mu), axis=-1, keepdims=True)
    y = (xf - mu) * lax.rsqrt(var + EPS)
    return (y * g.astype(jnp.float32) + b.astype(jnp.float32)).astype(x.dtype)


def swiglu(x, w_gate, w_up, w_down):
    return (jax.nn.silu(x @ w_gate) * (x @ w_up)) @ w_down


def hgrn_chunk(s0, q, k, v, logf):
    c = q.shape[1]
    g_cum = jnp.cumsum(logf, axis=1)
    g_last = g_cum[:, -1]
    q_dec = q * jnp.exp(g_cum)
    k_dec = k * jnp.exp(-g_cum)
    a = jnp.einsum('bthk,bshk->bhts', q_dec, k_dec)
    a = jnp.where(jnp.tril(jnp.ones((c, c), dtype=bool)), a, 0.0)
    o = jnp.einsum('bhts,bshv->bthv', a, v) + jnp.einsum('bthk,bhkv->bthv', q_dec, s0)
    k_end = k * jnp.exp(g_last[:, None] - g_cum)
    s1 = s0 * jnp.exp(g_last)[..., None] + jnp.einsum('bshk,bshv->bhkv', k_end, v)
    return s1, o


def hgrn_recurrence(s0, q, k, v, logf):
    b, t = q.shape[:2]
    c = min(CHUNK, t)
    n = t // c

    def to_blocks(a):
        return a.reshape(b, n, c, *a.shape[2:]).swapaxes(0, 1)

    def step(s, xs):
        return hgrn_chunk(s, *xs)

    s1, o = lax.scan(step, s0, (to_blocks(q), to_blocks(k), to_blocks(v), to_blocks(logf)))
    o = o.swapaxes(0, 1).reshape(b, t, HGRN_HEADS, HGRN_DV)
    return s1, o


def conv_group(c_val, c_gate, buf, w_dw, b_dw, ln_g, ln_b):
    u = c_val * jax.nn.sigmoid(c_gate)
    u_ext = jnp.concatenate([buf.astype(u.dtype), u], axis=1)
    y = lax.conv_general_dilated(
        u_ext, w_dw[:, None, :].astype(u.dtype), window_strides=(1,), padding='VALID',
        dimension_numbers=('NWC', 'WIO', 'NWC'), feature_group_count=D_CONV) + b_dw
    y = jax.nn.silu(layer_norm(y, ln_g, ln_b))
    return y, u_ext[:, -CONV_BUF:]


def hgrn_group(hq, hf, hi, hg, s0, lb, g_norm):
    b, t, _ = hq.shape

    def heads(a):
        return a.astype(jnp.float32).reshape(b, t, HGRN_HEADS, -1)

    lbh = lb.reshape(HGRN_HEADS, HGRN_DK)
    f = lbh + (1.0 - lbh) * jax.nn.sigmoid(heads(hf))
    q = jax.nn.silu(heads(hq))
    s1, o = hgrn_recurrence(s0.astype(jnp.float32), q, 1.0 - f, heads(hi), jnp.log(f))
    o = rms_norm(o, g_norm) * jax.nn.silu(heads(hg))
    return o.reshape(b, t, D_HGRN).astype(hq.dtype), s1


def mem_kv(mem, g, wk, wv):
    b = mem.shape[0]
    m = rms_norm(mem, g)
    k = (m @ wk).reshape(b, N_MEM, MEM_HEADS, MEM_HEAD_DIM)
    v = (m @ wv).reshape(b, N_MEM, MEM_HEADS, MEM_HEAD_DIM)
    return k, v


def mem_attend(h, k, v, wq, wo):
    b, t, _ = h.shape
    q = (h @ wq).reshape(b, t, MEM_HEADS, MEM_HEAD_DIM)
    s = jnp.einsum('bthd,bmhd->bhtm', q, k).astype(jnp.float32) * (MEM_HEAD_DIM ** -0.5)
    p = jax.nn.softmax(s, axis=-1).astype(v.dtype)
    o = jnp.einsum('bhtm,bmhd->bthd', p, v).reshape(b, t, D_MODEL)
    return o @ wo


def encoder_layer(x, conv_buf, s0, mk, mv, lb, w):
    x = x + 0.5 * swiglu(rms_norm(x, w['ffn1_norm']), w['ffn1_w_gate'], w['ffn1_w_up'], w['ffn1_w_down'])
    z = rms_norm(x, w['mix_norm']) @ w['w_in']
    c_val, c_gate, hq, hf, hi, hg = jnp.split(z, SPLITS, axis=-1)
    y_conv, new_buf = conv_group(c_val, c_gate, conv_buf, w['conv_dw'], w['conv_dw_b'],
                                 w['conv_ln_g'], w['conv_ln_b'])
    y_hgrn, new_s = hgrn_group(hq, hf, hi, hg, s0, lb, w['hgrn_norm'])
    x = x + jnp.concatenate([y_conv, y_hgrn], axis=-1) @ w['w_out']
    x = x + mem_attend(rms_norm(x, w['xattn_norm']), mk, mv, w['xattn_wq'], w['xattn_wo'])
    x = x + 0.5 * swiglu(rms_norm(x, w['ffn2_norm']), w['ffn2_w_gate'], w['ffn2_w_up'], w['ffn2_w_down'])
    return x, new_buf, new_s


def setup_inputs(seed: int = 0) -> dict:
    key = jax.random.key(seed)
    ks = iter(jax.random.split(key, 40))

    def nrm(shape, scale):
        return jax.random.normal(next(ks), shape, jnp.float32) * scale

    def gain(shape):
        return 1.0 + nrm(shape, 0.02)

    return {
        'x_prompt': nrm((BATCH, SEQ, D_MODEL), 1.0),
        'x_sample': nrm((DEC_BATCH, DEC_SEQ, D_MODEL), 1.0),
        'mem_prompt': nrm((BATCH, N_MEM, D_MODEL), 1.0),
        'state_conv': nrm((DEPTH, DEC_BATCH, CONV_BUF, D_CONV), 0.5),
        'state_hgrn': nrm((DEPTH, DEC_BATCH, HGRN_HEADS, HGRN_DK, HGRN_DV), 0.5),
        'cache_mem_k': nrm((DEPTH, DEC_BATCH, N_MEM, MEM_HEADS, MEM_HEAD_DIM), 1.0),
        'cache_mem_v': nrm((DEPTH, DEC_BATCH, N_MEM, MEM_HEADS, MEM_HEAD_DIM), 1.0),
        'ffn1_norm': gain((DEPTH, D_MODEL)),
        'ffn1_w_gate': nrm((DEPTH, D_MODEL, D_FF), D_MODEL ** -0.5),
        'ffn1_w_up': nrm((DEPTH, D_MODEL, D_FF), D_MODEL ** -0.5),
        'ffn1_w_down': nrm((DEPTH, D_FF, D_MODEL), D_FF ** -0.5),
        'mix_norm': gain((DEPTH, D_MODEL)),
        'w_in': nrm((DEPTH, D_MODEL, D_IN), D_MODEL ** -0.5),
        'conv_dw': nrm((DEPTH, CONV_WIDTH, D_CONV), CONV_WIDTH ** -0.5),
        'conv_dw_b': nrm((DEPTH, D_CONV), 0.02),
        'conv_ln_g': gain((DEPTH, D_CONV)),
        'conv_ln_b': nrm((DEPTH, D_CONV), 0.02),
        'hgrn_lb': nrm((DEPTH + 1, D_HGRN), 0.1),
        'hgrn_norm': gain((DEPTH, HGRN_HEADS, HGRN_DV)),
        'w_out': nrm((DEPTH, D_MIX, D_MODEL), D_MIX ** -0.5),
        'mem_norm': gain((DEPTH, D_MODEL)),
        'mem_wk': nrm((DEPTH, D_MODEL, D_MODEL), D_MODEL ** -0.5),
        'mem_wv': nrm((DEPTH, D_MODEL, D_MODEL), D_MODEL ** -0.5),
        'xattn_norm': gain((DEPTH, D_MODEL)),
        'xattn_wq': nrm((DEPTH, D_MODEL, D_MODEL), D_MODEL ** -0.5),
        'xattn_wo': nrm((DEPTH, D_MODEL, D_MODEL), D_MODEL ** -0.5),
        'ffn2_norm': gain((DEPTH, D_MODEL)),
        'ffn2_w_gate': nrm((DEPTH, D_MODEL, D_FF), D_MODEL ** -0.5),
        'ffn2_w_up': nrm((DEPTH, D_MODEL, D_FF), D_MODEL ** -0.5),
        'ffn2_w_down': nrm((DEPTH, D_FF, D_MODEL), D_FF ** -0.5),
        'final_norm': gain((D_MODEL,)),
    }


def reference(x_prompt, x_sample, mem_prompt, state_conv, state_hgrn, cache_mem_k, cache_mem_v,
              ffn1_norm, ffn1_w_gate, ffn1_w_up, ffn1_w_down, mix_norm, w_in, conv_dw, conv_dw_b,
              conv_ln_g, conv_ln_b, hgrn_lb, hgrn_norm, w_out, mem_norm, mem_wk, mem_wv,
              xattn_norm, xattn_wq, xattn_wo, ffn2_norm, ffn2_w_gate, ffn2_w_up, ffn2_w_down,
              final_norm):
    lb_all = jnp.cumsum(jax.nn.softmax(hgrn_lb.astype(jnp.float32), axis=0), axis=0)
    bp = x_prompt.shape[0]
    xp, xs = x_prompt, x_sample
    conv_p, hgrn_p, mk_p, mv_p, conv_s, hgrn_s = [], [], [], [], [], []
    for l in range(DEPTH):
        w = {
            'ffn1_norm': ffn1_norm[l], 'ffn1_w_gate': ffn1_w_gate[l], 'ffn1_w_up': ffn1_w_up[l],
            'ffn1_w_down': ffn1_w_down[l], 'mix_norm': mix_norm[l], 'w_in': w_in[l],
            'conv_dw': conv_dw[l], 'conv_dw_b': conv_dw_b[l], 'conv_ln_g': conv_ln_g[l],
            'conv_ln_b': conv_ln_b[l], 'hgrn_norm': hgrn_norm[l], 'w_out': w_out[l],
            'xattn_norm': xattn_norm[l], 'xattn_wq': xattn_wq[l], 'xattn_wo': xattn_wo[l],
            'ffn2_norm': ffn2_norm[l], 'ffn2_w_gate': ffn2_w_gate[l], 'ffn2_w_up': ffn2_w_up[l],
            'ffn2_w_down': ffn2_w_down[l],
        }
        mk, mv = mem_kv(mem_prompt, mem_norm[l], mem_wk[l], mem_wv[l])
        buf0 = jnp.zeros((bp, CONV_BUF, D_CONV), xp.dtype)
        s0 = jnp.zeros((bp, HGRN_HEADS, HGRN_DK, HGRN_DV), jnp.float32)
        xp, buf_p, s_p = encoder_layer(xp, buf0, s0, mk, mv, lb_all[l], w)
        xs, buf_s, s_s = encoder_layer(xs, state_conv[l], state_hgrn[l], cache_mem_k[l],
                                       cache_mem_v[l], lb_all[l], w)
        conv_p.append(buf_p)
        hgrn_p.append(s_p.astype(x_prompt.dtype))
        mk_p.append(mk)
        mv_p.append(mv)
        conv_s.append(buf_s.astype(state_conv.dtype))
        hgrn_s.append(s_s.astype(state_hgrn.dtype))
    y_prompt = rms_norm(xp, final_norm)
    y_sample = rms_norm(xs, final_norm)
    return (y_prompt, y_sample, jnp.stack(conv_p), jnp.stack(hgrn_p), jnp.stack(mk_p),
            jnp.stack(mv_p), jnp.stack(conv_s), jnp.stack(hgrn_s))
```

```python
import os
import numpy as np
from contextlib import ExitStack
import concourse.bass as bass
import concourse.mybir as mybir
from concourse.bass_utils import run_bass_kernel_spmd

F32 = mybir.dt.float32
BF16 = mybir.dt.bfloat16
AF = mybir.ActivationFunctionType
ALU = mybir.AluOpType
AX = mybir.AxisListType

D = 1024
KC = 8
FF = 2816
FC = 22
NPR = 2048
NSM = 32
T = NPR + 2 * NSM
NT = 17
TP = NT * 128
DIN = 3072
NMEM = 256
EPS = 1e-6
N_CORES = 8


class Chan:
    def __init__(self, name, sem):
        self.name = name
        self.sem = sem
        self.count = 0


class Sched:
    ENGS = ("pe", "act", "dve", "pool", "sp")

    def __init__(self, nc, stack):
        self.nc = nc
        self.stack = stack
        self.sem = {}
        self.cnt = {}
        for e in ("pe", "act", "dve", "pool"):
            self.sem[("eng", e)] = stack.enter_context(nc.semaphore("s_" + e))
            self.cnt[e] = 0
        self.prog = {e: [] for e in self.ENGS}
        self.waited = {e: {} for e in self.ENGS}
        self.res = {}
        self.chans = {}
        self.nops = {e: 0 for e in self.ENGS}
        self.rec = None

    def record(self, fn):
        assert self.rec is None
        self.rec = []
        fn()
        out, self.rec = self.rec, None
        return out

    def emit_interleaved(self, *lists):
        lists = [l for l in lists if l]
        pos = [0] * len(lists)
        while True:
            best, bf = None, None
            for i, l in enumerate(lists):
                if pos[i] < len(l):
                    f = pos[i] / len(l)
                    if bf is None or f < bf:
                        best, bf = i, f
            if best is None:
                break
            kind, args, kw = lists[best][pos[best]]
            pos[best] += 1
            getattr(self, kind)(*args, **kw)

    def chan(self, name):
        if name not in self.chans:
            sem = self.stack.enter_context(self.nc.semaphore("c_" + name))
            self.chans[name] = Chan(name, sem)
            self.sem[("chan", name)] = sem
        return self.chans[name]

    PSUM_KEYS = {"pg", "pu", "pd", "pT", "a_pT", "a_pA", "a_pS", "a_pO",
                 "c_pT", "c_pp0", "c_pp1", "c_pc0", "c_pc1", "c_pc2", "c_pc3",
                 "hb0", "hb1", "hb2", "hb3", "hb4", "hb5", "hb6", "hb7"}

    def _is_psum(self, key):
        base = key if isinstance(key, str) else key[0]
        return base in self.PSUM_KEYS

    def _deps(self, reads, writes, eng=None):
        deps = {}

        def add(k, v):
            if deps.get(k, 0) < v:
                deps[k] = v
        for r in reads:
            ent = self.res.get(r)
            if ent is not None and ent[0] is not None:
                add(*ent[0])
            if ent is not None and self._is_psum(r):
                for k, v in ent[1].items():
                    if k != ("eng", eng):
                        add(k, v)
        for w in writes:
            ent = self.res.get(w)
            if ent is not None:
                if ent[0] is not None:
                    add(*ent[0])
                for k, v in ent[1].items():
                    add(k, v)
        return deps

    def _emit_waits(self, eng, deps):
        for k, v in deps.items():
            if k == ("eng", "pe") and eng == "pe":
                continue
            if k[0] == "eng":
                assert v <= self.cnt[k[1]], f"wait on future signal {k} {v} > {self.cnt[k[1]]}"
            else:
                assert v <= self.chans[k[1]].count
            if self.waited[eng].get(k, 0) >= v:
                continue
            self.waited[eng][k] = v
            self.prog[eng].append(("wait", self.sem[k], v))

    def _record(self, tok, reads, writes):
        for r in reads:
            ent = self.res.setdefault(r, [None, {}])
            if ent[1].get(tok[0], 0) < tok[1]:
                ent[1][tok[0]] = tok[1]
        for w in writes:
            self.res[w] = [tok, {}]

    def op(self, eng, fn, reads=(), writes=(), signal=True):
        if self.rec is not None:
            self.rec.append(("op", (eng, fn), dict(reads=list(reads), writes=list(writes), signal=signal)))
            return None
        deps = self._deps(reads, writes, eng)
        self._emit_waits(eng, deps)
        if signal:
            self.cnt[eng] += 1
            tok = (("eng", eng), self.cnt[eng])
            self.prog[eng].append(("op", fn, self.sem[("eng", eng)], 1))
        else:
            tok = (("eng", eng), self.cnt[eng] + 1)
            self.prog[eng].append(("op", fn, None, 0))
        self.nops[eng] += 1
        self._record(tok, reads, writes)
        return tok

    def dma(self, q, out, in_, chan, reads=(), writes=(), **kw):
        if self.rec is not None:
            self.rec.append(("dma", (q, out, in_, chan), dict(reads=list(reads), writes=list(writes), **kw)))
            return None
        c = self.chan(chan)
        deps = self._deps(reads, writes)
        self._emit_waits(q, deps)
        c.count += 16
        tok = (("chan", chan), c.count)
        self.prog[q].append(("op", lambda e, out=out, in_=in_, kw=kw: e.dma_start(out=out, in_=in_, **kw), c.sem, 16))
        self.nops[q] += 1
        self._record(tok, reads, writes)
        return tok

    def barrier(self):
        for e in self.ENGS:
            for other in ("pe", "act", "dve", "pool"):
                if other == e or not self.cnt[other]:
                    continue
                k = ("eng", other)
                if self.waited[e].get(k, 0) < self.cnt[other]:
                    self.waited[e][k] = self.cnt[other]
                    self.prog[e].append(("wait", self.sem[k], self.cnt[other]))
            for name, c in self.chans.items():
                k = ("chan", name)
                if c.count and self.waited[e].get(k, 0) < c.count:
                    self.waited[e][k] = c.count
                    self.prog[e].append(("wait", c.sem, c.count))

    def build(self):
        for name, c in self.chans.items():
            if c.count:
                self.prog["sp"].append(("wait", c.sem, c.count))
        for e in ("pe", "act", "dve", "pool"):
            if self.cnt[e]:
                self.prog["sp"].append(("wait", self.sem[("eng", e)], self.cnt[e]))
        prog = self.prog

        def replay(name):
            def f(e):
                for ent in prog[name]:
                    if ent[0] == "wait":
                        e.wait_ge(ent[1], ent[2])
                    else:
                        ins = ent[1](e)
                        if ent[2] is not None:
                            ins.then_inc(ent[2], ent[3])
            return f

        with self.nc.Block() as block:
            block.tensor(replay("pe"))
            block.scalar(replay("act"))
            block.vector(replay("dve"))
            block.gpsimd(replay("pool"))
            block.sync(replay("sp"))


def build_nc(stop_after=None):
    nc = bass.Bass("TRN2", target_bir_lowering=False)

    def din(name, shape):
        return nc.dram_tensor(name, list(shape), F32, kind="ExternalInput").ap()

    def dout(name, shape):
        return nc.dram_tensor(name, list(shape), F32, kind="ExternalOutput").ap()

    x_in = din("x_in", [T, D])
    mem_in = din("mem_in", [NMEM, D])
    st_conv = din("st_conv", [2, 30, 512])
    st_hgrn = din("st_hgrn", [2, 4, 128, 128])
    ck_in = din("ck_in", [2, NMEM, D])
    cv_in = din("cv_in", [2, NMEM, D])
    w1g = din("w1g", [D, FF]); w1u = din("w1u", [D, FF]); w1d = din("w1d", [FF, D])
    w2g = din("w2g", [D, FF]); w2u = din("w2u", [D, FF]); w2d = din("w2d", [FF, D])
    w_in = din("w_in", [D, DIN]); w_out = din("w_out", [D, D])
    wk_d = din("wk", [D, D]); wv_d = din("wv", [D, D]); wq_d = din("wq", [D, D]); wo_d = din("wo", [D, D])
    n_ffn1 = din("n_ffn1", [D]); n_mix = din("n_mix", [D]); n_mem = din("n_mem", [D])
    n_xattn = din("n_xattn", [D]); n_ffn2 = din("n_ffn2", [D]); n_final = din("n_final", [D])
    cw_d = din("cw", [512, 31])
    cb_d = din("cb", [128, 4]); clg_d = din("clg", [128, 4]); clb_d = din("clb", [128, 4])
    lb2_d = din("lb2", [128, 8]); gn_d = din("gn", [128, 4])
    ident_d = din("ident", [128, 128])
    tri64_d = din("tri64", [128, 128]); tri32_d = din("tri32", [64, 64])
    rst64_d = din("rst64", [128, 512]); rst32_d = din("rst32", [128, 64])

    y_out = dout("y_out", [T, D])
    convp_out = dout("convp_out", [30, 512])
    hgrnp_out = dout("hgrnp_out", [4, 128, 128])
    mk_out = dout("mk_out", [NMEM, D])
    mv_out = dout("mv_out", [NMEM, D])
    convs_out = dout("convs_out", [2, 30, 512])
    hgrns_out = dout("hgrns_out", [2, 4, 128, 128])

    with ExitStack() as st:
        S = Sched(nc, st)

        def sb(ctx, name, shape, dt):
            return ctx.enter_context(nc.sbuf_tensor("sb_" + name, list(shape), dt))

        def ps(ctx, name, shape, dt=F32):
            return ctx.enter_context(nc.psum_tensor("ps_" + name, list(shape), dt))

        def A(fn, r, w, **k):
            return S.op("act", fn, reads=r, writes=w, **k)

        def V(fn, r, w, **k):
            return S.op("dve", fn, reads=r, writes=w, **k)

        def G(fn, r, w, **k):
            return S.op("pool", fn, reads=r, writes=w, **k)

        def P(fn, r, w, signal=True):
            return S.op("pe", fn, reads=r, writes=w, signal=signal)

        xres = sb(st, "xres", [128, NT, D], F32)
        gb = sb(st, "gb", [128, D], F32)
        idf = sb(st, "idf", [128, 128], F32)
        idb = sb(st, "idb", [128, 128], BF16)
        ss = sb(st, "ss", [128, NT], F32)
        lnt = sb(st, "lnt", [128, NT], F32)
        rstd = sb(st, "rstd", [128, NT], F32)
        sqj = sb(st, "sqj", [128, D], BF16)
        xnb = [sb(st, "xnb0", [128, D], BF16)] * 2

        XK = [("x", t) for t in range(NT)]

        S.dma("sp", idf[:], ident_d, "ld_idf", writes=["idf"])
        def load_x(g4, after=()):
            S.dma("sp", xres[:, 4 * g4:4 * g4 + 4, :],
                  x_in[g4 * 512:(g4 + 1) * 512, :].rearrange("(t p) d -> p t d", p=128),
                  f"ldx{g4}", writes=[("x", 4 * g4 + i) for i in range(4)], reads=list(after))

        def load_x_rest(after):
            for g4 in range(1, 4):
                load_x(g4, after if g4 == 1 else ())
            S.dma("sp", xres[0:64, 16, :], x_in[2048:2112, :], "ldx4", writes=[("x", 16)], reads=[("xpad",)])

        load_x(0)
        V(lambda e: e.memset(xres[64:128, 16, :], 0.0), [], [("xpad",)])
        V(lambda e: e.tensor_copy(out=idb[:], in_=idf[:]), ["idf"], ["idb"])

        tile_rows = lambda t: 64 if t == 16 else 128

        def load_gain(g_dram):
            S.dma("sp", gb[:], g_dram.partition_broadcast(128), "ld_gb", writes=["gb"])

        def norm_stats():
            for t0 in range(0, NT, 4):
                t1 = min(NT, t0 + 4)
                ck = t0 // 4
                for t in range(t0, t1):
                    A(lambda e, t=t: e.activation(out=sqj[:], in_=xres[:, t, :], func=AF.Square, accum_out=ss[:, t:t + 1]),
                      [("x", t), ("xpad",)], ["sqj", ("ss", t)])
                A(lambda e, t0=t0, t1=t1: e.activation(out=lnt[:, t0:t1], in_=ss[:, t0:t1], func=AF.Ln, bias=EPS, scale=1.0 / D),
                  [("ss", t) for t in range(t0, t1)], [("lnt", ck)])
                A(lambda e, t0=t0, t1=t1: e.activation(out=rstd[:, t0:t1], in_=lnt[:, t0:t1], func=AF.Exp, scale=-0.5), [("lnt", ck)], [("rstd", ck)])

        def norm_tile_T(t, pT, pTkey, dst_ap, dst_keys, par, xb1=None, cp_eng="act"):
            if par == 0:
                xb, xk = xnb[0], "xnb"
            elif xb1 is not None:
                xb, xk = xb1, "xnb1"
            else:
                xb, xk = sqj, "sqj"
            V(lambda e: e.scalar_tensor_tensor(out=xb[:], in0=xres[:, t, :], scalar=rstd[:, t:t + 1], in1=gb[:],
                                               op0=ALU.mult, op1=ALU.mult),
              [("x", t), ("xpad",), ("rstd", t // 4), "gb"], [xk])
            for k in range(KC):
                P(lambda e, k=k: e.transpose(out=pT[:, k * 128:(k + 1) * 128], in_=xb[:, k * 128:(k + 1) * 128], identity=idb[:]),
                  [xk, "idb"], [pTkey], signal=(k == KC - 1))
            if cp_eng == "act":
                A(lambda e: e.copy(out=dst_ap, in_=pT[:, 0:1024].rearrange("p (k t) -> p k t", k=KC)), [pTkey], dst_keys)
            else:
                V(lambda e: e.tensor_copy(out=dst_ap, in_=pT[:, 0:1024].rearrange("p (k t) -> p k t", k=KC)), [pTkey], dst_keys)

        def ffn_phase(tag, g_dram, wg_d, wu_d, wd_d, final=False):
            GF = 4
            groups = []
            f0 = 0
            while f0 < FC:
                gf = min(GF, FC - f0)
                groups.append((f0, gf))
                f0 += gf
            NS = 2
            tgs = [(0, 512), (512, 512), (1024, 512), (1536, 512), (2048, 64)]
            with ExitStack() as ph:
                xnT = sb(ph, f"xnT_{tag}", [128, KC, TP], BF16)
                xnb1 = sb(ph, f"xnb1_{tag}", [128, D], BF16)
                wgs = [sb(ph, f"wg_{tag}{s}", [128, KC, GF * 128], BF16) for s in range(NS)]
                wus = [sb(ph, f"wu_{tag}{s}", [128, KC, GF * 128], BF16) for s in range(NS)]
                wds = [sb(ph, f"wd_{tag}{s}", [128, GF, D], BF16) for s in range(NS)]
                sg = [sb(ph, f"sg_{tag}{i}", [128, 512], F32) for i in range(2)]
                hT = [sb(ph, f"hT_{tag}{i}", [128, GF, 512], BF16) for i in range(2)]
                pT = [ps(ph, f"pT_{tag}{i}", [128, 1024], BF16) for i in range(2)]
                pg = [ps(ph, f"pg_{tag}{i}", [128, 512]) for i in range(2)]
                pu = [ps(ph, f"pu_{tag}{i}", [128, 512]) for i in range(2)]
                pd = [ps(ph, f"pd_{tag}{i}", [128, 512]) for i in range(2)]

                wg_v = wg_d.rearrange("(k p) f -> p k f", p=128)
                wu_v = wu_d.rearrange("(k p) f -> p k f", p=128)
                wd_v = wd_d.rearrange("(f p) d -> p f d", p=128)

                def load_group(gi, after=()):
                    f0, gf = groups[gi]
                    s = gi % NS
                    S.dma("pool", wgs[s][:, :, 0:gf * 128], wg_v[:, :, f0 * 128:(f0 + gf) * 128], f"ldwg{s}", writes=[("wg", tag, s)], reads=list(after))
                    S.dma("pool", wus[s][:, :, 0:gf * 128], wu_v[:, :, f0 * 128:(f0 + gf) * 128], f"ldwu{s}", writes=[("wu", tag, s)])
                    S.dma("pool", wds[s][:, 0:gf, :], wd_v[:, f0:f0 + gf, :], f"ldwd{s}", writes=[("wd", tag, s)])

                load_group(0)
                if tag == "f1":
                    load_x_rest([("wg", tag, 0), ("wu", tag, 0), ("wd", tag, 0)])
                load_group(1, after=[("wg", tag, 0), ("wu", tag, 0), ("wd", tag, 0)])
                load_gain(g_dram)
                norm_stats()
                first_tiles = tgs[0][1] // 128
                for t in range(first_tiles):
                    norm_tile_T(t, pT[t % 2], ("pT", tag, t % 2), xnT[:, :, t * 128:(t + 1) * 128], [("xnT", tag, t)], t % 2, xb1=xnb1, cp_eng=("dve" if t % 2 else "act"))

                if final:
                    gb2 = sb(ph, "gb2", [128, D], F32)
                    ss2 = sb(ph, "ss2", [128, NT], F32)
                    ln2 = sb(ph, "ln2", [128, NT], F32)
                    rs2 = sb(ph, "rs2", [128, NT], F32)
                    yb = [sb(ph, f"yb{i}", [128, D], F32) for i in range(2)]
                    S.dma("sp", gb2[:], n_final.partition_broadcast(128), "ld_gb2", writes=["gb2"])

                def final_tail(ti):
                    c0, n = tgs[ti]
                    t0, t1 = c0 // 128, (c0 + n + 127) // 128
                    for t in range(t0, t1):
                        A(lambda e, t=t: e.activation(out=sqj[:], in_=xres[:, t, :], func=AF.Square, accum_out=ss2[:, t:t + 1]),
                          [("x", t)], ["sqj", ("ss2", t)])
                    A(lambda e: e.activation(out=ln2[:, t0:t1], in_=ss2[:, t0:t1], func=AF.Ln, bias=EPS, scale=1.0 / D),
                      [("ss2", t) for t in range(t0, t1)], [("ln2", ti)])
                    A(lambda e: e.activation(out=rs2[:, t0:t1], in_=ln2[:, t0:t1], func=AF.Exp, scale=-0.5), [("ln2", ti)], [("rs2", ti)])
                    for t in range(t0, t1):
                        par = t % 2
                        rows = tile_rows(t)
                        V(lambda e, t=t, par=par: e.scalar_tensor_tensor(out=yb[par][:], in0=xres[:, t, :], scalar=rs2[:, t:t + 1], in1=gb2[:],
                                                                        op0=ALU.mult, op1=ALU.mult),
                          [("x", t), ("rs2", ti), "gb2"], [("yb", par)])
                        S.dma("sp", y_out[t * 128:t * 128 + rows, :], yb[par][0:rows, :], f"st_y{par}", reads=[("yb", par)])

                items = [(gi, ti) for gi in range(len(groups)) for ti in range(len(tgs))]

                def GU(i):
                    gi, ti = items[i]
                    f0, gf = groups[gi]
                    s = gi % NS
                    c0, n = tgs[ti]
                    tiles = list(range(c0 // 128, (c0 + n + 127) // 128))
                    par = i % 2
                    rkg = [("wg", tag, s)] + [("xnT", tag, t) for t in tiles]
                    rku = [("wu", tag, s)] + [("xnT", tag, t) for t in tiles]
                    for fi in range(gf):
                        b = fi % 2
                        for k in range(KC):
                            P(lambda e, k=k, fi=fi, b=b: e.matmul(pg[b][:, 0:n], lhsT=wgs[s][:, k, fi * 128:(fi + 1) * 128],
                                                                 rhs=xnT[:, k, c0:c0 + n], start=(k == 0), stop=(k == KC - 1)),
                              rkg, [("pg", tag, b)], signal=(k == KC - 1))
                        for k in range(KC):
                            P(lambda e, k=k, fi=fi, b=b: e.matmul(pu[b][:, 0:n], lhsT=wus[s][:, k, fi * 128:(fi + 1) * 128],
                                                                 rhs=xnT[:, k, c0:c0 + n], start=(k == 0), stop=(k == KC - 1)),
                              rku, [("pu", tag, b)], signal=(k == KC - 1))
                        A(lambda e, b=b: e.activation(out=sg[b][:, 0:n], in_=pg[b][:, 0:n], func=AF.Silu),
                          [("pg", tag, b)], [("sg", tag, b)])
                        V(lambda e, b=b, fi=fi: e.tensor_tensor(out=hT[par][:, fi, 0:n], in0=sg[b][:, 0:n], in1=pu[b][:, 0:n], op=ALU.mult),
                          [("sg", tag, b), ("pu", tag, b)], [("hT", tag, par, fi)])

                dcount = [0]

                def DOWN(i):
                    gi, ti = items[i]
                    f0, gf = groups[gi]
                    s = gi % NS
                    c0, n = tgs[ti]
                    tiles = list(range(c0 // 128, (c0 + n + 127) // 128))
                    par = i % 2
                    for t in tiles:
                        lc = t * 128 - c0
                        rows = min(128, c0 + n - t * 128)
                        for dh in range(2):
                            b = dcount[0] % 2
                            dcount[0] += 1
                            for fi in range(gf):
                                P(lambda e, fi=fi, b=b, lc=lc, dh=dh, rows=rows: e.matmul(pd[b][0:rows, :], lhsT=hT[par][:, fi, lc:lc + rows],
                                                                                         rhs=wds[s][:, fi, dh * 512:(dh + 1) * 512],
                                                                                         start=(fi == 0), stop=(fi == gf - 1)),
                                  [("hT", tag, par, fi), ("wd", tag, s)], [("pd", tag, b)], signal=(fi == gf - 1))
                            V(lambda e, b=b, t=t, dh=dh, rows=rows: e.scalar_tensor_tensor(
                                out=xres[0:rows, t, dh * 512:(dh + 1) * 512], in0=pd[b][0:rows, :], scalar=0.5,
                                in1=xres[0:rows, t, dh * 512:(dh + 1) * 512], op0=ALU.mult, op1=ALU.add),
                              [("pd", tag, b), ("x", t)], [("x", t)])
                    if ti == len(tgs) - 1 and gi + NS < len(groups):
                        load_group(gi + NS)

                GU(0)
                for t in range(first_tiles, NT):
                    norm_tile_T(t, pT[t % 2], ("pT", tag, t % 2), xnT[:, :, t * 128:(t + 1) * 128], [("xnT", tag, t)], t % 2, xb1=xnb1, cp_eng=("dve" if t % 2 else "act"))
                for i in range(len(items)):
                    if i + 1 < len(items):
                        GU(i + 1)
                    DOWN(i)
                    if final and items[i][0] == len(groups) - 1:
                        final_tail(items[i][1])
                S.barrier()

        def conv_phase(ymc):
            NG = 512
            with ExitStack() as ph:
                winc = sb(ph, "winc", [128, KC, 1024], BF16)
                dg = sb(ph, "dg", [128, 124, 128], BF16)
                cw = sb(ph, "cw", [128, 4, 31], F32)
                cb = sb(ph, "cb", [128, 4], F32); clg = sb(ph, "clg", [128, 4], F32); clb = sb(ph, "clb", [128, 4], F32)
                onesln = sb(ph, "onesln", [128, 128], F32)
                xg = sb(ph, "c_xg", [128, KC, NG], BF16)
                cxnb1 = sb(ph, "c_xnb1", [128, D], BF16)
                uext = sb(ph, "uext", [128, 4, 30 + NG], F32)
                ubf = sb(ph, "ubf", [128, 4, 30 + NG], BF16)
                uextS = sb(ph, "uextS", [128, 4, 124], F32)
                sig = [sb(ph, f"sig{i}", [128, NG], F32) for i in range(2)]
                acc = sb(ph, "acc", [128, 4, NG], F32)
                ysq = sb(ph, "ysq", [128, 4, NG], F32)
                mean_sb = sb(ph, "mean_sb", [128, NG], F32)
                var_sb = sb(ph, "var_sb", [128, NG], F32)
                rs_sb = sb(ph, "rs_sb", [128, NG], F32)
                cvo = sb(ph, "cvo", [30, 512], F32)
                hist = [acc[0:30, 0, :], acc[0:30, 1, :]]

                pTs = [ps(ph, f"c_pT{i}", [128, 1024], BF16) for i in range(2)]
                pp = [ps(ph, f"c_pp{i}", [128, 512]) for i in range(2)]
                pc = [ps(ph, f"c_pc{i}", [128, 512]) for i in range(4)]
                PB = {0: (pp[0], pp[1], "c_pp0", "c_pp1"), 1: (pc[2], pc[3], "c_pc2", "c_pc3"),
                      2: (pp[0], pp[1], "c_pp0", "c_pp1"), 3: (pc[0], pc[1], "c_pc0", "c_pc1")}

                S.dma("pool", winc[:], w_in.rearrange("(k p) f -> p k f", p=128)[:, :, 0:1024], "ldwinc", writes=["winc"])
                S.dma("sp", cw[:], cw_d.rearrange("(cc p) j -> p cc j", p=128), "ld_cw", writes=["cw"])
                for (tt, dd, nm) in ((cb, cb_d, "cb"), (clg, clg_d, "clg"), (clb, clb_d, "clb")):
                    S.dma("sp", tt[:], dd, "ld_" + nm, writes=[nm])
                V(lambda e: e.memset(onesln[:], 1.0 / 512.0), [], ["onesln"])
                G(lambda e: e.memset(ubf[:, :, 0:30], 0.0), [], [("ubh",)])
                for cc in range(4):
                    V(lambda e, cc=cc: e.tensor_tensor(out=dg[:, cc * 31:(cc + 1) * 31, :],
                                                       in0=idb[:, :].unsqueeze(1).to_broadcast([128, 31, 128]),
                                                       in1=cw[:, cc, :].unsqueeze(2).to_broadcast([128, 31, 128]), op=ALU.mult),
                      ["idb", "cw"], [("dg", cc * 31 + j) for j in range(31)])
                DG = [("dg", i) for i in range(124)]
                load_gain(n_mix)
                norm_stats()

                groups = [("p", g * NG, NG, [4 * g + i for i in range(4)]) for g in range(NPR // NG)] + [("s", NPR, 64, [16])]

                def do_group(gidx, kind, c0, n, tiles, nxt_tiles):
                    if gidx == 0:
                        for ti, t in enumerate(tiles):
                            norm_tile_T(t, pTs[ti % 2], ("c_pT", ti % 2), xg[:, :, ti * 128:(ti + 1) * 128], [("c_xg", ti)], ti % 2, xb1=cxnb1)
                    XG = [("c_xg", ti) for ti in range(len(tiles))]
                    if kind == "s":
                        HK = [("acc", cc) for cc in range(4)]
                        for i in range(2):
                            S.dma("sp", hist[i], st_conv[i], f"ld_hist{i}", writes=HK if i == 0 else [("hist1",)])
                        for i in range(2):
                            for cc in range(4):
                                P(lambda e, i=i, cc=cc: e.transpose(out=pc[0][:, cc * 32:cc * 32 + 30], in_=hist[i][:, cc * 128:(cc + 1) * 128],
                                                                    identity=idf[0:30, 0:30]),
                                  HK + [("hist1",), "idf"], ["c_pc0"], signal=(cc == 3))
                            V(lambda e, i=i: e.tensor_copy(out=uextS[:, :, i * 62:i * 62 + 30],
                                                           in_=pc[0][:, 0:128].rearrange("p (c j) -> p c j", c=4)[:, :, 0:30]),
                              ["c_pc0"], [("uSh", i)])
                    for cc in range(4):
                        bv, bg, kv_, kg_ = PB[cc]
                        for (bank, bkey, coff) in ((bv, kv_, 0), (bg, kg_, 512)):
                            for k in range(KC):
                                P(lambda e, k=k, cc=cc, bank=bank, coff=coff: e.matmul(
                                    bank[:, 0:n], lhsT=winc[:, k, coff + cc * 128:coff + (cc + 1) * 128], rhs=xg[:, k, 0:n],
                                    start=(k == 0), stop=(k == KC - 1)),
                                  XG + ["winc"], [bkey], signal=(k == KC - 1))
                        sgi = sig[cc % 2]
                        A(lambda e, sgi=sgi, bg=bg: e.activation(out=sgi[:, 0:n], in_=bg[:, 0:n], func=AF.Sigmoid), [kg_], [("sig", cc % 2)])
                        if kind == "p":
                            V(lambda e, cc=cc, sgi=sgi, bv=bv: e.tensor_tensor(out=ubf[:, cc, 30:30 + n], in0=bv[:, 0:n], in1=sgi[:, 0:n], op=ALU.mult),
                              [kv_, ("sig", cc % 2)], [("ub", cc)])
                            if gidx == NPR // NG - 1:
                                V(lambda e, cc=cc, sgi=sgi, bv=bv: e.tensor_tensor(out=uext[:, cc, n:n + 30], in0=bv[:, n - 30:n], in1=sgi[:, n - 30:n], op=ALU.mult),
                                  [kv_, ("sig", cc % 2)], [("u", cc)])
                        else:
                            for i in range(2):
                                V(lambda e, cc=cc, sgi=sgi, i=i, bv=bv: e.tensor_tensor(out=uextS[:, cc, i * 62 + 30:i * 62 + 62], in0=bv[:, i * 32:(i + 1) * 32],
                                                                                       in1=sgi[:, i * 32:(i + 1) * 32], op=ALU.mult),
                                  [kv_, ("sig", cc % 2)], [("uS", cc, i)])
                    if kind == "p":
                        L = n
                        UB = [("ub", cc) for cc in range(4)] + [("ubh",)]
                    else:
                        L = 94
                        G(lambda e: e.tensor_copy(out=ubf[:, :, 0:124], in_=uextS[:, :, :]),
                          [("uS", cc, i) for cc in range(4) for i in range(2)] + [("uSh", 0), ("uSh", 1)], [("ub", cc) for cc in range(4)] + [("ubh",)])
                        UB = [("ub", cc) for cc in range(4)] + [("ubh",)]
                    for cc in range(4):
                        for j in range(31):
                            P(lambda e, cc=cc, j=j: e.matmul(pc[cc][:, 0:L], lhsT=dg[:, cc * 31 + j, :], rhs=ubf[:, cc, j:j + L],
                                                             start=(j == 0), stop=(j == 30)),
                              UB + [("dg", cc * 31 + j)], [f"c_pc{cc}"], signal=(j == 30))
                        if cc < len(nxt_tiles):
                            norm_tile_T(nxt_tiles[cc], pTs[cc % 2], ("c_pT", cc % 2), xg[:, :, cc * 128:(cc + 1) * 128], [("c_xg", cc)], cc % 2, xb1=cxnb1)
                    if kind == "p" and gidx < NPR // NG - 1:
                        G(lambda e: e.tensor_copy(out=ubf[:, :, 0:30], in_=ubf[:, :, NG:NG + 30]), [("ub", cc) for cc in range(4)], [("ubh",)])
                    for cc in range(4):
                        if kind == "p":
                            pieces = [(0, 0, n)]
                        else:
                            pieces = [(0, 0, 32), (32, 62, 32)]
                        for (d0, s0, ln) in pieces:
                            A(lambda e, cc=cc, d0=d0, s0=s0, ln=ln: e.activation(out=acc[:, cc, d0:d0 + ln], in_=pc[cc][:, s0:s0 + ln], func=AF.Identity,
                                                                               bias=cb[:, cc:cc + 1], scale=1.0),
                              [f"c_pc{cc}", "cb"], [("acc", cc)])
                            V(lambda e, cc=cc, d0=d0, ln=ln: e.tensor_tensor(out=ysq[:, cc, d0:d0 + ln], in0=acc[:, cc, d0:d0 + ln],
                                                                            in1=acc[:, cc, d0:d0 + ln], op=ALU.mult),
                              [("acc", cc)], [("ysq", cc)])
                    ACC = [("acc", cc) for cc in range(4)]
                    YSQ = [("ysq", cc) for cc in range(4)]
                    for cc in range(4):
                        P(lambda e, cc=cc: e.matmul(pp[0][:, 0:n], lhsT=onesln[:], rhs=acc[:, cc, 0:n], start=(cc == 0), stop=(cc == 3)),
                          ACC + ["onesln"], ["c_pp0"], signal=(cc == 3))
                    for cc in range(4):
                        P(lambda e, cc=cc: e.matmul(pp[1][:, 0:n], lhsT=onesln[:], rhs=ysq[:, cc, 0:n], start=(cc == 0), stop=(cc == 3)),
                          YSQ + ["onesln"], ["c_pp1"], signal=(cc == 3))
                    A(lambda e: e.copy(out=mean_sb[:, 0:n], in_=pp[0][:, 0:n]), ["c_pp0"], ["mean_sb"])
                    V(lambda e: e.tensor_tensor(out=var_sb[:, 0:n], in0=mean_sb[:, 0:n], in1=mean_sb[:, 0:n], op=ALU.mult), ["mean_sb"], ["var_sb"])
                    V(lambda e: e.tensor_tensor(out=var_sb[:, 0:n], in0=pp[1][:, 0:n], in1=var_sb[:, 0:n], op=ALU.subtract), ["c_pp1", "var_sb"], ["var_sb"])
                    A(lambda e: e.activation(out=var_sb[:, 0:n], in_=var_sb[:, 0:n], func=AF.Ln, bias=EPS, scale=1.0), ["var_sb"], ["var_sb"])
                    A(lambda e: e.activation(out=rs_sb[:, 0:n], in_=var_sb[:, 0:n], func=AF.Exp, scale=-0.5), ["var_sb"], ["rs_sb"])
                    V(lambda e: e.tensor_tensor(out=ysq[:, :, 0:n], in0=acc[:, :, 0:n], in1=mean_sb[:, 0:n].unsqueeze(1).to_broadcast([128, 4, n]),
                                                op=ALU.subtract), ACC + YSQ + ["mean_sb"], YSQ)
                    V(lambda e: e.tensor_tensor(out=ysq[:, :, 0:n], in0=ysq[:, :, 0:n], in1=rs_sb[:, 0:n].unsqueeze(1).to_broadcast([128, 4, n]),
                                                op=ALU.mult), YSQ + ["rs_sb"], YSQ)
                    for cc in range(4):
                        A(lambda e, cc=cc: e.activation(out=ymc[:, cc, c0:c0 + n], in_=ysq[:, cc, 0:n], func=AF.Silu, bias=clb[:, cc:cc + 1], scale=clg[:, cc:cc + 1]),
                          YSQ + ["clg", "clb"], [("ymc", cc, gidx)])

                    def conv_out(src_ap, dst_ap, srck):
                        for cc in range(4):
                            P(lambda e, cc=cc: e.transpose(out=pp[0][0:30, cc * 128:(cc + 1) * 128], in_=src_ap[:, cc, :], identity=idf[:]),
                              srck + ["idf"], ["c_pp0"], signal=(cc == 3))
                        V(lambda e: e.tensor_copy(out=cvo[:], in_=pp[0][0:30, :]), ["c_pp0"], ["cvo"])
                        S.dma("sp", dst_ap, cvo[:], "st_cvo", reads=["cvo"])
                    if kind == "p" and gidx == NPR // NG - 1:
                        conv_out(uext[:, :, NG:NG + 30], convp_out, [("u", cc) for cc in range(4)])
                    if kind == "s":
                        for i in range(2):
                            conv_out(uextS[:, :, i * 62 + 32:i * 62 + 62], convs_out[i], [("uS", cc, i) for cc in range(4)])

                for gidx, (kind, c0, n, tiles) in enumerate(groups):
                    do_group(gidx, kind, c0, n, tiles, groups[gidx + 1][3] if gidx + 1 < len(groups) else [])
                S.barrier()

        def hgrn_phase(ymc):
            NB = 512
            with ExitStack() as ph:
                win = sb(ph, "win", [128, KC, 2048], BF16)
                wout = sb(ph, "wout", [128, KC, D], BF16)
                lb2 = sb(ph, "lb2", [128, 8], F32); gn = sb(ph, "gn", [128, 4], F32)
                lb = sb(ph, "lb", [128, 4], F32); oml = sb(ph, "oml", [128, 4], F32); lbd = sb(ph, "lbd", [128, 4], F32)
                fc1 = sb(ph, "fc1", [128, 4], F32); fc0 = sb(ph, "fc0", [128, 4], F32)
                tri64 = sb(ph, "tri64", [128, 128], F32); tri32 = sb(ph, "tri32", [64, 64], F32)
                rstP = sb(ph, "rstP", [128, 512], F32); rstS = sb(ph, "rstS", [128, 64], F32)
                onesrm = sb(ph, "onesrm", [128, 128], F32)
                xg = sb(ph, "h_xg", [128, KC, NB], BF16)
                qd = sb(ph, "qd", [128, 4, NB], BF16)
                kd = sb(ph, "kd", [128, 4, NB], BF16)
                vtm = sb(ph, "vtm", [128, 4, 512], BF16)
                sgate = sb(ph, "sgate", [128, 4, NB], F32)
                egl = sb(ph, "egl", [128, 4, 8], F32)
                qs = [sb(ph, "qs0", [128, NB], F32)] * 2
                fgt = [sb(ph, "fgt0", [128, NB], F32)] * 2
                logf = [sb(ph, "logf0", [128, NB], F32)] * 2
                gc = [sb(ph, "gc0", [128, NB], F32)] * 2
                eg = [sb(ph, "eg0", [128, NB], F32)] * 2
                osq = [sb(ph, "osq0", [128, 4, 128], F32)] * 2
                rso = [sb(ph, "rso0", [128, 4, 128], F32)] * 2
                ymh = [sb(ph, f"ymh{i}", [128, 4, 128], BF16) for i in range(2)]
                kdT = [sb(ph, f"kdT{i}", [128, 512], BF16) for i in range(2)]
                ATm = [sb(ph, f"ATm{i}", [128, 4, 128], BF16) for i in range(2)]
                Sst = [sb(ph, f"Sst{i}", [128, 4, 128], F32) for i in range(3)]
                Sbp = [sb(ph, f"Sbp{i}", [128, 4, 128], BF16) for i in range(2)]
                SbS = [sb(ph, f"SbS{i}", [128, 4, 128], BF16) for i in range(2)]
                Rt = sb(ph, "Rt", [128, 4, 128], F32)
                f2v = lambda tns: tns[:, :, :].rearrange("p a b -> p (a b)")
                qs = [qs[0], f2v(Rt)]; qsk = ["qs", "Rt"]
                fgt = [fgt[0], f2v(osq[0])]; fgk = ["fgt", "osq"]
                logf = [logf[0], f2v(rso[0])]; lfk = ["logf", "rso"]
                gc = [gc[0], xnb[0][:, :].bitcast(F32)]; gck = ["gc", "xnb"]
                eg = [eg[0], sqj[:, :].bitcast(F32)]; egk = ["eg", "sqj"]

                bk = [ps(ph, f"hb{i}", [128, 512]) for i in range(8)]
                bkk = [f"hb{i}" for i in range(8)]
                bT = bk[0][:, :].bitcast(BF16)
                bTx = bk[7][:, :].bitcast(BF16)

                win_v = w_in.rearrange("(k p) f -> p k f", p=128)
                for part in (2, 0, 3, 1):
                    S.dma("pool", win[:, :, part * 512:(part + 1) * 512], win_v[:, :, 1024 + part * 512:1024 + (part + 1) * 512],
                          f"ldwin{part}", writes=[("win", part)], reads=([] if part == 2 else [("win", 2)]))
                S.dma("pool", wout[:], w_out.rearrange("(k p) f -> p k f", p=128), "ldwout", writes=["wout"])
                for (tt, dd, nm) in ((lb2, lb2_d, "lb2"), (gn, gn_d, "gn"), (tri64, tri64_d, "tri64"), (tri32, tri32_d, "tri32"),
                                     (rstP, rst64_d, "rstP"), (rstS, rst32_d, "rstS")):
                    S.dma("sp", tt[:], dd, "ld_" + nm, writes=[nm])
                for i in range(2):
                    S.dma("sp", Sst[1 + i][:], st_hgrn[i].rearrange("h k v -> k h v"), f"ld_S{i}", writes=[("Sst", 1 + i)])
                V(lambda e: e.memset(onesrm[:], 1.0 / 128.0), [], ["onesrm"])
                V(lambda e: e.memset(Sst[0][:], 0.0), [], [("Sst", 0)])
                V(lambda e: e.memset(Sbp[0][:], 0.0), [], [("Sbp", 0)])
                for i in range(2):
                    V(lambda e, i=i: e.tensor_copy(out=SbS[i][:], in_=Sst[1 + i][:]), [("Sst", 1 + i)], [("SbS", i)])
                V(lambda e: e.tensor_tensor(out=lbd[:], in0=lb2[:, 0:4], in1=lb2[:, 4:8], op=ALU.subtract), ["lb2"], ["lbd"])
                A(lambda e: e.activation(out=lb[:], in_=lbd[:], func=AF.Sigmoid), ["lbd"], ["lb"])
                V(lambda e: e.tensor_scalar(out=oml[:], in0=lb[:], scalar1=-1.0, scalar2=1.0, op0=ALU.mult, op1=ALU.add), ["lb"], ["oml"])
                V(lambda e: e.tensor_scalar(out=fc1[:], in0=oml[:], scalar1=0.5, scalar2=None, op0=ALU.mult), ["oml"], ["fc1"])
                V(lambda e: e.tensor_tensor(out=fc0[:], in0=lb[:], in1=fc1[:], op=ALU.add), ["lb", "fc1"], ["fc0"])
                gnb = gn[:, :].unsqueeze(2).to_broadcast([128, 4, 128])

                blocks = [("p", b * NB, NB, [4 * b + i for i in range(4)]) for b in range(NPR // NB)] + [("s", NPR, 64, [16])]
                f2 = lambda ap: ap.rearrange("p a b -> p (a b)")
                v3 = lambda bank: bank[:, :].rearrange("p (a b) -> p a b", a=4)
                jgc = [0]

                def pass_x_norm(tiles, ti):
                    norm_tile_T(tiles[ti], bTx, "hb7", xg[:, :, ti * 128:(ti + 1) * 128], [("h_xg", ti)], ti % 2)
                    rows = 128 if tiles[ti] < 16 else 64
                    for k in range(KC):
                        P(lambda e, k=k: e.matmul(bk[7][0:rows, :], lhsT=xg[:, k, ti * 128:ti * 128 + rows], rhs=win[:, k, 1024:1536],
                                                  start=(k == 0), stop=(k == KC - 1)),
                          [("h_xg", ti), ("win", 2)], ["hb7"], signal=(k == KC - 1))
                    A(lambda e: e.copy(out=vtm[0:rows, ti, :], in_=bk[7][0:rows, :]), ["hb7"], [("vtm", ti)])

                def pass_x(kind, c0, n, tiles):
                    rows = 128 if kind == "p" else 64
                    XG = [("h_xg", ti) for ti in range(len(tiles))]
                    rst, rstk = (rstP, "rstP") if kind == "p" else (rstS, "rstS")
                    csz = 64 if kind == "p" else 32
                    nch = n // csz
                    def head_ops(h):
                        hp = h % 2
                        b3 = (2, 3, 4) if hp == 0 else (5, 6, 7)
                        for (bi, coff, part) in ((b3[0], 0, 0), (b3[1], 1536, 3), (b3[2], 512, 1)):
                            for k in range(KC):
                                P(lambda e, k=k, h=h, bi=bi, coff=coff: e.matmul(bk[bi][:, 0:n], lhsT=win[:, k, coff + h * 128:coff + (h + 1) * 128],
                                                                                rhs=xg[:, k, 0:n], start=(k == 0), stop=(k == KC - 1)),
                                  XG + [("win", part)], [bkk[bi]], signal=(k == KC - 1))
                        bq, bg, bf_ = bk[b3[0]], bk[b3[1]], bk[b3[2]]
                        kq, kg, kf = bkk[b3[0]], bkk[b3[1]], bkk[b3[2]]
                        A(lambda e, hp=hp, bq=bq: e.activation(out=qs[hp][:, 0:n], in_=bq[:, 0:n], func=AF.Silu), [kq], [qsk[hp]])
                        A(lambda e, h=h, bg=bg: e.activation(out=sgate[:, h, 0:n], in_=bg[:, 0:n], func=AF.Silu), [kg], [("sgate", h)])
                        A(lambda e, hp=hp, bf_=bf_: e.activation(out=fgt[hp][:, 0:n], in_=bf_[:, 0:n], func=AF.Tanh, scale=0.5), [kf], [fgk[hp]])
                        V(lambda e, hp=hp, h=h: e.tensor_scalar(out=fgt[hp][:, 0:n], in0=fgt[hp][:, 0:n], scalar1=fc1[:, h:h + 1], scalar2=fc0[:, h:h + 1],
                                                              op0=ALU.mult, op1=ALU.add), [fgk[hp], "fc1", "fc0"], [fgk[hp]])
                        A(lambda e, hp=hp: e.activation(out=logf[hp][:, 0:n], in_=fgt[hp][:, 0:n], func=AF.Ln), [fgk[hp]], [lfk[hp]])
                        V(lambda e, hp=hp: e.tensor_tensor_scan(out=gc[hp][:, 0:n], data0=rst[:, 0:n], data1=logf[hp][:, 0:n], initial=0.0,
                                                               op0=ALU.mult, op1=ALU.add), [lfk[hp], rstk], [gck[hp]])
                        V(lambda e, hp=hp: e.tensor_scalar(out=fgt[hp][:, 0:n], in0=fgt[hp][:, 0:n], scalar1=-1.0, scalar2=1.0, op0=ALU.mult, op1=ALU.add),
                          [fgk[hp], lfk[hp]], [fgk[hp]])
                        A(lambda e, hp=hp: e.activation(out=eg[hp][:, 0:n], in_=gc[hp][:, 0:n], func=AF.Exp), [gck[hp]], [egk[hp]])
                        A(lambda e, hp=hp: e.activation(out=gc[hp][:, 0:n], in_=gc[hp][:, 0:n], func=AF.Exp, scale=-1.0), [gck[hp], egk[hp]], [gck[hp]])
                        V(lambda e, hp=hp, h=h: e.tensor_tensor(out=qd[:, h, 0:n], in0=qs[hp][:, 0:n], in1=eg[hp][:, 0:n], op=ALU.mult),
                          [qsk[hp], egk[hp]], [("qd", h)])
                        V(lambda e, hp=hp, h=h: e.tensor_tensor(out=kd[:, h, 0:n], in0=fgt[hp][:, 0:n], in1=gc[hp][:, 0:n], op=ALU.mult),
                          [fgk[hp], gck[hp]], [("kd", h)])
                        G(lambda e, hp=hp, h=h: e.tensor_copy(out=egl[:, h, 0:nch], in_=eg[hp][:, 0:n].rearrange("p (c j) -> p c j", j=csz)[:, :, csz - 1]),
                          [egk[hp]], [("egl", h)])

                    hl = [S.record(lambda h=h: head_ops(h)) for h in range(4)]
                    for l_ in hl:
                        assert len(l_) == 36, len(l_)
                    for p0 in (0, 2):
                        la, lb = hl[p0], hl[p0 + 1]
                        for part in (la[:28], lb[:28], la[28:29], lb[28:29]):
                            S.emit_interleaved(part)
                        S.emit_interleaved(la[29:], lb[29:])

                def pass_y1(kind, c0, ti, t, yp):
                    rows = 128 if kind == "p" else 64
                    lo = ti * 128
                    pAT, po, pPc = bk[1 + yp], bk[3 + yp], bk[5 + yp]
                    kAT, ko, kPc = bkk[1 + yp], bkk[3 + yp], bkk[5 + yp]
                    KD = [("kd", h) for h in range(4)]
                    QD = [("qd", h) for h in range(4)]
                    for h in range(4):
                        P(lambda e, h=h: e.transpose(out=bT[0:rows, h * 128:(h + 1) * 128], in_=kd[:, h, lo:lo + rows], identity=idb[:]),
                          KD + ["idb"], ["hb0"], signal=(h == 3))
                    A(lambda e: e.copy(out=kdT[yp][0:rows, :], in_=bT[0:rows, 0:512]), ["hb0"], [("kdT", yp)])
                    if kind == "p":
                        chunks = [(0, 64, 0), (64, 64, 0)]
                        tri, trik = tri64, "tri64"
                    else:
                        chunks = [(0, 32, 1), (32, 32, 2)]
                        tri, trik = tri32, "tri32"
                    for h in range(4):
                        P(lambda e, h=h: e.matmul(pAT[0:rows, h * 128:h * 128 + rows], lhsT=kd[:, h, lo:lo + rows], rhs=qd[:, h, lo:lo + rows],
                                                  start=True, stop=True),
                          KD + QD, [kAT], signal=(h == 3))
                    V(lambda e: e.tensor_tensor(out=ATm[yp][0:rows, :, 0:rows], in0=v3(pAT)[0:rows, :, 0:rows],
                                                in1=tri[0:rows, 0:rows].unsqueeze(1).to_broadcast([rows, 4, rows]), op=ALU.mult),
                      [kAT, trik], [("ATm", yp)])
                    for ci, (cc0, cl, sidx) in enumerate(chunks):
                        if kind == "p":
                            jg = jgc[0]
                            jgc[0] += 1
                            Sb_cur, Sbk_cur = Sbp[jg % 2], ("Sbp", jg % 2)
                            Sb_nxt, Sbk_nxt = Sbp[(jg + 1) % 2], ("Sbp", (jg + 1) % 2)
                        else:
                            Sb_cur, Sbk_cur = SbS[sidx - 1], ("SbS", sidx - 1)
                            Sb_nxt, Sbk_nxt = None, None
                        for h in range(4):
                            P(lambda e, h=h, cc0=cc0, cl=cl: e.matmul(pPc[:, h * 128:(h + 1) * 128], lhsT=kdT[yp][cc0:cc0 + cl, h * 128:(h + 1) * 128],
                                                                     rhs=vtm[cc0:cc0 + cl, ti, h * 128:(h + 1) * 128], start=True, stop=True),
                              [("kdT", yp), ("vtm", ti)], [kPc], signal=(h == 3))
                        for h in range(4):
                            P(lambda e, h=h, cc0=cc0, cl=cl: e.matmul(po[:, h * 128 + cc0:h * 128 + cc0 + cl], lhsT=vtm[0:rows, ti, h * 128:(h + 1) * 128],
                                                                     rhs=ATm[yp][0:rows, h, cc0:cc0 + cl], start=True, stop=False),
                              [("vtm", ti), ("ATm", yp)], [ko], signal=False)
                            P(lambda e, h=h, cc0=cc0, cl=cl, Sb_cur=Sb_cur: e.matmul(po[:, h * 128 + cc0:h * 128 + cc0 + cl], lhsT=Sb_cur[:, h, :],
                                                                                    rhs=qd[:, h, lo + cc0:lo + cc0 + cl], start=False, stop=True),
                              [Sbk_cur] + QD, [ko], signal=(h == 3))
                        V(lambda e, sidx=sidx: e.tensor_tensor(out=Rt[:], in0=v3(pPc), in1=Sst[sidx][:], op=ALU.add), [kPc, ("Sst", sidx)], ["Rt"])
                        cidx = (lo + cc0) // cl
                        V(lambda e, sidx=sidx, cidx=cidx: e.tensor_tensor(out=Sst[sidx][:], in0=Rt[:],
                                                                         in1=egl[:, :, cidx:cidx + 1].to_broadcast([128, 4, 128]), op=ALU.mult),
                          ["Rt"] + [("egl", h) for h in range(4)], [("Sst", sidx)])
                        if Sb_nxt is not None:
                            A(lambda e, sidx=sidx, Sb_nxt=Sb_nxt: e.copy(out=Sb_nxt[:], in_=Sst[sidx][:]), [("Sst", sidx)], [Sbk_nxt])

                def pass_y2(kind, c0, ti, t, yp, last):
                    rows = 128 if kind == "p" else 64
                    lo = ti * 128
                    pAT, po, pPc = bk[1 + yp], bk[3 + yp], bk[5 + yp]
                    kAT, ko, kPc = bkk[1 + yp], bkk[3 + yp], bkk[5 + yp]
                    SG = [("sgate", h) for h in range(4)]
                    A(lambda e: e.activation(out=osq[yp][:], in_=v3(po), func=AF.Square), [ko], ["osq"])
                    P(lambda e: e.matmul(pAT[:, :], lhsT=onesrm[:], rhs=f2(osq[yp][:]), start=True, stop=True), ["osq", "onesrm"], [kAT])
                    A(lambda e: e.activation(out=rso[yp][:], in_=v3(pAT), func=AF.Ln, bias=EPS, scale=1.0), [kAT], ["rso"])
                    A(lambda e: e.activation(out=rso[yp][:], in_=rso[yp][:], func=AF.Exp, scale=-0.5), ["rso"], ["rso"])
                    V(lambda e: e.tensor_tensor(out=osq[yp][:], in0=v3(po), in1=rso[yp][:], op=ALU.mult), [ko, "rso", "osq"], ["osq"])
                    V(lambda e: e.tensor_tensor(out=osq[yp][:], in0=osq[yp][:], in1=gnb, op=ALU.mult), ["osq", "gn"], ["osq"])
                    V(lambda e: e.tensor_tensor(out=ymh[yp][:], in0=osq[yp][:], in1=sgate[:, :, lo:lo + 128], op=ALU.mult),
                      ["osq"] + SG, [("ymh", yp)])
                    for dh, (pb, pk) in enumerate(((pAT, kAT), (pPc, kPc))):
                        for c in range(8):
                            lhs = ymc[:, c, c0 + lo:c0 + lo + rows] if c < 4 else ymh[yp][:, c - 4, 0:rows]
                            P(lambda e, c=c, lhs=lhs, dh=dh, pb=pb: e.matmul(pb[0:rows, :], lhsT=lhs, rhs=wout[:, c, dh * 512:(dh + 1) * 512],
                                                                            start=(c == 0), stop=(c == 7)),
                              [("ymh", yp), "wout"], [pk], signal=(c == 7))
                        V(lambda e, dh=dh, pb=pb: e.tensor_tensor(out=xres[0:rows, t, dh * 512:(dh + 1) * 512], in0=pb[0:rows, :],
                                                                 in1=xres[0:rows, t, dh * 512:(dh + 1) * 512], op=ALU.add),
                          [pk, ("x", t)], [("x", t)])
                    if kind == "p" and t == 15:
                        S.dma("sp", hgrnp_out.rearrange("h k v -> k h v"), Sst[0][:], "st_S0", reads=[("Sst", 0)])
                    if kind == "s":
                        for i in range(2):
                            S.dma("sp", hgrns_out[i].rearrange("h k v -> k h v"), Sst[1 + i][:], f"st_S{1 + i}", reads=[("Sst", 1 + i)])

                ytile = [0]
                for ti in range(len(blocks[0][3])):
                    pass_x_norm(blocks[0][3], ti)
                for bi, (kind, c0, n, tiles) in enumerate(blocks):
                    pass_x(kind, c0, n, tiles)
                    nxt = blocks[bi + 1][3] if bi + 1 < len(blocks) else []
                    yps = [(ytile[0] + i) % 2 for i in range(len(tiles))]
                    ytile[0] += len(tiles)
                    pass_y1(kind, c0, 0, tiles[0], yps[0])
                    for ti, t in enumerate(tiles):
                        l2 = S.record(lambda: pass_y2(kind, c0, ti, t, yps[ti], ti == len(tiles) - 1))
                        l1 = S.record(lambda: pass_y1(kind, c0, ti + 1, tiles[ti + 1], yps[ti + 1])) if ti + 1 < len(tiles) else []
                        ln_ = S.record(lambda: pass_x_norm(nxt, ti)) if ti < len(nxt) else []
                        S.emit_interleaved(l1, l2, ln_)
                S.barrier()

        def mixer_phase():
            with ExitStack() as mph:
                ymc = sb(mph, "ymc", [128, 4, TP], BF16)
                conv_phase(ymc)
                hgrn_phase(ymc)

        def attn_phase():
            with ExitStack() as ph:
                wq = sb(ph, "wq", [128, KC, D], BF16)
                wo = sb(ph, "wo", [128, KC, D], BF16)
                wkv = sb(ph, "wkv", [128, KC, D], BF16)
                memt = sb(ph, "memt", [128, 2, D], F32)
                mnT = sb(ph, "mnT", [128, KC, NMEM], BF16)
                mss = sb(ph, "mss", [128, 2], F32)
                mrs = sb(ph, "mrs", [128, 2], F32)
                KT = [sb(ph, f"KT{i}", [128, KC, NMEM], BF16) for i in range(3)]
                Vt = [sb(ph, f"Vt{i}", [128, 2, D], BF16) for i in range(3)]
                kvo = sb(ph, "kvo", [128, D], F32)
                ktm = sb(ph, "ktm", [128, 2, D], BF16)
                xg = sb(ph, "a_xg", [128, KC, 512], BF16)
                qT = sb(ph, "qT", [128, KC, 512], BF16)
                mx = sb(ph, "mx", [128, 12], F32)
                nmx = sb(ph, "nmx", [128, 12], F32)
                rsum = sb(ph, "rsum", [128, 8], F32)
                rrec = sb(ph, "rrec", [128, 8], F32)
                Pf = sb(ph, "Pf", [128, 4, NMEM], F32)
                Pn = sb(ph, "Pn", [128, 4, NMEM], BF16)
                PnT = sb(ph, "PnT", [128, 8, 512], BF16)
                oT = sb(ph, "oT", [128, KC, 512], BF16)

                pTs = [ps(ph, f"a_pT{i}", [128, 1024], BF16) for i in range(2)]
                pA = [ps(ph, f"a_pA{i}", [128, 512]) for i in range(2)]
                pS = [ps(ph, f"a_pS{i}", [128, 512]) for i in range(2)]
                pO = [ps(ph, f"a_pO{i}", [128, 512]) for i in range(2)]

                _astop = int(os.environ.get("MK_ATTN_STOP", "99"))
                wvA = PnT[:, :, :].rearrange("p a b -> p (a b)").rearrange("p (k f) -> p k f", k=4)
                wvB = oT[:, :, :].rearrange("p a b -> p (a b)").rearrange("p (k f) -> p k f", k=4)
                wv_v = wv_d.rearrange("(k p) f -> p k f", p=128)
                S.dma("sp", memt[:], mem_in.rearrange("(t p) d -> p t d", p=128), "ld_mem", writes=["memt"])
                ktmB = [ktm[:, :, :], kvo[:, :].bitcast(BF16).rearrange("p (t d) -> p t d", t=2)]
                KTK = [["ktm"], [("kvo", 0), ("kvo", 1)]]
                for i in range(2):
                    S.dma("pool", ktmB[i], ck_in[i].rearrange("(t p) d -> p t d", p=128), f"ld_ck{i}", writes=KTK[i])
                S.dma("pool", wkv[:], wk_d.rearrange("(k p) f -> p k f", p=128), "ld_wkv", writes=["wkv"])
                S.dma("pool", wvA, wv_v[:, 0:4, :], "ld_wvA", writes=["wvA"], reads=["wkv"])
                S.dma("pool", wvB, wv_v[:, 4:8, :], "ld_wvB", writes=["wvB"])
                for i in range(2):
                    S.dma("pool", Vt[1 + i][:], cv_in[i].rearrange("(t p) d -> p t d", p=128), f"ld_cv{i}", writes=[("Vt", 1 + i)])
                S.dma("pool", wq[:], wq_d.rearrange("(k p) f -> p k f", p=128), "ld_wq", writes=["wq"])
                S.dma("pool", wo[:], wo_d.rearrange("(k p) f -> p k f", p=128), "ld_wo", writes=["wo"])
                norm_stats()
                for i in range(2):
                    for mt in range(2):
                        for k in range(KC):
                            P(lambda e, k=k, mt=mt, i=i: e.transpose(out=pTs[mt % 2][:, k * 128:(k + 1) * 128], in_=ktmB[i][:, mt, k * 128:(k + 1) * 128], identity=idb[:]),
                              KTK[i] + ["idb"], [("a_pT", mt % 2)], signal=(k == KC - 1))
                        A(lambda e, mt=mt, i=i: e.copy(out=KT[1 + i][:, :, mt * 128:(mt + 1) * 128], in_=pTs[mt % 2][:, 0:1024].rearrange("p (k t) -> p k t", k=KC)),
                          [("a_pT", mt % 2)], [("KT", 1 + i)])

                if _astop <= 0:
                    S.barrier(); return
                load_gain(n_mem)
                for mt in range(2):
                    A(lambda e, mt=mt: e.activation(out=sqj[:], in_=memt[:, mt, :], func=AF.Square, accum_out=mss[:, mt:mt + 1]), ["memt"], ["sqj", "mss"])
                A(lambda e: e.activation(out=mrs[:], in_=mss[:], func=AF.Ln, bias=EPS, scale=1.0 / D), ["mss"], ["mrs"])
                A(lambda e: e.activation(out=mrs[:], in_=mrs[:], func=AF.Exp, scale=-0.5), ["mrs"], ["mrs"])
                for mt in range(2):
                    V(lambda e, mt=mt: e.scalar_tensor_tensor(out=xnb[0][:], in0=memt[:, mt, :], scalar=mrs[:, mt:mt + 1], in1=gb[:],
                                                             op0=ALU.mult, op1=ALU.mult), ["memt", "mrs", "gb"], ["xnb"])
                    for k in range(KC):
                        P(lambda e, k=k, mt=mt: e.transpose(out=pTs[mt % 2][:, k * 128:(k + 1) * 128], in_=xnb[0][:, k * 128:(k + 1) * 128], identity=idb[:]),
                          ["xnb", "idb"], [("a_pT", mt % 2)], signal=(k == KC - 1))
                    A(lambda e, mt=mt: e.copy(out=mnT[:, :, mt * 128:(mt + 1) * 128], in_=pTs[mt % 2][:, 0:1024].rearrange("p (k t) -> p k t", k=KC)),
                      [("a_pT", mt % 2)], [("mnT", mt)])
                MNT = [("mnT", 0), ("mnT", 1)]

                def kv_token_major(out_dram, dst_bf, dst_key, wsel, wkeys):
                    cnt = 0
                    for mt in range(2):
                        for dh in range(2):
                            pb = pA[cnt % 2]
                            pk = ("a_pA", cnt % 2)
                            cnt += 1
                            for k in range(KC):
                                P(lambda e, k=k, mt=mt, dh=dh, pb=pb: e.matmul(pb[:, :], lhsT=mnT[:, k, mt * 128:(mt + 1) * 128],
                                                                              rhs=wsel(k, dh), start=(k == 0), stop=(k == KC - 1)),
                                  MNT + wkeys, [pk], signal=(k == KC - 1))
                            A(lambda e, dh=dh, pb=pb: e.copy(out=kvo[:, dh * 512:(dh + 1) * 512], in_=pb[:, :]), [pk], [("kvo", dh)])
                            if dst_bf is not None:
                                V(lambda e, mt=mt, dh=dh: e.tensor_copy(out=dst_bf[:, mt, dh * 512:(dh + 1) * 512], in_=kvo[:, dh * 512:(dh + 1) * 512]),
                                  [("kvo", dh)], [dst_key])
                        S.dma("sp", out_dram[mt * 128:(mt + 1) * 128, :], kvo[:], "st_kvo", reads=[("kvo", 0), ("kvo", 1)])

                if _astop <= 1:
                    S.barrier(); return
                kv_token_major(mk_out, None, None, lambda k, dh: wkv[:, k, dh * 512:(dh + 1) * 512], ["wkv"])
                for c in range(KC):
                    pb = pA[c % 2]
                    pk = ("a_pA", c % 2)
                    for k in range(KC):
                        P(lambda e, k=k, c=c, pb=pb: e.matmul(pb[:, 0:NMEM], lhsT=wkv[:, k, c * 128:(c + 1) * 128], rhs=mnT[:, k, :],
                                                             start=(k == 0), stop=(k == KC - 1)),
                          MNT + ["wkv"], [pk], signal=(k == KC - 1))
                    A(lambda e, c=c, pb=pb: e.copy(out=KT[0][:, c, :], in_=pb[:, 0:NMEM]), [pk], [("KT", 0)])
                if _astop <= 2:
                    S.barrier(); return
                kv_token_major(mv_out, Vt[0], ("Vt", 0),
                               lambda k, dh: (wvA if k < 4 else wvB)[:, k % 4, dh * 512:(dh + 1) * 512], ["wvA", "wvB"])
                if _astop <= 3:
                    S.barrier(); return
                if _astop <= 4:
                    S.barrier(); return
                load_gain(n_xattn)
                tgs = [(0, 512), (512, 512), (1024, 512), (1536, 512), (2048, 64)]

                def tg_norm(c0, n):
                    tiles = list(range(c0 // 128, (c0 + n + 127) // 128))
                    for ti, t in enumerate(tiles):
                        norm_tile_T(t, pTs[ti % 2], ("a_pT", ti % 2), xg[:, :, ti * 128:(ti + 1) * 128], [("a_xg", ti)], ti % 2)

                QTB = [qT, wkv[:, :, 0:512]]

                def tg_qproj(gi, c0, n):
                    tiles = list(range(c0 // 128, (c0 + n + 127) // 128))
                    XG = [("a_xg", ti) for ti in range(len(tiles))]
                    qb = QTB[gi % 2]
                    xk = ["wkv"] if gi % 2 == 1 else []
                    for c in range(KC):
                        pb = pA[c % 2]
                        pk = ("a_pA", c % 2)
                        for k in range(KC):
                            P(lambda e, k=k, c=c, pb=pb: e.matmul(pb[:, 0:n], lhsT=wq[:, k, c * 128:(c + 1) * 128], rhs=xg[:, k, 0:n],
                                                                 start=(k == 0), stop=(k == KC - 1)),
                              XG + ["wq"], [pk], signal=(k == KC - 1))
                        A(lambda e, c=c, pb=pb, qb=qb: e.copy(out=qb[:, c, 0:n], in_=pb[:, 0:n]), [pk], [("qT", gi % 2, c)] + xk)

                def do_tg(gi, c0, n, nxt):
                    tiles = list(range(c0 // 128, (c0 + n + 127) // 128))
                    qT = QTB[gi % 2]
                    xkq = ["wkv"] if gi % 2 == 1 else []
                    if c0 < NPR:
                        segs = [(ti * 128, 128, 0) for ti in range(len(tiles))]
                    else:
                        segs = [(0, 32, 1), (32, 32, 2)]
                    SBANK = [(pS, "a_pS"), (pO, "a_pO")]
                    PFB = [Pf, memt[:, 0, :].rearrange("p (a b) -> p a b", a=4)]
                    PNB = [Pn, ktm[:, 0, :].rearrange("p (a b) -> p a b", a=4)]
                    PFX = [[], ["memt"]]
                    PNX = [[], ["ktm"]]

                    def seg_scores(si):
                        lc, ln, kvi = segs[si]
                        bank, bkey = SBANK[si % 2]
                        for h in range(4):
                            for cc in range(2):
                                c = 2 * h + cc
                                P(lambda e, c=c, cc=cc, h=h, lc=lc, ln=ln, kvi=kvi, bank=bank: e.matmul(
                                    bank[h // 2][0:ln, (h % 2) * 256:(h % 2) * 256 + 256], lhsT=qT[:, c, lc:lc + ln], rhs=KT[kvi][:, c, :],
                                    start=(cc == 0), stop=(cc == 1)),
                                  [("qT", gi % 2, c), ("KT", kvi)] + xkq, [(bkey, h // 2)], signal=(cc == 1))

                    def seg_max(si):
                        lc, ln, kvi = segs[si]
                        rows = ln
                        bank, bkey = SBANK[si % 2]
                        par = si % 2
                        st0 = 4 * (si % 3)
                        for hb in range(2):
                            V(lambda e, hb=hb, rows=rows, bank=bank, st0=st0: e.tensor_reduce(
                                out=mx[0:rows, st0 + 2 * hb:st0 + 2 * hb + 2], in_=bank[hb][0:rows, :].rearrange("p (a b) -> p a b", a=2),
                                axis=AX.X, op=ALU.max),
                              [(bkey, hb)], [("mx", si % 3, hb)])
                        V(lambda e, rows=rows, st0=st0: e.tensor_scalar(out=nmx[0:rows, st0:st0 + 4], in0=mx[0:rows, st0:st0 + 4], scalar1=-1.0 / 16.0,
                                                                       scalar2=None, op0=ALU.mult),
                          [("mx", si % 3, 0), ("mx", si % 3, 1)], [("nmx", si % 3)])

                    def seg_exp(si):
                        lc, ln, kvi = segs[si]
                        rows = ln
                        bank, bkey = SBANK[si % 2]
                        par = si % 2
                        pf_ = PFB[par]
                        st0 = 4 * par
                        for h in range(4):
                            A(lambda e, h=h, rows=rows, bank=bank, st0=st0, pf_=pf_: e.activation(
                                out=pf_[0:rows, h, :], in_=bank[h // 2][0:rows, (h % 2) * 256:(h % 2) * 256 + 256],
                                func=AF.Exp, bias=nmx[0:rows, 4 * (si % 3) + h:4 * (si % 3) + h + 1], scale=1.0 / 16.0, accum_out=rsum[0:rows, st0 + h:st0 + h + 1]),
                              [(bkey, h // 2), ("nmx", si % 3)], [("Pf", par, h), ("rsum", par, h)] + PFX[par])

                    def seg_norm(si):
                        lc, ln, kvi = segs[si]
                        rows = ln
                        par = si % 2
                        pf_, pn_ = PFB[par], PNB[par]
                        st0 = 4 * par
                        RS = [("rsum", par, h) for h in range(4)]
                        V(lambda e, rows=rows, st0=st0: e.reciprocal(out=rrec[0:rows, st0:st0 + 4], in_=rsum[0:rows, st0:st0 + 4]), RS, [("rrec", par)])
                        V(lambda e, rows=rows, st0=st0, pf_=pf_, pn_=pn_: e.tensor_tensor(
                            out=pn_[0:rows, :, :], in0=pf_[0:rows, :, :],
                            in1=rrec[0:rows, st0:st0 + 4].unsqueeze(2).to_broadcast([rows, 4, NMEM]), op=ALU.mult),
                          [("Pf", par, h) for h in range(4)] + [("rrec", par)] + PFX[par], [("Pn", par)] + PNX[par])

                    def seg_transpose(si):
                        lc, ln, kvi = segs[si]
                        rows = ln
                        par = si % 2
                        pn_ = PNB[par]
                        for j in range(8):
                            P(lambda e, j=j, rows=rows, pn_=pn_, par=par: e.transpose(out=pTs[par][:, j * 128:j * 128 + rows],
                                                                                     in_=pn_[0:rows, j // 2, (j % 2) * 128:(j % 2) * 128 + 128],
                                                                                     identity=idb[0:rows, 0:rows]),
                              [("Pn", par), "idb"] + PNX[par], [("a_pT", par)], signal=(j == 7))
                        A(lambda e, lc=lc, rows=rows, par=par: e.copy(out=PnT[:, :, lc:lc + rows],
                                                                     in_=pTs[par][:, 0:1024].rearrange("p (j t) -> p j t", j=8)[:, :, 0:rows]),
                          [("a_pT", par)], [("PnT", si), "wvA"])

                    ns = len(segs)

                    def softmax_all():
                        for si in range(min(2, ns)):
                            seg_scores(si)
                            seg_max(si)
                        seg_exp(0)
                        for si in range(ns):
                            if si + 2 < ns:
                                seg_scores(si + 2)
                                seg_max(si + 2)
                            if si + 1 < ns:
                                seg_exp(si + 1)
                            seg_norm(si)
                            seg_transpose(si)

                    if nxt is not None:
                        tg_norm(*nxt)
                        lq = S.record(lambda: tg_qproj(gi + 1, *nxt))
                        lsm = S.record(softmax_all)
                        S.emit_interleaved(lq, lsm)
                    else:
                        softmax_all()
                    tiles_pnt = len(segs)
                    PNT = [("PnT", si) for si in range(tiles_pnt)]
                    if c0 < NPR:
                        osegs = [(0, n, 0)]
                    else:
                        osegs = [(0, 32, 1), (32, 32, 2)]
                    for c in range(KC):
                        h = c // 2
                        pb = pO[c % 2]
                        pk = ("a_pO", c % 2)
                        for si, (lc, ln, kvi) in enumerate(osegs):
                            for mh in range(2):
                                P(lambda e, c=c, h=h, mh=mh, lc=lc, ln=ln, kvi=kvi, pb=pb: e.matmul(
                                    pb[:, lc:lc + ln], lhsT=Vt[kvi][:, mh, c * 128:(c + 1) * 128], rhs=PnT[:, 2 * h + mh, lc:lc + ln],
                                    start=(mh == 0), stop=(mh == 1)),
                                  PNT + [("Vt", kvi)], [pk], signal=(mh == 1))
                        A(lambda e, c=c, pb=pb: e.copy(out=oT[:, c, 0:n], in_=pb[:, 0:n]), [pk], [("oT", c), "wvB"])
                    OT = [("oT", c) for c in range(KC)]
                    cnt = 0
                    for ti, t in enumerate(tiles):
                        rows = 128 if t < 16 else 64
                        for dh in range(2):
                            pb = pA[cnt % 2]
                            pk = ("a_pA", cnt % 2)
                            cnt += 1
                            for c in range(KC):
                                P(lambda e, c=c, ti=ti, rows=rows, dh=dh, pb=pb: e.matmul(pb[0:rows, :], lhsT=oT[:, c, ti * 128:ti * 128 + rows],
                                                                                         rhs=wo[:, c, dh * 512:(dh + 1) * 512], start=(c == 0), stop=(c == KC - 1)),
                                  OT + ["wo"], [pk], signal=(c == KC - 1))
                            V(lambda e, t=t, rows=rows, dh=dh, pb=pb: e.tensor_tensor(out=xres[0:rows, t, dh * 512:(dh + 1) * 512], in0=pb[0:rows, :],
                                                                                     in1=xres[0:rows, t, dh * 512:(dh + 1) * 512], op=ALU.add),
                              [pk, ("x", t)], [("x", t)])

                tg_norm(*tgs[0])
                tg_qproj(0, *tgs[0])
                for gi_, (c0, n) in enumerate(tgs):
                    do_tg(gi_, c0, n, tgs[gi_ + 1] if gi_ + 1 < len(tgs) else None)
                S.barrier()

        if os.environ.get("MK_SKIP_FFN1") != "1":
            ffn_phase("f1", n_ffn1, w1g, w1u, w1d)
        else:
            load_x_rest(())
        if stop_after != "ffn1":
            if os.environ.get("MK_SKIP_MIX") != "1":
                mixer_phase()
            if stop_after != "mix":
                attn_phase()
                if stop_after != "attn":
                    ffn_phase("f2", n_ffn2, w2g, w2u, w2d, final=(stop_after not in ("ffn1", "mix", "attn", "ffn2")))

        def final_phase():
            with ExitStack() as ph:
                yb = [sb(ph, f"yb{i}", [128, D], F32) for i in range(2)]
                load_gain(n_final)
                norm_stats()
                for t in range(NT):
                    par = t % 2
                    rows = tile_rows(t)
                    V(lambda e, t=t, par=par: e.scalar_tensor_tensor(out=yb[par][:], in0=xres[:, t, :], scalar=rstd[:, t:t + 1], in1=gb[:],
                                                                    op0=ALU.mult, op1=ALU.mult),
                      [("x", t), ("xpad",), ("rstd", t // 4), "gb"], [("yb", par)])
                    S.dma("sp", y_out[t * 128:t * 128 + rows, :], yb[par][0:rows, :], f"st_y{par}", reads=[("yb", par)])

        def dump_x():
            for t in range(NT):
                rows = tile_rows(t)
                S.dma("sp", y_out[t * 128:t * 128 + rows, :], xres[0:rows, t, :], "st_dbg", reads=[("x", t)])

        if stop_after in ("ffn1", "mix", "attn", "ffn2"):
            dump_x()
        S.build()
    return nc


def _consts():
    ident = np.eye(128, dtype=np.float32)
    s = np.arange(128)[:, None]
    t = np.arange(128)[None, :]
    tri64 = ((s // 64 == t // 64) & (s <= t)).astype(np.float32)
    s2 = np.arange(64)[:, None]
    t2 = np.arange(64)[None, :]
    tri32 = ((s2 // 32 == t2 // 32) & (s2 <= t2)).astype(np.float32)
    rst64 = np.ones((128, 512), np.float32)
    rst64[:, ::64] = 0.0
    rst32 = np.ones((128, 64), np.float32)
    rst32[:, ::32] = 0.0
    return dict(ident=ident, tri64=tri64, tri32=tri32, rst64=rst64, rst32=rst32)


def _fm4(v):
    return np.ascontiguousarray(np.asarray(v, np.float32).reshape(4, 128).T)


def make_in_maps(inp):
    c = _consts()
    f = lambda a: np.ascontiguousarray(np.asarray(a, dtype=np.float32))
    shared = dict(
        w1g=f(inp["ffn1_w_gate"][0]), w1u=f(inp["ffn1_w_up"][0]), w1d=f(inp["ffn1_w_down"][0]),
        w2g=f(inp["ffn2_w_gate"][0]), w2u=f(inp["ffn2_w_up"][0]), w2d=f(inp["ffn2_w_down"][0]),
        w_in=f(inp["w_in"][0]), w_out=f(inp["w_out"][0]),
        wk=f(inp["mem_wk"][0]), wv=f(inp["mem_wv"][0]), wq=f(inp["xattn_wq"][0]), wo=f(inp["xattn_wo"][0]),
        n_ffn1=f(inp["ffn1_norm"][0]), n_mix=f(inp["mix_norm"][0]), n_mem=f(inp["mem_norm"][0]),
        n_xattn=f(inp["xattn_norm"][0]), n_ffn2=f(inp["ffn2_norm"][0]), n_final=f(inp["final_norm"]),
        cw=f(np.asarray(inp["conv_dw"][0]).T),
        cb=_fm4(inp["conv_dw_b"][0]), clg=_fm4(inp["conv_ln_g"][0]), clb=_fm4(inp["conv_ln_b"][0]),
        lb2=np.ascontiguousarray(np.concatenate([_fm4(inp["hgrn_lb"][0]), _fm4(inp["hgrn_lb"][1])], axis=1)),
        gn=np.ascontiguousarray(np.asarray(inp["hgrn_norm"][0], np.float32).T),
        **c,
    )
    maps = []
    for b in range(N_CORES):
        m = dict(shared)
        m["x_in"] = np.ascontiguousarray(np.concatenate(
            [inp["x_prompt"][b], inp["x_sample"][2 * b], inp["x_sample"][2 * b + 1]], axis=0).astype(np.float32))
        m["mem_in"] = f(inp["mem_prompt"][b])
        m["st_conv"] = f(inp["state_conv"][0, 2 * b:2 * b + 2])
        m["st_hgrn"] = f(inp["state_hgrn"][0, 2 * b:2 * b + 2])
        m["ck_in"] = f(np.asarray(inp["cache_mem_k"][0, 2 * b:2 * b + 2]).reshape(2, NMEM, D))
        m["cv_in"] = f(np.asarray(inp["cache_mem_v"][0, 2 * b:2 * b + 2]).reshape(2, NMEM, D))
        maps.append(m)
    return maps


def assemble(results):
    y_p = np.stack([r["y_out"][0:NPR] for r in results])
    y_s = np.stack([r["y_out"][NPR + i * NSM:NPR + (i + 1) * NSM] for r in results for i in range(2)])
    conv_p = np.stack([r["convp_out"] for r in results])[None]
    hgrn_p = np.stack([r["hgrnp_out"] for r in results])[None]
    mk_p = np.stack([r["mk_out"].reshape(NMEM, 4, 256) for r in results])[None]
    mv_p = np.stack([r["mv_out"].reshape(NMEM, 4, 256) for r in results])[None]
    conv_s = np.concatenate([r["convs_out"] for r in results], axis=0)[None]
    hgrn_s = np.concatenate([r["hgrns_out"] for r in results], axis=0)[None]
    outs = (y_p, y_s, conv_p, hgrn_p, mk_p, mv_p, conv_s, hgrn_s)
    return tuple(np.ascontiguousarray(o.astype(np.float32)) for o in outs)


def kernel(**inputs):
    nc = build_nc(stop_after=os.environ.get("MK_STOP_AFTER"))
    in_maps = make_in_maps(inputs)
    res = run_bass_kernel_spmd(nc, in_maps, core_ids=list(range(N_CORES)))
    return assemble(res.results)
```

```python
import os
import numpy as np
from contextlib import ExitStack
import concourse.bass as bass
import concourse.mybir as mybir
from concourse.bass_utils import run_bass_kernel_spmd

F32 = mybir.dt.float32
BF16 = mybir.dt.bfloat16
AF = mybir.ActivationFunctionType
ALU = mybir.AluOpType
AX = mybir.AxisListType

D = 1024
KC = 8
FF = 2816
FC = 22
NPR = 2048
NSM = 32
T = NPR + 2 * NSM
NT = 17
TP = NT * 128
DIN = 3072
NMEM = 256
EPS = 1e-6
N_CORES = 8


class Chan:
    def __init__(self, name, sem):
        self.name = name
        self.sem = sem
        self.count = 0


class Sched:
    ENGS = ("pe", "act", "dve", "pool", "sp")

    def __init__(self, nc, stack):
        self.nc = nc
        self.stack = stack
        self.sem = {}
        self.cnt = {}
        for e in ("pe", "act", "dve", "pool"):
            self.sem[("eng", e)] = stack.enter_context(nc.semaphore("s_" + e))
            self.cnt[e] = 0
        self.prog = {e: [] for e in self.ENGS}
        self.waited = {e: {} for e in self.ENGS}
        self.res = {}
        self.chans = {}
        self.nops = {e: 0 for e in self.ENGS}
        self.rec = None

    def record(self, fn):
        assert self.rec is None
        self.rec = []
        fn()
        out, self.rec = self.rec, None
        return out

    def emit_interleaved(self, *lists):
        lists = [l for l in lists if l]
        pos = [0] * len(lists)
        while True:
            best, bf = None, None
            for i, l in enumerate(lists):
                if pos[i] < len(l):
                    f = pos[i] / len(l)
                    if bf is None or f < bf:
                        best, bf = i, f
            if best is None:
                break
            kind, args, kw = lists[best][pos[best]]
            pos[best] += 1
            getattr(self, kind)(*args, **kw)

    def chan(self, name):
        if name not in self.chans:
            sem = self.stack.enter_context(self.nc.semaphore("c_" + name))
            self.chans[name] = Chan(name, sem)
            self.sem[("chan", name)] = sem
        return self.chans[name]

    PSUM_KEYS = {"pg", "pu", "pd", "pT", "a_pT", "a_pA", "a_pS", "a_pO",
                 "c_pT", "c_pp0", "c_pp1", "c_pc0", "c_pc1", "c_pc2", "c_pc3",
                 "hb0", "hb1", "hb2", "hb3", "hb4", "hb5", "hb6", "hb7"}

    def _is_psum(self, key):
        base = key if isinstance(key, str) else key[0]
        return base in self.PSUM_KEYS

    def _deps(self, reads, writes, eng=None):
        deps = {}

        def add(k, v):
            if deps.get(k, 0) < v:
                deps[k] = v
        for r in reads:
            ent = self.res.get(r)
            if ent is not None and ent[0] is not None:
                add(*ent[0])
            if ent is not None and self._is_psum(r):
                for k, v in ent[1].items():
                    if k != ("eng", eng):
                        add(k, v)
        for w in writes:
            ent = self.res.get(w)
            if ent is not None:
                if ent[0] is not None:
                    add(*ent[0])
                for k, v in ent[1].items():
                    add(k, v)
        return deps

    def _emit_waits(self, eng, deps):
        for k, v in deps.items():
            if k == ("eng", "pe") and eng == "pe":
                continue
            if k[0] == "eng":
                assert v <= self.cnt[k[1]], f"wait on future signal {k} {v} > {self.cnt[k[1]]}"
            else:
                assert v <= self.chans[k[1]].count
            if self.waited[eng].get(k, 0) >= v:
                continue
            self.waited[eng][k] = v
            self.prog[eng].append(("wait", self.sem[k], v))

    def _record(self, tok, reads, writes):
        for r in reads:
            ent = self.res.setdefault(r, [None, {}])
            if ent[1].get(tok[0], 0) < tok[1]:
                ent[1][tok[0]] = tok[1]
        for w in writes:
            self.res[w] = [tok, {}]

    def op(self, eng, fn, reads=(), writes=(), signal=True):
        if self.rec is not None:
            self.rec.append(("op", (eng, fn), dict(reads=list(reads), writes=list(writes), signal=signal)))
            return None
        deps = self._deps(reads, writes, eng)
        self._emit_waits(eng, deps)
        if signal:
            self.cnt[eng] += 1
            tok = (("eng", eng), self.cnt[eng])
            self.prog[eng].append(("op", fn, self.sem[("eng", eng)], 1))
        else:
            tok = (("eng", eng), self.cnt[eng] + 1)
            self.prog[eng].append(("op", fn, None, 0))
        self.nops[eng] += 1
        self._record(tok, reads, writes)
        return tok

    def dma(self, q, out, in_, chan, reads=(), writes=(), **kw):
        if self.rec is not None:
            self.rec.append(("dma", (q, out, in_, chan), dict(reads=list(reads), writes=list(writes), **kw)))
            return None
        c = self.chan(chan)
        deps = self._deps(reads, writes)
        self._emit_waits(q, deps)
        c.count += 16
        tok = (("chan", chan), c.count)
        self.prog[q].append(("op", lambda e, out=out, in_=in_, kw=kw: e.dma_start(out=out, in_=in_, **kw), c.sem, 16))
        self.nops[q] += 1
        self._record(tok, reads, writes)
        return tok

    def barrier(self):
        for e in self.ENGS:
            for other in ("pe", "act", "dve", "pool"):
                if other == e or not self.cnt[other]:
                    continue
                k = ("eng", other)
                if self.waited[e].get(k, 0) < self.cnt[other]:
                    self.waited[e][k] = self.cnt[other]
                    self.prog[e].append(("wait", self.sem[k], self.cnt[other]))
            for name, c in self.chans.items():
                k = ("chan", name)
                if c.count and self.waited[e].get(k, 0) < c.count:
                    self.waited[e][k] = c.count
                    self.prog[e].append(("wait", c.sem, c.count))

    def build(self):
        for name, c in self.chans.items():
            if c.count:
                self.prog["sp"].append(("wait", c.sem, c.count))
        for e in ("pe", "act", "dve", "pool"):
            if self.cnt[e]:
                self.prog["sp"].append(("wait", self.sem[("eng", e)], self.cnt[e]))
        prog = self.prog

        def replay(name):
            def f(e):
                for ent in prog[name]:
                    if ent[0] == "wait":
                        e.wait_ge(ent[1], ent[2])
                    else:
                        ins = ent[1](e)
                        if ent[2] is not None:
                            ins.then_inc(ent[2], ent[3])
            return f

        with self.nc.Block() as block:
            block.tensor(replay("pe"))
            block.scalar(replay("act"))
            block.vector(replay("dve"))
            block.gpsimd(replay("pool"))
            block.sync(replay("sp"))


def build_nc(stop_after=None):
    nc = bass.Bass("TRN2", target_bir_lowering=False)

    def din(name, shape):
        return nc.dram_tensor(name, list(shape), F32, kind="ExternalInput").ap()

    def dout(name, shape):
        return nc.dram_tensor(name, list(shape), F32, kind="ExternalOutput").ap()

    x_in = din("x_in", [T, D])
    mem_in = din("mem_in", [NMEM, D])
    st_conv = din("st_conv", [2, 30, 512])
    st_hgrn = din("st_hgrn", [2, 4, 128, 128])
    ck_in = din("ck_in", [2, NMEM, D])
    cv_in = din("cv_in", [2, NMEM, D])
    w1g = din("w1g", [D, FF]); w1u = din("w1u", [D, FF]); w1d = din("w1d", [FF, D])
    w2g = din("w2g", [D, FF]); w2u = din("w2u", [D, FF]); w2d = din("w2d", [FF, D])
    w_in = din("w_in", [D, DIN]); w_out = din("w_out", [D, D])
    wk_d = din("wk", [D, D]); wv_d = din("wv", [D, D]); wq_d = din("wq", [D, D]); wo_d = din("wo", [D, D])
    n_ffn1 = din("n_ffn1", [D]); n_mix = din("n_mix", [D]); n_mem = din("n_mem", [D])
    n_xattn = din("n_xattn", [D]); n_ffn2 = din("n_ffn2", [D]); n_final = din("n_final", [D])
    cw_d = din("cw", [512, 31])
    cb_d = din("cb", [128, 4]); clg_d = din("clg", [128, 4]); clb_d = din("clb", [128, 4])
    lb2_d = din("lb2", [128, 8]); gn_d = din("gn", [128, 4])
    ident_d = din("ident", [128, 128])
    tri64_d = din("tri64", [128, 128]); tri32_d = din("tri32", [64, 64])
    rst64_d = din("rst64", [128, 512]); rst32_d = din("rst32", [128, 64])

    y_out = dout("y_out", [T, D])
    convp_out = dout("convp_out", [30, 512])
    hgrnp_out = dout("hgrnp_out", [4, 128, 128])
    mk_out = dout("mk_out", [NMEM, D])
    mv_out = dout("mv_out", [NMEM, D])
    convs_out = dout("convs_out", [2, 30, 512])
    hgrns_out = dout("hgrns_out", [2, 4, 128, 128])

    with ExitStack() as st:
        S = Sched(nc, st)

        def sb(ctx, name, shape, dt):
            return ctx.enter_context(nc.sbuf_tensor("sb_" + name, list(shape), dt))

        def ps(ctx, name, shape, dt=F32):
            return ctx.enter_context(nc.psum_tensor("ps_" + name, list(shape), dt))

        def A(fn, r, w, **k):
            return S.op("act", fn, reads=r, writes=w, **k)

        def V(fn, r, w, **k):
            return S.op("dve", fn, reads=r, writes=w, **k)

        def G(fn, r, w, **k):
            return S.op("pool", fn, reads=r, writes=w, **k)

        def P(fn, r, w, signal=True):
            return S.op("pe", fn, reads=r, writes=w, signal=signal)

        xres = sb(st, "xres", [128, NT, D], F32)
        gb = sb(st, "gb", [128, D], F32)
        idf = sb(st, "idf", [128, 128], F32)
        idb = sb(st, "idb", [128, 128], BF16)
        ss = sb(st, "ss", [128, NT], F32)
        lnt = sb(st, "lnt", [128, NT], F32)
        rstd = sb(st, "rstd", [128, NT], F32)
        sqj = sb(st, "sqj", [128, D], BF16)
        xnb = [sb(st, "xnb0", [128, D], BF16)] * 2

        XK = [("x", t) for t in range(NT)]

        S.dma("sp", idf[:], ident_d, "ld_idf", writes=["idf"])
        for g4 in range(4):
            S.dma("sp", xres[:, 4 * g4:4 * g4 + 4, :],
                  x_in[g4 * 512:(g4 + 1) * 512, :].rearrange("(t p) d -> p t d", p=128),
                  f"ldx{g4}", writes=[("x", 4 * g4 + i) for i in range(4)])
        V(lambda e: e.memset(xres[64:128, 16, :], 0.0), [], [("xpad",)])
        S.dma("sp", xres[0:64, 16, :], x_in[2048:2112, :], "ldx4", writes=[("x", 16)], reads=[("xpad",)])
        V(lambda e: e.tensor_copy(out=idb[:], in_=idf[:]), ["idf"], ["idb"])

        tile_rows = lambda t: 64 if t == 16 else 128

        def load_gain(g_dram):
            S.dma("sp", gb[:], g_dram.partition_broadcast(128), "ld_gb", writes=["gb"])

        def norm_stats():
            for t0 in range(0, NT, 4):
                t1 = min(NT, t0 + 4)
                ck = t0 // 4
                for t in range(t0, t1):
                    A(lambda e, t=t: e.activation(out=sqj[:], in_=xres[:, t, :], func=AF.Square, accum_out=ss[:, t:t + 1]),
                      [("x", t), ("xpad",)], ["sqj", ("ss", t)])
                A(lambda e, t0=t0, t1=t1: e.activation(out=lnt[:, t0:t1], in_=ss[:, t0:t1], func=AF.Ln, bias=EPS, scale=1.0 / D),
                  [("ss", t) for t in range(t0, t1)], [("lnt", ck)])
                A(lambda e, t0=t0, t1=t1: e.activation(out=rstd[:, t0:t1], in_=lnt[:, t0:t1], func=AF.Exp, scale=-0.5), [("lnt", ck)], [("rstd", ck)])

        def norm_tile_T(t, pT, pTkey, dst_ap, dst_keys, par, xb1=None, cp_eng="act"):
            if par == 0:
                xb, xk = xnb[0], "xnb"
            elif xb1 is not None:
                xb, xk = xb1, "xnb1"
            else:
                xb, xk = sqj, "sqj"
            V(lambda e: e.scalar_tensor_tensor(out=xb[:], in0=xres[:, t, :], scalar=rstd[:, t:t + 1], in1=gb[:],
                                               op0=ALU.mult, op1=ALU.mult),
              [("x", t), ("xpad",), ("rstd", t // 4), "gb"], [xk])
            for k in range(KC):
                P(lambda e, k=k: e.transpose(out=pT[:, k * 128:(k + 1) * 128], in_=xb[:, k * 128:(k + 1) * 128], identity=idb[:]),
                  [xk, "idb"], [pTkey], signal=(k == KC - 1))
            if cp_eng == "act":
                A(lambda e: e.copy(out=dst_ap, in_=pT[:, 0:1024].rearrange("p (k t) -> p k t", k=KC)), [pTkey], dst_keys)
            else:
                V(lambda e: e.tensor_copy(out=dst_ap, in_=pT[:, 0:1024].rearrange("p (k t) -> p k t", k=KC)), [pTkey], dst_keys)

        def ffn_phase(tag, g_dram, wg_d, wu_d, wd_d, final=False):
            GF = 4
            groups = []
            f0 = 0
            while f0 < FC:
                gf = min(GF, FC - f0)
                groups.append((f0, gf))
                f0 += gf
            NS = 2
            tgs = [(0, 512), (512, 512), (1024, 512), (1536, 512), (2048, 64)]
            with ExitStack() as ph:
                xnT = sb(ph, f"xnT_{tag}", [128, KC, TP], BF16)
                xnb1 = sb(ph, f"xnb1_{tag}", [128, D], BF16)
                wgs = [sb(ph, f"wg_{tag}{s}", [128, KC, GF * 128], BF16) for s in range(NS)]
                wus = [sb(ph, f"wu_{tag}{s}", [128, KC, GF * 128], BF16) for s in range(NS)]
                wds = [sb(ph, f"wd_{tag}{s}", [128, GF, D], BF16) for s in range(NS)]
                sg = [sb(ph, f"sg_{tag}{i}", [128, 512], F32) for i in range(2)]
                hT = [sb(ph, f"hT_{tag}{i}", [128, GF, 512], BF16) for i in range(2)]
                pT = [ps(ph, f"pT_{tag}{i}", [128, 1024], BF16) for i in range(2)]
                pg = [ps(ph, f"pg_{tag}{i}", [128, 512]) for i in range(2)]
                pu = [ps(ph, f"pu_{tag}{i}", [128, 512]) for i in range(2)]
                pd = [ps(ph, f"pd_{tag}{i}", [128, 512]) for i in range(2)]

                wg_v = wg_d.rearrange("(k p) f -> p k f", p=128)
                wu_v = wu_d.rearrange("(k p) f -> p k f", p=128)
                wd_v = wd_d.rearrange("(f p) d -> p f d", p=128)

                def load_group(gi, after=()):
                    f0, gf = groups[gi]
                    s = gi % NS
                    S.dma("pool", wgs[s][:, :, 0:gf * 128], wg_v[:, :, f0 * 128:(f0 + gf) * 128], f"ldwg{s}", writes=[("wg", tag, s)], reads=list(after))
                    S.dma("pool", wus[s][:, :, 0:gf * 128], wu_v[:, :, f0 * 128:(f0 + gf) * 128], f"ldwu{s}", writes=[("wu", tag, s)])
                    S.dma("pool", wds[s][:, 0:gf, :], wd_v[:, f0:f0 + gf, :], f"ldwd{s}", writes=[("wd", tag, s)])

                load_group(0)
                load_group(1, after=[("wg", tag, 0), ("wu", tag, 0), ("wd", tag, 0)])
                load_gain(g_dram)
                norm_stats()
                first_tiles = tgs[0][1] // 128
                for t in range(first_tiles):
                    norm_tile_T(t, pT[t % 2], ("pT", tag, t % 2), xnT[:, :, t * 128:(t + 1) * 128], [("xnT", tag, t)], t % 2, xb1=xnb1, cp_eng=("dve" if t % 2 else "act"))

                if final:
                    gb2 = sb(ph, "gb2", [128, D], F32)
                    ss2 = sb(ph, "ss2", [128, NT], F32)
                    ln2 = sb(ph, "ln2", [128, NT], F32)
                    rs2 = sb(ph, "rs2", [128, NT], F32)
                    yb = [sb(ph, f"yb{i}", [128, D], F32) for i in range(2)]
                    S.dma("sp", gb2[:], n_final.partition_broadcast(128), "ld_gb2", writes=["gb2"])

                def final_tail(ti):
                    c0, n = tgs[ti]
                    t0, t1 = c0 // 128, (c0 + n + 127) // 128
                    for t in range(t0, t1):
                        A(lambda e, t=t: e.activation(out=sqj[:], in_=xres[:, t, :], func=AF.Square, accum_out=ss2[:, t:t + 1]),
                          [("x", t)], ["sqj", ("ss2", t)])
                    A(lambda e: e.activation(out=ln2[:, t0:t1], in_=ss2[:, t0:t1], func=AF.Ln, bias=EPS, scale=1.0 / D),
                      [("ss2", t) for t in range(t0, t1)], [("ln2", ti)])
                    A(lambda e: e.activation(out=rs2[:, t0:t1], in_=ln2[:, t0:t1], func=AF.Exp, scale=-0.5), [("ln2", ti)], [("rs2", ti)])
                    for t in range(t0, t1):
                        par = t % 2
                        rows = tile_rows(t)
                        V(lambda e, t=t, par=par: e.scalar_tensor_tensor(out=yb[par][:], in0=xres[:, t, :], scalar=rs2[:, t:t + 1], in1=gb2[:],
                                                                        op0=ALU.mult, op1=ALU.mult),
                          [("x", t), ("rs2", ti), "gb2"], [("yb", par)])
                        S.dma("sp", y_out[t * 128:t * 128 + rows, :], yb[par][0:rows, :], f"st_y{par}", reads=[("yb", par)])

                items = [(gi, ti) for gi in range(len(groups)) for ti in range(len(tgs))]

                def GU(i):
                    gi, ti = items[i]
                    f0, gf = groups[gi]
                    s = gi % NS
                    c0, n = tgs[ti]
                    tiles = list(range(c0 // 128, (c0 + n + 127) // 128))
                    par = i % 2
                    rkg = [("wg", tag, s)] + [("xnT", tag, t) for t in tiles]
                    rku = [("wu", tag, s)] + [("xnT", tag, t) for t in tiles]
                    for fi in range(gf):
                        b = fi % 2
                        for k in range(KC):
                            P(lambda e, k=k, fi=fi, b=b: e.matmul(pg[b][:, 0:n], lhsT=wgs[s][:, k, fi * 128:(fi + 1) * 128],
                                                                 rhs=xnT[:, k, c0:c0 + n], start=(k == 0), stop=(k == KC - 1)),
                              rkg, [("pg", tag, b)], signal=(k == KC - 1))
                        for k in range(KC):
                            P(lambda e, k=k, fi=fi, b=b: e.matmul(pu[b][:, 0:n], lhsT=wus[s][:, k, fi * 128:(fi + 1) * 128],
                                                                 rhs=xnT[:, k, c0:c0 + n], start=(k == 0), stop=(k == KC - 1)),
                              rku, [("pu", tag, b)], signal=(k == KC - 1))
                        A(lambda e, b=b: e.activation(out=sg[b][:, 0:n], in_=pg[b][:, 0:n], func=AF.Silu),
                          [("pg", tag, b)], [("sg", tag, b)])
                        V(lambda e, b=b, fi=fi: e.tensor_tensor(out=hT[par][:, fi, 0:n], in0=sg[b][:, 0:n], in1=pu[b][:, 0:n], op=ALU.mult),
                          [("sg", tag, b), ("pu", tag, b)], [("hT", tag, par, fi)])

                dcount = [0]

                def DOWN(i):
                    gi, ti = items[i]
                    f0, gf = groups[gi]
                    s = gi % NS
                    c0, n = tgs[ti]
                    tiles = list(range(c0 // 128, (c0 + n + 127) // 128))
                    par = i % 2
                    for t in tiles:
                        lc = t * 128 - c0
                        rows = min(128, c0 + n - t * 128)
                        for dh in range(2):
                            b = dcount[0] % 2
                            dcount[0] += 1
                            for fi in range(gf):
                                P(lambda e, fi=fi, b=b, lc=lc, dh=dh, rows=rows: e.matmul(pd[b][0:rows, :], lhsT=hT[par][:, fi, lc:lc + rows],
                                                                                         rhs=wds[s][:, fi, dh * 512:(dh + 1) * 512],
                                                                                         start=(fi == 0), stop=(fi == gf - 1)),
                                  [("hT", tag, par, fi), ("wd", tag, s)], [("pd", tag, b)], signal=(fi == gf - 1))
                            V(lambda e, b=b, t=t, dh=dh, rows=rows: e.scalar_tensor_tensor(
                                out=xres[0:rows, t, dh * 512:(dh + 1) * 512], in0=pd[b][0:rows, :], scalar=0.5,
                                in1=xres[0:rows, t, dh * 512:(dh + 1) * 512], op0=ALU.mult, op1=ALU.add),
                              [("pd", tag, b), ("x", t)], [("x", t)])
                    if ti == len(tgs) - 1 and gi + NS < len(groups):
                        load_group(gi + NS)

                GU(0)
                for t in range(first_tiles, NT):
                    norm_tile_T(t, pT[t % 2], ("pT", tag, t % 2), xnT[:, :, t * 128:(t + 1) * 128], [("xnT", tag, t)], t % 2, xb1=xnb1, cp_eng=("dve" if t % 2 else "act"))
                for i in range(len(items)):
                    if i + 1 < len(items):
                        GU(i + 1)
                    DOWN(i)
                    if final and items[i][0] == len(groups) - 1:
                        final_tail(items[i][1])
                S.barrier()

        def conv_phase(ymc):
            NG = 512
            with ExitStack() as ph:
                winc = sb(ph, "winc", [128, KC, 1024], BF16)
                dg = sb(ph, "dg", [128, 124, 128], BF16)
                cw = sb(ph, "cw", [128, 4, 31], F32)
                cb = sb(ph, "cb", [128, 4], F32); clg = sb(ph, "clg", [128, 4], F32); clb = sb(ph, "clb", [128, 4], F32)
                onesln = sb(ph, "onesln", [128, 128], F32)
                xg = sb(ph, "c_xg", [128, KC, NG], BF16)
                cxnb1 = sb(ph, "c_xnb1", [128, D], BF16)
                uext = sb(ph, "uext", [128, 4, 30 + NG], F32)
                ubf = sb(ph, "ubf", [128, 4, 30 + NG], BF16)
                uextS = sb(ph, "uextS", [128, 4, 124], F32)
                sig = [sb(ph, f"sig{i}", [128, NG], F32) for i in range(2)]
                acc = sb(ph, "acc", [128, 4, NG], F32)
                ysq = sb(ph, "ysq", [128, 4, NG], F32)
                mean_sb = sb(ph, "mean_sb", [128, NG], F32)
                var_sb = sb(ph, "var_sb", [128, NG], F32)
                rs_sb = sb(ph, "rs_sb", [128, NG], F32)
                cvo = sb(ph, "cvo", [30, 512], F32)
                hist = [acc[0:30, 0, :], acc[0:30, 1, :]]

                pTs = [ps(ph, f"c_pT{i}", [128, 1024], BF16) for i in range(2)]
                pp = [ps(ph, f"c_pp{i}", [128, 512]) for i in range(2)]
                pc = [ps(ph, f"c_pc{i}", [128, 512]) for i in range(4)]
                PB = {0: (pp[0], pp[1], "c_pp0", "c_pp1"), 1: (pc[2], pc[3], "c_pc2", "c_pc3"),
                      2: (pp[0], pp[1], "c_pp0", "c_pp1"), 3: (pc[0], pc[1], "c_pc0", "c_pc1")}

                S.dma("pool", winc[:], w_in.rearrange("(k p) f -> p k f", p=128)[:, :, 0:1024], "ldwinc", writes=["winc"])
                S.dma("sp", cw[:], cw_d.rearrange("(cc p) j -> p cc j", p=128), "ld_cw", writes=["cw"])
                for (tt, dd, nm) in ((cb, cb_d, "cb"), (clg, clg_d, "clg"), (clb, clb_d, "clb")):
                    S.dma("sp", tt[:], dd, "ld_" + nm, writes=[nm])
                V(lambda e: e.memset(onesln[:], 1.0 / 512.0), [], ["onesln"])
                G(lambda e: e.memset(ubf[:, :, 0:30], 0.0), [], [("ubh",)])
                for cc in range(4):
                    V(lambda e, cc=cc: e.tensor_tensor(out=dg[:, cc * 31:(cc + 1) * 31, :],
                                                       in0=idb[:, :].unsqueeze(1).to_broadcast([128, 31, 128]),
                                                       in1=cw[:, cc, :].unsqueeze(2).to_broadcast([128, 31, 128]), op=ALU.mult),
                      ["idb", "cw"], [("dg", cc * 31 + j) for j in range(31)])
                DG = [("dg", i) for i in range(124)]
                load_gain(n_mix)
                norm_stats()

                groups = [("p", g * NG, NG, [4 * g + i for i in range(4)]) for g in range(NPR // NG)] + [("s", NPR, 64, [16])]

                def do_group(gidx, kind, c0, n, tiles, nxt_tiles):
                    if gidx == 0:
                        for ti, t in enumerate(tiles):
                            norm_tile_T(t, pTs[ti % 2], ("c_pT", ti % 2), xg[:, :, ti * 128:(ti + 1) * 128], [("c_xg", ti)], ti % 2, xb1=cxnb1)
                    XG = [("c_xg", ti) for ti in range(len(tiles))]
                    if kind == "s":
                        HK = [("acc", cc) for cc in range(4)]
                        for i in range(2):
                            S.dma("sp", hist[i], st_conv[i], f"ld_hist{i}", writes=HK if i == 0 else [("hist1",)])
                        for i in range(2):
                            for cc in range(4):
                                P(lambda e, i=i, cc=cc: e.transpose(out=pc[0][:, cc * 32:cc * 32 + 30], in_=hist[i][:, cc * 128:(cc + 1) * 128],
                                                                    identity=idf[0:30, 0:30]),
                                  HK + [("hist1",), "idf"], ["c_pc0"], signal=(cc == 3))
                            V(lambda e, i=i: e.tensor_copy(out=uextS[:, :, i * 62:i * 62 + 30],
                                                           in_=pc[0][:, 0:128].rearrange("p (c j) -> p c j", c=4)[:, :, 0:30]),
                              ["c_pc0"], [("uSh", i)])
                    for cc in range(4):
                        bv, bg, kv_, kg_ = PB[cc]
                        for (bank, bkey, coff) in ((bv, kv_, 0), (bg, kg_, 512)):
                            for k in range(KC):
                                P(lambda e, k=k, cc=cc, bank=bank, coff=coff: e.matmul(
                                    bank[:, 0:n], lhsT=winc[:, k, coff + cc * 128:coff + (cc + 1) * 128], rhs=xg[:, k, 0:n],
                                    start=(k == 0), stop=(k == KC - 1)),
                                  XG + ["winc"], [bkey], signal=(k == KC - 1))
                        sgi = sig[cc % 2]
                        A(lambda e, sgi=sgi, bg=bg: e.activation(out=sgi[:, 0:n], in_=bg[:, 0:n], func=AF.Sigmoid), [kg_], [("sig", cc % 2)])
                        if kind == "p":
                            V(lambda e, cc=cc, sgi=sgi, bv=bv: e.tensor_tensor(out=ubf[:, cc, 30:30 + n], in0=bv[:, 0:n], in1=sgi[:, 0:n], op=ALU.mult),
                              [kv_, ("sig", cc % 2)], [("ub", cc)])
                            if gidx == NPR // NG - 1:
                                V(lambda e, cc=cc, sgi=sgi, bv=bv: e.tensor_tensor(out=uext[:, cc, n:n + 30], in0=bv[:, n - 30:n], in1=sgi[:, n - 30:n], op=ALU.mult),
                                  [kv_, ("sig", cc % 2)], [("u", cc)])
                        else:
                            for i in range(2):
                                V(lambda e, cc=cc, sgi=sgi, i=i, bv=bv: e.tensor_tensor(out=uextS[:, cc, i * 62 + 30:i * 62 + 62], in0=bv[:, i * 32:(i + 1) * 32],
                                                                                       in1=sgi[:, i * 32:(i + 1) * 32], op=ALU.mult),
                                  [kv_, ("sig", cc % 2)], [("uS", cc, i)])
                    if kind == "p":
                        L = n
                        UB = [("ub", cc) for cc in range(4)] + [("ubh",)]
                    else:
                        L = 94
                        G(lambda e: e.tensor_copy(out=ubf[:, :, 0:124], in_=uextS[:, :, :]),
                          [("uS", cc, i) for cc in range(4) for i in range(2)] + [("uSh", 0), ("uSh", 1)], [("ub", cc) for cc in range(4)] + [("ubh",)])
                        UB = [("ub", cc) for cc in range(4)] + [("ubh",)]
                    for cc in range(4):
                        for j in range(31):
                            P(lambda e, cc=cc, j=j: e.matmul(pc[cc][:, 0:L], lhsT=dg[:, cc * 31 + j, :], rhs=ubf[:, cc, j:j + L],
                                                             start=(j == 0), stop=(j == 30)),
                              UB + [("dg", cc * 31 + j)], [f"c_pc{cc}"], signal=(j == 30))
                        if cc < len(nxt_tiles):
                            norm_tile_T(nxt_tiles[cc], pTs[cc % 2], ("c_pT", cc % 2), xg[:, :, cc * 128:(cc + 1) * 128], [("c_xg", cc)], cc % 2, xb1=cxnb1)
                    if kind == "p" and gidx < NPR // NG - 1:
                        G(lambda e: e.tensor_copy(out=ubf[:, :, 0:30], in_=ubf[:, :, NG:NG + 30]), [("ub", cc) for cc in range(4)], [("ubh",)])
                    for cc in range(4):
                        if kind == "p":
                            pieces = [(0, 0, n)]
                        else:
                            pieces = [(0, 0, 32), (32, 62, 32)]
                        for (d0, s0, ln) in pieces:
                            A(lambda e, cc=cc, d0=d0, s0=s0, ln=ln: e.activation(out=acc[:, cc, d0:d0 + ln], in_=pc[cc][:, s0:s0 + ln], func=AF.Identity,
                                                                               bias=cb[:, cc:cc + 1], scale=1.0),
                              [f"c_pc{cc}", "cb"], [("acc", cc)])
                            V(lambda e, cc=cc, d0=d0, ln=ln: e.tensor_tensor(out=ysq[:, cc, d0:d0 + ln], in0=acc[:, cc, d0:d0 + ln],
                                                                            in1=acc[:, cc, d0:d0 + ln], op=ALU.mult),
                              [("acc", cc)], [("ysq", cc)])
                    ACC = [("acc", cc) for cc in range(4)]
                    YSQ = [("ysq", cc) for cc in range(4)]
                    for cc in range(4):
                        P(lambda e, cc=cc: e.matmul(pp[0][:, 0:n], lhsT=onesln[:], rhs=acc[:, cc, 0:n], start=(cc == 0), stop=(cc == 3)),
                          ACC + ["onesln"], ["c_pp0"], signal=(cc == 3))
                    for cc in range(4):
                        P(lambda e, cc=cc: e.matmul(pp[1][:, 0:n], lhsT=onesln[:], rhs=ysq[:, cc, 0:n], start=(cc == 0), stop=(cc == 3)),
                          YSQ + ["onesln"], ["c_pp1"], signal=(cc == 3))
                    A(lambda e: e.copy(out=mean_sb[:, 0:n], in_=pp[0][:, 0:n]), ["c_pp0"], ["mean_sb"])
                    V(lambda e: e.tensor_tensor(out=var_sb[:, 0:n], in0=mean_sb[:, 0:n], in1=mean_sb[:, 0:n], op=ALU.mult), ["mean_sb"], ["var_sb"])
                    V(lambda e: e.tensor_tensor(out=var_sb[:, 0:n], in0=pp[1][:, 0:n], in1=var_sb[:, 0:n], op=ALU.subtract), ["c_pp1", "var_sb"], ["var_sb"])
                    A(lambda e: e.activation(out=var_sb[:, 0:n], in_=var_sb[:, 0:n], func=AF.Ln, bias=EPS, scale=1.0), ["var_sb"], ["var_sb"])
                    A(lambda e: e.activation(out=rs_sb[:, 0:n], in_=var_sb[:, 0:n], func=AF.Exp, scale=-0.5), ["var_sb"], ["rs_sb"])
                    V(lambda e: e.tensor_tensor(out=ysq[:, :, 0:n], in0=acc[:, :, 0:n], in1=mean_sb[:, 0:n].unsqueeze(1).to_broadcast([128, 4, n]),
                                                op=ALU.subtract), ACC + YSQ + ["mean_sb"], YSQ)
                    V(lambda e: e.tensor_tensor(out=ysq[:, :, 0:n], in0=ysq[:, :, 0:n], in1=rs_sb[:, 0:n].unsqueeze(1).to_broadcast([128, 4, n]),
                                                op=ALU.mult), YSQ + ["rs_sb"], YSQ)
                    for cc in range(4):
                        A(lambda e, cc=cc: e.activation(out=ymc[:, cc, c0:c0 + n], in_=ysq[:, cc, 0:n], func=AF.Silu, bias=clb[:, cc:cc + 1], scale=clg[:, cc:cc + 1]),
                          YSQ + ["clg", "clb"], [("ymc", cc, gidx)])

                    def conv_out(src_ap, dst_ap, srck):
                        for cc in range(4):
                            P(lambda e, cc=cc: e.transpose(out=pp[0][0:30, cc * 128:(cc + 1) * 128], in_=src_ap[:, cc, :], identity=idf[:]),
                              srck + ["idf"], ["c_pp0"], signal=(cc == 3))
                        V(lambda e: e.tensor_copy(out=cvo[:], in_=pp[0][0:30, :]), ["c_pp0"], ["cvo"])
                        S.dma("sp", dst_ap, cvo[:], "st_cvo", reads=["cvo"])
                    if kind == "p" and gidx == NPR // NG - 1:
                        conv_out(uext[:, :, NG:NG + 30], convp_out, [("u", cc) for cc in range(4)])
                    if kind == "s":
                        for i in range(2):
                            conv_out(uextS[:, :, i * 62 + 32:i * 62 + 62], convs_out[i], [("uS", cc, i) for cc in range(4)])

                for gidx, (kind, c0, n, tiles) in enumerate(groups):
                    do_group(gidx, kind, c0, n, tiles, groups[gidx + 1][3] if gidx + 1 < len(groups) else [])
                S.barrier()

        def hgrn_phase(ymc):
            NB = 512
            with ExitStack() as ph:
                win = sb(ph, "win", [128, KC, 2048], BF16)
                wout = sb(ph, "wout", [128, KC, D], BF16)
                lb2 = sb(ph, "lb2", [128, 8], F32); gn = sb(ph, "gn", [128, 4], F32)
                lb = sb(ph, "lb", [128, 4], F32); oml = sb(ph, "oml", [128, 4], F32); lbd = sb(ph, "lbd", [128, 4], F32)
                fc1 = sb(ph, "fc1", [128, 4], F32); fc0 = sb(ph, "fc0", [128, 4], F32)
                tri64 = sb(ph, "tri64", [128, 128], F32); tri32 = sb(ph, "tri32", [64, 64], F32)
                rstP = sb(ph, "rstP", [128, 512], F32); rstS = sb(ph, "rstS", [128, 64], F32)
                onesrm = sb(ph, "onesrm", [128, 128], F32)
                xg = sb(ph, "h_xg", [128, KC, NB], BF16)
                qd = sb(ph, "qd", [128, 4, NB], BF16)
                kd = sb(ph, "kd", [128, 4, NB], BF16)
                vtm = sb(ph, "vtm", [128, 4, 512], BF16)
                sgate = sb(ph, "sgate", [128, 4, NB], F32)
                egl = sb(ph, "egl", [128, 4, 8], F32)
                qs = [sb(ph, "qs0", [128, NB], F32)] * 2
                fgt = [sb(ph, "fgt0", [128, NB], F32)] * 2
                logf = [sb(ph, "logf0", [128, NB], F32)] * 2
                gc = [sb(ph, "gc0", [128, NB], F32)] * 2
                eg = [sb(ph, "eg0", [128, NB], F32)] * 2
                osq = [sb(ph, "osq0", [128, 4, 128], F32)] * 2
                rso = [sb(ph, "rso0", [128, 4, 128], F32)] * 2
                ymh = [sb(ph, f"ymh{i}", [128, 4, 128], BF16) for i in range(2)]
                kdT = [sb(ph, f"kdT{i}", [128, 512], BF16) for i in range(2)]
                ATm = [sb(ph, f"ATm{i}", [128, 4, 128], BF16) for i in range(2)]
                Sst = [sb(ph, f"Sst{i}", [128, 4, 128], F32) for i in range(3)]
                Sbp = [sb(ph, f"Sbp{i}", [128, 4, 128], BF16) for i in range(2)]
                SbS = [sb(ph, f"SbS{i}", [128, 4, 128], BF16) for i in range(2)]
                Rt = sb(ph, "Rt", [128, 4, 128], F32)
                f2v = lambda tns: tns[:, :, :].rearrange("p a b -> p (a b)")
                qs = [qs[0], f2v(Rt)]; qsk = ["qs", "Rt"]
                fgt = [fgt[0], f2v(osq[0])]; fgk = ["fgt", "osq"]
                logf = [logf[0], f2v(rso[0])]; lfk = ["logf", "rso"]
                gc = [gc[0], xnb[0][:, :].bitcast(F32)]; gck = ["gc", "xnb"]
                eg = [eg[0], sqj[:, :].bitcast(F32)]; egk = ["eg", "sqj"]

                bk = [ps(ph, f"hb{i}", [128, 512]) for i in range(8)]
                bkk = [f"hb{i}" for i in range(8)]
                bT = bk[0][:, :].bitcast(BF16)
                bTx = bk[7][:, :].bitcast(BF16)

                win_v = w_in.rearrange("(k p) f -> p k f", p=128)
                for part in (2, 0, 3, 1):
                    S.dma("pool", win[:, :, part * 512:(part + 1) * 512], win_v[:, :, 1024 + part * 512:1024 + (part + 1) * 512],
                          f"ldwin{part}", writes=[("win", part)], reads=([] if part == 2 else [("win", 2)]))
                S.dma("pool", wout[:], w_out.rearrange("(k p) f -> p k f", p=128), "ldwout", writes=["wout"])
                for (tt, dd, nm) in ((lb2, lb2_d, "lb2"), (gn, gn_d, "gn"), (tri64, tri64_d, "tri64"), (tri32, tri32_d, "tri32"),
                                     (rstP, rst64_d, "rstP"), (rstS, rst32_d, "rstS")):
                    S.dma("sp", tt[:], dd, "ld_" + nm, writes=[nm])
                for i in range(2):
                    S.dma("sp", Sst[1 + i][:], st_hgrn[i].rearrange("h k v -> k h v"), f"ld_S{i}", writes=[("Sst", 1 + i)])
                V(lambda e: e.memset(onesrm[:], 1.0 / 128.0), [], ["onesrm"])
                V(lambda e: e.memset(Sst[0][:], 0.0), [], [("Sst", 0)])
                V(lambda e: e.memset(Sbp[0][:], 0.0), [], [("Sbp", 0)])
                for i in range(2):
                    V(lambda e, i=i: e.tensor_copy(out=SbS[i][:], in_=Sst[1 + i][:]), [("Sst", 1 + i)], [("SbS", i)])
                V(lambda e: e.tensor_tensor(out=lbd[:], in0=lb2[:, 0:4], in1=lb2[:, 4:8], op=ALU.subtract), ["lb2"], ["lbd"])
                A(lambda e: e.activation(out=lb[:], in_=lbd[:], func=AF.Sigmoid), ["lbd"], ["lb"])
                V(lambda e: e.tensor_scalar(out=oml[:], in0=lb[:], scalar1=-1.0, scalar2=1.0, op0=ALU.mult, op1=ALU.add), ["lb"], ["oml"])
                V(lambda e: e.tensor_scalar(out=fc1[:], in0=oml[:], scalar1=0.5, scalar2=None, op0=ALU.mult), ["oml"], ["fc1"])
                V(lambda e: e.tensor_tensor(out=fc0[:], in0=lb[:], in1=fc1[:], op=ALU.add), ["lb", "fc1"], ["fc0"])
                gnb = gn[:, :].unsqueeze(2).to_broadcast([128, 4, 128])

                blocks = [("p", b * NB, NB, [4 * b + i for i in range(4)]) for b in range(NPR // NB)] + [("s", NPR, 64, [16])]
                f2 = lambda ap: ap.rearrange("p a b -> p (a b)")
                v3 = lambda bank: bank[:, :].rearrange("p (a b) -> p a b", a=4)
                jgc = [0]

                def pass_x_norm(tiles, ti):
                    norm_tile_T(tiles[ti], bTx, "hb7", xg[:, :, ti * 128:(ti + 1) * 128], [("h_xg", ti)], ti % 2)
                    rows = 128 if tiles[ti] < 16 else 64
                    for k in range(KC):
                        P(lambda e, k=k: e.matmul(bk[7][0:rows, :], lhsT=xg[:, k, ti * 128:ti * 128 + rows], rhs=win[:, k, 1024:1536],
                                                  start=(k == 0), stop=(k == KC - 1)),
                          [("h_xg", ti), ("win", 2)], ["hb7"], signal=(k == KC - 1))
                    A(lambda e: e.copy(out=vtm[0:rows, ti, :], in_=bk[7][0:rows, :]), ["hb7"], [("vtm", ti)])

                def pass_x(kind, c0, n, tiles):
                    rows = 128 if kind == "p" else 64
                    XG = [("h_xg", ti) for ti in range(len(tiles))]
                    rst, rstk = (rstP, "rstP") if kind == "p" else (rstS, "rstS")
                    csz = 64 if kind == "p" else 32
                    nch = n // csz
                    def head_ops(h):
                        hp = h % 2
                        b3 = (2, 3, 4) if hp == 0 else (5, 6, 7)
                        for (bi, coff, part) in ((b3[0], 0, 0), (b3[1], 1536, 3), (b3[2], 512, 1)):
                            for k in range(KC):
                                P(lambda e, k=k, h=h, bi=bi, coff=coff: e.matmul(bk[bi][:, 0:n], lhsT=win[:, k, coff + h * 128:coff + (h + 1) * 128],
                                                                                rhs=xg[:, k, 0:n], start=(k == 0), stop=(k == KC - 1)),
                                  XG + [("win", part)], [bkk[bi]], signal=(k == KC - 1))
                        bq, bg, bf_ = bk[b3[0]], bk[b3[1]], bk[b3[2]]
                        kq, kg, kf = bkk[b3[0]], bkk[b3[1]], bkk[b3[2]]
                        A(lambda e, hp=hp, bq=bq: e.activation(out=qs[hp][:, 0:n], in_=bq[:, 0:n], func=AF.Silu), [kq], [qsk[hp]])
                        A(lambda e, h=h, bg=bg: e.activation(out=sgate[:, h, 0:n], in_=bg[:, 0:n], func=AF.Silu), [kg], [("sgate", h)])
                        A(lambda e, hp=hp, bf_=bf_: e.activation(out=fgt[hp][:, 0:n], in_=bf_[:, 0:n], func=AF.Tanh, scale=0.5), [kf], [fgk[hp]])
                        V(lambda e, hp=hp, h=h: e.tensor_scalar(out=fgt[hp][:, 0:n], in0=fgt[hp][:, 0:n], scalar1=fc1[:, h:h + 1], scalar2=fc0[:, h:h + 1],
                                                              op0=ALU.mult, op1=ALU.add), [fgk[hp], "fc1", "fc0"], [fgk[hp]])
                        A(lambda e, hp=hp: e.activation(out=logf[hp][:, 0:n], in_=fgt[hp][:, 0:n], func=AF.Ln), [fgk[hp]], [lfk[hp]])
                        V(lambda e, hp=hp: e.tensor_tensor_scan(out=gc[hp][:, 0:n], data0=rst[:, 0:n], data1=logf[hp][:, 0:n], initial=0.0,
                                                               op0=ALU.mult, op1=ALU.add), [lfk[hp], rstk], [gck[hp]])
                        V(lambda e, hp=hp: e.tensor_scalar(out=fgt[hp][:, 0:n], in0=fgt[hp][:, 0:n], scalar1=-1.0, scalar2=1.0, op0=ALU.mult, op1=ALU.add),
                          [fgk[hp], lfk[hp]], [fgk[hp]])
                        A(lambda e, hp=hp: e.activation(out=eg[hp][:, 0:n], in_=gc[hp][:, 0:n], func=AF.Exp), [gck[hp]], [egk[hp]])
                        A(lambda e, hp=hp: e.activation(out=gc[hp][:, 0:n], in_=gc[hp][:, 0:n], func=AF.Exp, scale=-1.0), [gck[hp], egk[hp]], [gck[hp]])
                        V(lambda e, hp=hp, h=h: e.tensor_tensor(out=qd[:, h, 0:n], in0=qs[hp][:, 0:n], in1=eg[hp][:, 0:n], op=ALU.mult),
                          [qsk[hp], egk[hp]], [("qd", h)])
                        V(lambda e, hp=hp, h=h: e.tensor_tensor(out=kd[:, h, 0:n], in0=fgt[hp][:, 0:n], in1=gc[hp][:, 0:n], op=ALU.mult),
                          [fgk[hp], gck[hp]], [("kd", h)])
                        G(lambda e, hp=hp, h=h: e.tensor_copy(out=egl[:, h, 0:nch], in_=eg[hp][:, 0:n].rearrange("p (c j) -> p c j", j=csz)[:, :, csz - 1]),
                          [egk[hp]], [("egl", h)])

                    hl = [S.record(lambda h=h: head_ops(h)) for h in range(4)]
                    for l_ in hl:
                        assert len(l_) == 36, len(l_)
                    for p0 in (0, 2):
                        la, lb = hl[p0], hl[p0 + 1]
                        for part in (la[:28], lb[:28], la[28:29], lb[28:29]):
                            S.emit_interleaved(part)
                        S.emit_interleaved(la[29:], lb[29:])

                def pass_y1(kind, c0, ti, t, yp):
                    rows = 128 if kind == "p" else 64
                    lo = ti * 128
                    pAT, po, pPc = bk[1 + yp], bk[3 + yp], bk[5 + yp]
                    kAT, ko, kPc = bkk[1 + yp], bkk[3 + yp], bkk[5 + yp]
                    KD = [("kd", h) for h in range(4)]
                    QD = [("qd", h) for h in range(4)]
                    for h in range(4):
                        P(lambda e, h=h: e.transpose(out=bT[0:rows, h * 128:(h + 1) * 128], in_=kd[:, h, lo:lo + rows], identity=idb[:]),
                          KD + ["idb"], ["hb0"], signal=(h == 3))
                    A(lambda e: e.copy(out=kdT[yp][0:rows, :], in_=bT[0:rows, 0:512]), ["hb0"], [("kdT", yp)])
                    if kind == "p":
                        chunks = [(0, 64, 0), (64, 64, 0)]
                        tri, trik = tri64, "tri64"
                    else:
                        chunks = [(0, 32, 1), (32, 32, 2)]
                        tri, trik = tri32, "tri32"
                    for h in range(4):
                        P(lambda e, h=h: e.matmul(pAT[0:rows, h * 128:h * 128 + rows], lhsT=kd[:, h, lo:lo + rows], rhs=qd[:, h, lo:lo + rows],
                                                  start=True, stop=True),
                          KD + QD, [kAT], signal=(h == 3))
                    V(lambda e: e.tensor_tensor(out=ATm[yp][0:rows, :, 0:rows], in0=v3(pAT)[0:rows, :, 0:rows],
                                                in1=tri[0:rows, 0:rows].unsqueeze(1).to_broadcast([rows, 4, rows]), op=ALU.mult),
                      [kAT, trik], [("ATm", yp)])
                    for ci, (cc0, cl, sidx) in enumerate(chunks):
                        if kind == "p":
                            jg = jgc[0]
                            jgc[0] += 1
                            Sb_cur, Sbk_cur = Sbp[jg % 2], ("Sbp", jg % 2)
                            Sb_nxt, Sbk_nxt = Sbp[(jg + 1) % 2], ("Sbp", (jg + 1) % 2)
                        else:
                            Sb_cur, Sbk_cur = SbS[sidx - 1], ("SbS", sidx - 1)
                            Sb_nxt, Sbk_nxt = None, None
                        for h in range(4):
                            P(lambda e, h=h, cc0=cc0, cl=cl: e.matmul(pPc[:, h * 128:(h + 1) * 128], lhsT=kdT[yp][cc0:cc0 + cl, h * 128:(h + 1) * 128],
                                                                     rhs=vtm[cc0:cc0 + cl, ti, h * 128:(h + 1) * 128], start=True, stop=True),
                              [("kdT", yp), ("vtm", ti)], [kPc], signal=(h == 3))
                        for h in range(4):
                            P(lambda e, h=h, cc0=cc0, cl=cl: e.matmul(po[:, h * 128 + cc0:h * 128 + cc0 + cl], lhsT=vtm[0:rows, ti, h * 128:(h + 1) * 128],
                                                                     rhs=ATm[yp][0:rows, h, cc0:cc0 + cl], start=True, stop=False),
                              [("vtm", ti), ("ATm", yp)], [ko], signal=False)
                            P(lambda e, h=h, cc0=cc0, cl=cl, Sb_cur=Sb_cur: e.matmul(po[:, h * 128 + cc0:h * 128 + cc0 + cl], lhsT=Sb_cur[:, h, :],
                                                                                    rhs=qd[:, h, lo + cc0:lo + cc0 + cl], start=False, stop=True),
                              [Sbk_cur] + QD, [ko], signal=(h == 3))
                        V(lambda e, sidx=sidx: e.tensor_tensor(out=Rt[:], in0=v3(pPc), in1=Sst[sidx][:], op=ALU.add), [kPc, ("Sst", sidx)], ["Rt"])
                        cidx = (lo + cc0) // cl
                        V(lambda e, sidx=sidx, cidx=cidx: e.tensor_tensor(out=Sst[sidx][:], in0=Rt[:],
                                                                         in1=egl[:, :, cidx:cidx + 1].to_broadcast([128, 4, 128]), op=ALU.mult),
                          ["Rt"] + [("egl", h) for h in range(4)], [("Sst", sidx)])
                        if Sb_nxt is not None:
                            A(lambda e, sidx=sidx, Sb_nxt=Sb_nxt: e.copy(out=Sb_nxt[:], in_=Sst[sidx][:]), [("Sst", sidx)], [Sbk_nxt])

                def pass_y2(kind, c0, ti, t, yp, last):
                    rows = 128 if kind == "p" else 64
                    lo = ti * 128
                    pAT, po, pPc = bk[1 + yp], bk[3 + yp], bk[5 + yp]
                    kAT, ko, kPc = bkk[1 + yp], bkk[3 + yp], bkk[5 + yp]
                    SG = [("sgate", h) for h in range(4)]
                    A(lambda e: e.activation(out=osq[yp][:], in_=v3(po), func=AF.Square), [ko], ["osq"])
                    P(lambda e: e.matmul(pAT[:, :], lhsT=onesrm[:], rhs=f2(osq[yp][:]), start=True, stop=True), ["osq", "onesrm"], [kAT])
                    A(lambda e: e.activation(out=rso[yp][:], in_=v3(pAT), func=AF.Ln, bias=EPS, scale=1.0), [kAT], ["rso"])
                    A(lambda e: e.activation(out=rso[yp][:], in_=rso[yp][:], func=AF.Exp, scale=-0.5), ["rso"], ["rso"])
                    V(lambda e: e.tensor_tensor(out=osq[yp][:], in0=v3(po), in1=rso[yp][:], op=ALU.mult), [ko, "rso", "osq"], ["osq"])
                    V(lambda e: e.tensor_tensor(out=osq[yp][:], in0=osq[yp][:], in1=gnb, op=ALU.mult), ["osq", "gn"], ["osq"])
                    V(lambda e: e.tensor_tensor(out=ymh[yp][:], in0=osq[yp][:], in1=sgate[:, :, lo:lo + 128], op=ALU.mult),
                      ["osq"] + SG, [("ymh", yp)])
                    for dh, (pb, pk) in enumerate(((pAT, kAT), (pPc, kPc))):
                        for c in range(8):
                            lhs = ymc[:, c, c0 + lo:c0 + lo + rows] if c < 4 else ymh[yp][:, c - 4, 0:rows]
                            P(lambda e, c=c, lhs=lhs, dh=dh, pb=pb: e.matmul(pb[0:rows, :], lhsT=lhs, rhs=wout[:, c, dh * 512:(dh + 1) * 512],
                                                                            start=(c == 0), stop=(c == 7)),
                              [("ymh", yp), "wout"], [pk], signal=(c == 7))
                        V(lambda e, dh=dh, pb=pb: e.tensor_tensor(out=xres[0:rows, t, dh * 512:(dh + 1) * 512], in0=pb[0:rows, :],
                                                                 in1=xres[0:rows, t, dh * 512:(dh + 1) * 512], op=ALU.add),
                          [pk, ("x", t)], [("x", t)])
                    if kind == "p" and t == 15:
                        S.dma("sp", hgrnp_out.rearrange("h k v -> k h v"), Sst[0][:], "st_S0", reads=[("Sst", 0)])
                    if kind == "s":
                        for i in range(2):
                            S.dma("sp", hgrns_out[i].rearrange("h k v -> k h v"), Sst[1 + i][:], f"st_S{1 + i}", reads=[("Sst", 1 + i)])

                ytile = [0]
                for ti in range(len(blocks[0][3])):
                    pass_x_norm(blocks[0][3], ti)
                for bi, (kind, c0, n, tiles) in enumerate(blocks):
                    pass_x(kind, c0, n, tiles)
                    nxt = blocks[bi + 1][3] if bi + 1 < len(blocks) else []
                    yps = [(ytile[0] + i) % 2 for i in range(len(tiles))]
                    ytile[0] += len(tiles)
                    pass_y1(kind, c0, 0, tiles[0], yps[0])
                    for ti, t in enumerate(tiles):
                        l2 = S.record(lambda: pass_y2(kind, c0, ti, t, yps[ti], ti == len(tiles) - 1))
                        l1 = S.record(lambda: pass_y1(kind, c0, ti + 1, tiles[ti + 1], yps[ti + 1])) if ti + 1 < len(tiles) else []
                        ln_ = S.record(lambda: pass_x_norm(nxt, ti)) if ti < len(nxt) else []
                        S.emit_interleaved(l1, l2, ln_)
                S.barrier()

        def mixer_phase():
            with ExitStack() as mph:
                ymc = sb(mph, "ymc", [128, 4, TP], BF16)
                conv_phase(ymc)
                hgrn_phase(ymc)

        def attn_phase():
            with ExitStack() as ph:
                wq = sb(ph, "wq", [128, KC, D], BF16)
                wo = sb(ph, "wo", [128, KC, D], BF16)
                wkv = sb(ph, "wkv", [128, KC, D], BF16)
                memt = sb(ph, "memt", [128, 2, D], F32)
                mnT = sb(ph, "mnT", [128, KC, NMEM], BF16)
                mss = sb(ph, "mss", [128, 2], F32)
                mrs = sb(ph, "mrs", [128, 2], F32)
                KT = [sb(ph, f"KT{i}", [128, KC, NMEM], BF16) for i in range(3)]
                Vt = [sb(ph, f"Vt{i}", [128, 2, D], BF16) for i in range(3)]
                kvo = sb(ph, "kvo", [128, D], F32)
                ktm = sb(ph, "ktm", [128, 2, D], BF16)
                xg = sb(ph, "a_xg", [128, KC, 512], BF16)
                qT = sb(ph, "qT", [128, KC, 512], BF16)
                mx = sb(ph, "mx", [128, 12], F32)
                nmx = sb(ph, "nmx", [128, 12], F32)
                rsum = sb(ph, "rsum", [128, 8], F32)
                rrec = sb(ph, "rrec", [128, 8], F32)
                Pf = sb(ph, "Pf", [128, 4, NMEM], F32)
                Pn = sb(ph, "Pn", [128, 4, NMEM], BF16)
                PnT = sb(ph, "PnT", [128, 8, 512], BF16)
                oT = sb(ph, "oT", [128, KC, 512], BF16)

                pTs = [ps(ph, f"a_pT{i}", [128, 1024], BF16) for i in range(2)]
                pA = [ps(ph, f"a_pA{i}", [128, 512]) for i in range(2)]
                pS = [ps(ph, f"a_pS{i}", [128, 512]) for i in range(2)]
                pO = [ps(ph, f"a_pO{i}", [128, 512]) for i in range(2)]

                _astop = int(os.environ.get("MK_ATTN_STOP", "99"))
                wvA = PnT[:, :, :].rearrange("p a b -> p (a b)").rearrange("p (k f) -> p k f", k=4)
                wvB = oT[:, :, :].rearrange("p a b -> p (a b)").rearrange("p (k f) -> p k f", k=4)
                wv_v = wv_d.rearrange("(k p) f -> p k f", p=128)
                S.dma("sp", memt[:], mem_in.rearrange("(t p) d -> p t d", p=128), "ld_mem", writes=["memt"])
                ktmB = [ktm[:, :, :], kvo[:, :].bitcast(BF16).rearrange("p (t d) -> p t d", t=2)]
                KTK = [["ktm"], [("kvo", 0), ("kvo", 1)]]
                for i in range(2):
                    S.dma("pool", ktmB[i], ck_in[i].rearrange("(t p) d -> p t d", p=128), f"ld_ck{i}", writes=KTK[i])
                S.dma("pool", wkv[:], wk_d.rearrange("(k p) f -> p k f", p=128), "ld_wkv", writes=["wkv"])
                S.dma("pool", wvA, wv_v[:, 0:4, :], "ld_wvA", writes=["wvA"], reads=["wkv"])
                S.dma("pool", wvB, wv_v[:, 4:8, :], "ld_wvB", writes=["wvB"])
                for i in range(2):
                    S.dma("pool", Vt[1 + i][:], cv_in[i].rearrange("(t p) d -> p t d", p=128), f"ld_cv{i}", writes=[("Vt", 1 + i)])
                S.dma("pool", wq[:], wq_d.rearrange("(k p) f -> p k f", p=128), "ld_wq", writes=["wq"])
                S.dma("pool", wo[:], wo_d.rearrange("(k p) f -> p k f", p=128), "ld_wo", writes=["wo"])
                norm_stats()
                for i in range(2):
                    for mt in range(2):
                        for k in range(KC):
                            P(lambda e, k=k, mt=mt, i=i: e.transpose(out=pTs[mt % 2][:, k * 128:(k + 1) * 128], in_=ktmB[i][:, mt, k * 128:(k + 1) * 128], identity=idb[:]),
                              KTK[i] + ["idb"], [("a_pT", mt % 2)], signal=(k == KC - 1))
                        A(lambda e, mt=mt, i=i: e.copy(out=KT[1 + i][:, :, mt * 128:(mt + 1) * 128], in_=pTs[mt % 2][:, 0:1024].rearrange("p (k t) -> p k t", k=KC)),
                          [("a_pT", mt % 2)], [("KT", 1 + i)])

                if _astop <= 0:
                    S.barrier(); return
                load_gain(n_mem)
                for mt in range(2):
                    A(lambda e, mt=mt: e.activation(out=sqj[:], in_=memt[:, mt, :], func=AF.Square, accum_out=mss[:, mt:mt + 1]), ["memt"], ["sqj", "mss"])
                A(lambda e: e.activation(out=mrs[:], in_=mss[:], func=AF.Ln, bias=EPS, scale=1.0 / D), ["mss"], ["mrs"])
                A(lambda e: e.activation(out=mrs[:], in_=mrs[:], func=AF.Exp, scale=-0.5), ["mrs"], ["mrs"])
                for mt in range(2):
                    V(lambda e, mt=mt: e.scalar_tensor_tensor(out=xnb[0][:], in0=memt[:, mt, :], scalar=mrs[:, mt:mt + 1], in1=gb[:],
                                                             op0=ALU.mult, op1=ALU.mult), ["memt", "mrs", "gb"], ["xnb"])
                    for k in range(KC):
                        P(lambda e, k=k, mt=mt: e.transpose(out=pTs[mt % 2][:, k * 128:(k + 1) * 128], in_=xnb[0][:, k * 128:(k + 1) * 128], identity=idb[:]),
                          ["xnb", "idb"], [("a_pT", mt % 2)], signal=(k == KC - 1))
                    A(lambda e, mt=mt: e.copy(out=mnT[:, :, mt * 128:(mt + 1) * 128], in_=pTs[mt % 2][:, 0:1024].rearrange("p (k t) -> p k t", k=KC)),
                      [("a_pT", mt % 2)], [("mnT", mt)])
                MNT = [("mnT", 0), ("mnT", 1)]

                def kv_token_major(out_dram, dst_bf, dst_key, wsel, wkeys):
                    cnt = 0
                    for mt in range(2):
                        for dh in range(2):
                            pb = pA[cnt % 2]
                            pk = ("a_pA", cnt % 2)
                            cnt += 1
                            for k in range(KC):
                                P(lambda e, k=k, mt=mt, dh=dh, pb=pb: e.matmul(pb[:, :], lhsT=mnT[:, k, mt * 128:(mt + 1) * 128],
                                                                              rhs=wsel(k, dh), start=(k == 0), stop=(k == KC - 1)),
                                  MNT + wkeys, [pk], signal=(k == KC - 1))
                            A(lambda e, dh=dh, pb=pb: e.copy(out=kvo[:, dh * 512:(dh + 1) * 512], in_=pb[:, :]), [pk], [("kvo", dh)])
                            if dst_bf is not None:
                                V(lambda e, mt=mt, dh=dh: e.tensor_copy(out=dst_bf[:, mt, dh * 512:(dh + 1) * 512], in_=kvo[:, dh * 512:(dh + 1) * 512]),
                                  [("kvo", dh)], [dst_key])
                        S.dma("sp", out_dram[mt * 128:(mt + 1) * 128, :], kvo[:], "st_kvo", reads=[("kvo", 0), ("kvo", 1)])

                if _astop <= 1:
                    S.barrier(); return
                kv_token_major(mk_out, None, None, lambda k, dh: wkv[:, k, dh * 512:(dh + 1) * 512], ["wkv"])
                for c in range(KC):
                    pb = pA[c % 2]
                    pk = ("a_pA", c % 2)
                    for k in range(KC):
                        P(lambda e, k=k, c=c, pb=pb: e.matmul(pb[:, 0:NMEM], lhsT=wkv[:, k, c * 128:(c + 1) * 128], rhs=mnT[:, k, :],
                                                             start=(k == 0), stop=(k == KC - 1)),
                          MNT + ["wkv"], [pk], signal=(k == KC - 1))
                    A(lambda e, c=c, pb=pb: e.copy(out=KT[0][:, c, :], in_=pb[:, 0:NMEM]), [pk], [("KT", 0)])
                if _astop <= 2:
                    S.barrier(); return
                kv_token_major(mv_out, Vt[0], ("Vt", 0),
                               lambda k, dh: (wvA if k < 4 else wvB)[:, k % 4, dh * 512:(dh + 1) * 512], ["wvA", "wvB"])
                if _astop <= 3:
                    S.barrier(); return
                if _astop <= 4:
                    S.barrier(); return
                load_gain(n_xattn)
                tgs = [(0, 512), (512, 512), (1024, 512), (1536, 512), (2048, 64)]

                def tg_norm(c0, n):
                    tiles = list(range(c0 // 128, (c0 + n + 127) // 128))
                    for ti, t in enumerate(tiles):
                        norm_tile_T(t, pTs[ti % 2], ("a_pT", ti % 2), xg[:, :, ti * 128:(ti + 1) * 128], [("a_xg", ti)], ti % 2,
                                    cp_eng=("dve" if ti % 2 else "act"))

                QTB = [qT, wkv[:, :, 0:512]]

                def tg_qproj(gi, c0, n):
                    tiles = list(range(c0 // 128, (c0 + n + 127) // 128))
                    XG = [("a_xg", ti) for ti in range(len(tiles))]
                    qb = QTB[gi % 2]
                    xk = ["wkv"] if gi % 2 == 1 else []
                    for c in range(KC):
                        pb = pA[c % 2]
                        pk = ("a_pA", c % 2)
                        for k in range(KC):
                            P(lambda e, k=k, c=c, pb=pb: e.matmul(pb[:, 0:n], lhsT=wq[:, k, c * 128:(c + 1) * 128], rhs=xg[:, k, 0:n],
                                                                 start=(k == 0), stop=(k == KC - 1)),
                              XG + ["wq"], [pk], signal=(k == KC - 1))
                        A(lambda e, c=c, pb=pb, qb=qb: e.copy(out=qb[:, c, 0:n], in_=pb[:, 0:n]), [pk], [("qT", gi % 2, c)] + xk)

                def do_tg(gi, c0, n, nxt):
                    tiles = list(range(c0 // 128, (c0 + n + 127) // 128))
                    qT = QTB[gi % 2]
                    xkq = ["wkv"] if gi % 2 == 1 else []
                    if c0 < NPR:
                        segs = [(ti * 128, 128, 0) for ti in range(len(tiles))]
                    else:
                        segs = [(0, 32, 1), (32, 32, 2)]
                    SBANK = [(pS, "a_pS"), (pO, "a_pO")]
                    PFB = [Pf, memt[:, 0, :].rearrange("p (a b) -> p a b", a=4)]
                    PNB = [Pn, ktm[:, 0, :].rearrange("p (a b) -> p a b", a=4)]
                    PFX = [[], ["memt"]]
                    PNX = [[], ["ktm"]]

                    def seg_scores(si):
                        lc, ln, kvi = segs[si]
                        bank, bkey = SBANK[si % 2]
                        for h in range(4):
                            for cc in range(2):
                                c = 2 * h + cc
                                P(lambda e, c=c, cc=cc, h=h, lc=lc, ln=ln, kvi=kvi, bank=bank: e.matmul(
                                    bank[h // 2][0:ln, (h % 2) * 256:(h % 2) * 256 + 256], lhsT=qT[:, c, lc:lc + ln], rhs=KT[kvi][:, c, :],
                                    start=(cc == 0), stop=(cc == 1)),
                                  [("qT", gi % 2, c), ("KT", kvi)] + xkq, [(bkey, h // 2)], signal=(cc == 1))

                    def seg_max(si):
                        lc, ln, kvi = segs[si]
                        rows = ln
                        bank, bkey = SBANK[si % 2]
                        par = si % 2
                        st0 = 4 * (si % 3)
                        for hb in range(2):
                            V(lambda e, hb=hb, rows=rows, bank=bank, st0=st0: e.tensor_reduce(
                                out=mx[0:rows, st0 + 2 * hb:st0 + 2 * hb + 2], in_=bank[hb][0:rows, :].rearrange("p (a b) -> p a b", a=2),
                                axis=AX.X, op=ALU.max),
                              [(bkey, hb)], [("mx", si % 3, hb)])
                        V(lambda e, rows=rows, st0=st0: e.tensor_scalar(out=nmx[0:rows, st0:st0 + 4], in0=mx[0:rows, st0:st0 + 4], scalar1=-1.0 / 16.0,
                                                                       scalar2=None, op0=ALU.mult),
                          [("mx", si % 3, 0), ("mx", si % 3, 1)], [("nmx", si % 3)])

                    def seg_exp(si):
                        lc, ln, kvi = segs[si]
                        rows = ln
                        bank, bkey = SBANK[si % 2]
                        par = si % 2
                        pf_ = PFB[par]
                        st0 = 4 * par
                        for h in range(4):
                            A(lambda e, h=h, rows=rows, bank=bank, st0=st0, pf_=pf_: e.activation(
                                out=pf_[0:rows, h, :], in_=bank[h // 2][0:rows, (h % 2) * 256:(h % 2) * 256 + 256],
                                func=AF.Exp, bias=nmx[0:rows, 4 * (si % 3) + h:4 * (si % 3) + h + 1], scale=1.0 / 16.0, accum_out=rsum[0:rows, st0 + h:st0 + h + 1]),
                              [(bkey, h // 2), ("nmx", si % 3)], [("Pf", par, h), ("rsum", par, h)] + PFX[par])

                    def seg_norm(si):
                        lc, ln, kvi = segs[si]
                        rows = ln
                        par = si % 2
                        pf_, pn_ = PFB[par], PNB[par]
                        st0 = 4 * par
                        RS = [("rsum", par, h) for h in range(4)]
                        V(lambda e, rows=rows, st0=st0: e.reciprocal(out=rrec[0:rows, st0:st0 + 4], in_=rsum[0:rows, st0:st0 + 4]), RS, [("rrec", par)])
                        V(lambda e, rows=rows, st0=st0, pf_=pf_, pn_=pn_: e.tensor_tensor(
                            out=pn_[0:rows, :, :], in0=pf_[0:rows, :, :],
                            in1=rrec[0:rows, st0:st0 + 4].unsqueeze(2).to_broadcast([rows, 4, NMEM]), op=ALU.mult),
                          [("Pf", par, h) for h in range(4)] + [("rrec", par)] + PFX[par], [("Pn", par)] + PNX[par])

                    def seg_transpose(si):
                        lc, ln, kvi = segs[si]
                        rows = ln
                        par = si % 2
                        pn_ = PNB[par]
                        for j in range(8):
                            P(lambda e, j=j, rows=rows, pn_=pn_, par=par: e.transpose(out=pTs[par][:, j * 128:j * 128 + rows],
                                                                                     in_=pn_[0:rows, j // 2, (j % 2) * 128:(j % 2) * 128 + 128],
                                                                                     identity=idb[0:rows, 0:rows]),
                              [("Pn", par), "idb"] + PNX[par], [("a_pT", par)], signal=(j == 7))
                        A(lambda e, lc=lc, rows=rows, par=par: e.copy(out=PnT[:, :, lc:lc + rows],
                                                                     in_=pTs[par][:, 0:1024].rearrange("p (j t) -> p j t", j=8)[:, :, 0:rows]),
                          [("a_pT", par)], [("PnT", si), "wvA"])

                    ns = len(segs)

                    def softmax_all():
                        for si in range(min(2, ns)):
                            seg_scores(si)
                            seg_max(si)
                        seg_exp(0)
                        for si in range(ns):
                            if si + 2 < ns:
                                seg_scores(si + 2)
                                seg_max(si + 2)
                            if si + 1 < ns:
                                seg_exp(si + 1)
                            seg_norm(si)
                            seg_transpose(si)

                    if nxt is not None:
                        tg_norm(*nxt)
                        lq = S.record(lambda: tg_qproj(gi + 1, *nxt))
                        lsm = S.record(softmax_all)
                        S.emit_interleaved(lq, lsm)
                    else:
                        softmax_all()
                    tiles_pnt = len(segs)
                    PNT = [("PnT", si) for si in range(tiles_pnt)]
                    if c0 < NPR:
                        osegs = [(0, n, 0)]
                    else:
                        osegs = [(0, 32, 1), (32, 32, 2)]
                    for c in range(KC):
                        h = c // 2
                        pb = pO[c % 2]
                        pk = ("a_pO", c % 2)
                        for si, (lc, ln, kvi) in enumerate(osegs):
                            for mh in range(2):
                                P(lambda e, c=c, h=h, mh=mh, lc=lc, ln=ln, kvi=kvi, pb=pb: e.matmul(
                                    pb[:, lc:lc + ln], lhsT=Vt[kvi][:, mh, c * 128:(c + 1) * 128], rhs=PnT[:, 2 * h + mh, lc:lc + ln],
                                    start=(mh == 0), stop=(mh == 1)),
                                  PNT + [("Vt", kvi)], [pk], signal=(mh == 1))
                        A(lambda e, c=c, pb=pb: e.copy(out=oT[:, c, 0:n], in_=pb[:, 0:n]), [pk], [("oT", c), "wvB"])
                    OT = [("oT", c) for c in range(KC)]
                    cnt = 0
                    for ti, t in enumerate(tiles):
                        rows = 128 if t < 16 else 64
                        for dh in range(2):
                            pb = pA[cnt % 2]
                            pk = ("a_pA", cnt % 2)
                            cnt += 1
                            for c in range(KC):
                                P(lambda e, c=c, ti=ti, rows=rows, dh=dh, pb=pb: e.matmul(pb[0:rows, :], lhsT=oT[:, c, ti * 128:ti * 128 + rows],
                                                                                         rhs=wo[:, c, dh * 512:(dh + 1) * 512], start=(c == 0), stop=(c == KC - 1)),
                                  OT + ["wo"], [pk], signal=(c == KC - 1))
                            V(lambda e, t=t, rows=rows, dh=dh, pb=pb: e.tensor_tensor(out=xres[0:rows, t, dh * 512:(dh + 1) * 512], in0=pb[0:rows, :],
                                                                                     in1=xres[0:rows, t, dh * 512:(dh + 1) * 512], op=ALU.add),
                              [pk, ("x", t)], [("x", t)])

                tg_norm(*tgs[0])
                tg_qproj(0, *tgs[0])
                for gi_, (c0, n) in enumerate(tgs):
                    do_tg(gi_, c0, n, tgs[gi_ + 1] if gi_ + 1 < len(tgs) else None)
                S.barrier()

        if os.environ.get("MK_SKIP_FFN1") != "1":
            ffn_phase("f1", n_ffn1, w1g, w1u, w1d)
        if stop_after != "ffn1":
            if os.environ.get("MK_SKIP_MIX") != "1":
                mixer_phase()
            if stop_after != "mix":
                attn_phase()
                if stop_after != "attn":
                    ffn_phase("f2", n_ffn2, w2g, w2u, w2d, final=(stop_after not in ("ffn1", "mix", "attn", "ffn2")))

        def final_phase():
            with ExitStack() as ph:
                yb = [sb(ph, f"yb{i}", [128, D], F32) for i in range(2)]
                load_gain(n_final)
                norm_stats()
                for t in range(NT):
                    par = t % 2
                    rows = tile_rows(t)
                    V(lambda e, t=t, par=par: e.scalar_tensor_tensor(out=yb[par][:], in0=xres[:, t, :], scalar=rstd[:, t:t + 1], in1=gb[:],
                                                                    op0=ALU.mult, op1=ALU.mult),
                      [("x", t), ("xpad",), ("rstd", t // 4), "gb"], [("yb", par)])
                    S.dma("sp", y_out[t * 128:t * 128 + rows, :], yb[par][0:rows, :], f"st_y{par}", reads=[("yb", par)])

        def dump_x():
            for t in range(NT):
                rows = tile_rows(t)
                S.dma("sp", y_out[t * 128:t * 128 + rows, :], xres[0:rows, t, :], "st_dbg", reads=[("x", t)])

        if stop_after in ("ffn1", "mix", "attn", "ffn2"):
            dump_x()
        S.build()
    return nc


def _consts():
    ident = np.eye(128, dtype=np.float32)
    s = np.arange(128)[:, None]
    t = np.arange(128)[None, :]
    tri64 = ((s // 64 == t // 64) & (s <= t)).astype(np.float32)
    s2 = np.arange(64)[:, None]
    t2 = np.arange(64)[None, :]
    tri32 = ((s2 // 32 == t2 // 32) & (s2 <= t2)).astype(np.float32)
    rst64 = np.ones((128, 512), np.float32)
    rst64[:, ::64] = 0.0
    rst32 = np.ones((128, 64), np.float32)
    rst32[:, ::32] = 0.0
    return dict(ident=ident, tri64=tri64, tri32=tri32, rst64=rst64, rst32=rst32)


def _fm4(v):
    return np.ascontiguousarray(np.asarray(v, np.float32).reshape(4, 128).T)


def make_in_maps(inp):
    c = _consts()
    f = lambda a: np.ascontiguousarray(np.asarray(a, dtype=np.float32))
    shared = dict(
        w1g=f(inp["ffn1_w_gate"][0]), w1u=f(inp["ffn1_w_up"][0]), w1d=f(inp["ffn1_w_down"][0]),
        w2g=f(inp["ffn2_w_gate"][0]), w2u=f(inp["ffn2_w_up"][0]), w2d=f(inp["ffn2_w_down"][0]),
        w_in=f(inp["w_in"][0]), w_out=f(inp["w_out"][0]),
        wk=f(inp["mem_wk"][0]), wv=f(inp["mem_wv"][0]), wq=f(inp["xattn_wq"][0]), wo=f(inp["xattn_wo"][0]),
        n_ffn1=f(inp["ffn1_norm"][0]), n_mix=f(inp["mix_norm"][0]), n_mem=f(inp["mem_norm"][0]),
        n_xattn=f(inp["xattn_norm"][0]), n_ffn2=f(inp["ffn2_norm"][0]), n_final=f(inp["final_norm"]),
        cw=f(np.asarray(inp["conv_dw"][0]).T),
        cb=_fm4(inp["conv_dw_b"][0]), clg=_fm4(inp["conv_ln_g"][0]), clb=_fm4(inp["conv_ln_b"][0]),
        lb2=np.ascontiguousarray(np.concatenate([_fm4(inp["hgrn_lb"][0]), _fm4(inp["hgrn_lb"][1])], axis=1)),
        gn=np.ascontiguousarray(np.asarray(inp["hgrn_norm"][0], np.float32).T),
        **c,
    )
    maps = []
    for b in range(N_CORES):
        m = dict(shared)
        m["x_in"] = np.ascontiguousarray(np.concatenate(
            [inp["x_prompt"][b], inp["x_sample"][2 * b], inp["x_sample"][2 * b + 1]], axis=0).astype(np.float32))
        m["mem_in"] = f(inp["mem_prompt"][b])
        m["st_conv"] = f(inp["state_conv"][0, 2 * b:2 * b + 2])
        m["st_hgrn"] = f(inp["state_hgrn"][0, 2 * b:2 * b + 2])
        m["ck_in"] = f(np.asarray(inp["cache_mem_k"][0, 2 * b:2 * b + 2]).reshape(2, NMEM, D))
        m["cv_in"] = f(np.asarray(inp["cache_mem_v"][0, 2 * b:2 * b + 2]).reshape(2, NMEM, D))
        maps.append(m)
    return maps


def assemble(results):
    y_p = np.stack([r["y_out"][0:NPR] for r in results])
    y_s = np.stack([r["y_out"][NPR + i * NSM:NPR + (i + 1) * NSM] for r in results for i in range(2)])
    conv_p = np.stack([r["convp_out"] for r in results])[None]
    hgrn_p = np.stack([r["hgrnp_out"] for r in results])[None]
    mk_p = np.stack([r["mk_out"].reshape(NMEM, 4, 256) for r in results])[None]
    mv_p = np.stack([r["mv_out"].reshape(NMEM, 4, 256) for r in results])[None]
    conv_s = np.concatenate([r["convs_out"] for r in results], axis=0)[None]
    hgrn_s = np.concatenate([r["hgrns_out"] for r in results], axis=0)[None]
    outs = (y_p, y_s, conv_p, hgrn_p, mk_p, mv_p, conv_s, hgrn_s)
    return tuple(np.ascontiguousarray(o.astype(np.float32)) for o in outs)


def kernel(**inputs):
    nc = build_nc(stop_after=os.environ.get("MK_STOP_AFTER"))
    in_maps = make_in_maps(inputs)
    res = run_bass_kernel_spmd(nc, in_maps, core_ids=list(range(N_CORES)))
    return assemble(res.results)
```

```python
import os
import numpy as np
from contextlib import ExitStack
import concourse.bass as bass
import concourse.mybir as mybir
from concourse.bass_utils import run_bass_kernel_spmd

F32 = mybir.dt.float32
BF16 = mybir.dt.bfloat16
AF = mybir.ActivationFunctionType
ALU = mybir.AluOpType
AX = mybir.AxisListType

D = 1024
KC = 8
FF = 2816
FC = 22
NPR = 2048
NSM = 32
T = NPR + 2 * NSM
NT = 17
TP = NT * 128
DIN = 3072
NMEM = 256
EPS = 1e-6
N_CORES = 8


class Chan:
    def __init__(self, name, sem):
        self.name = name
        self.sem = sem
        self.count = 0


class Sched:
    ENGS = ("pe", "act", "dve", "pool", "sp")

    def __init__(self, nc, stack):
        self.nc = nc
        self.stack = stack
        self.sem = {}
        self.cnt = {}
        for e in ("pe", "act", "dve", "pool"):
            self.sem[("eng", e)] = stack.enter_context(nc.semaphore("s_" + e))
            self.cnt[e] = 0
        self.prog = {e: [] for e in self.ENGS}
        self.waited = {e: {} for e in self.ENGS}
        self.res = {}
        self.chans = {}
        self.nops = {e: 0 for e in self.ENGS}
        self.rec = None

    def record(self, fn):
        assert self.rec is None
        self.rec = []
        fn()
        out, self.rec = self.rec, None
        return out

    def emit_interleaved(self, *lists):
        lists = [l for l in lists if l]
        pos = [0] * len(lists)
        while True:
            best, bf = None, None
            for i, l in enumerate(lists):
                if pos[i] < len(l):
                    f = pos[i] / len(l)
                    if bf is None or f < bf:
                        best, bf = i, f
            if best is None:
                break
            kind, args, kw = lists[best][pos[best]]
            pos[best] += 1
            getattr(self, kind)(*args, **kw)

    def chan(self, name):
        if name not in self.chans:
            sem = self.stack.enter_context(self.nc.semaphore("c_" + name))
            self.chans[name] = Chan(name, sem)
            self.sem[("chan", name)] = sem
        return self.chans[name]

    PSUM_KEYS = {"pg", "pu", "pd", "pT", "a_pT", "a_pA", "a_pS", "a_pO",
                 "c_pT", "c_pp0", "c_pp1", "c_pc0", "c_pc1", "c_pc2", "c_pc3",
                 "hb0", "hb1", "hb2", "hb3", "hb4", "hb5", "hb6", "hb7"}

    def _is_psum(self, key):
        base = key if isinstance(key, str) else key[0]
        return base in self.PSUM_KEYS

    def _deps(self, reads, writes, eng=None):
        deps = {}

        def add(k, v):
            if deps.get(k, 0) < v:
                deps[k] = v
        for r in reads:
            ent = self.res.get(r)
            if ent is not None and ent[0] is not None:
                add(*ent[0])
            if ent is not None and self._is_psum(r):
                for k, v in ent[1].items():
                    if k != ("eng", eng):
                        add(k, v)
        for w in writes:
            ent = self.res.get(w)
            if ent is not None:
                if ent[0] is not None:
                    add(*ent[0])
                for k, v in ent[1].items():
                    add(k, v)
        return deps

    def _emit_waits(self, eng, deps):
        for k, v in deps.items():
            if k == ("eng", "pe") and eng == "pe":
                continue
            if k[0] == "eng":
                assert v <= self.cnt[k[1]], f"wait on future signal {k} {v} > {self.cnt[k[1]]}"
            else:
                assert v <= self.chans[k[1]].count
            if self.waited[eng].get(k, 0) >= v:
                continue
            self.waited[eng][k] = v
            self.prog[eng].append(("wait", self.sem[k], v))

    def _record(self, tok, reads, writes):
        for r in reads:
            ent = self.res.setdefault(r, [None, {}])
            if ent[1].get(tok[0], 0) < tok[1]:
                ent[1][tok[0]] = tok[1]
        for w in writes:
            self.res[w] = [tok, {}]

    def op(self, eng, fn, reads=(), writes=(), signal=True):
        if self.rec is not None:
            self.rec.append(("op", (eng, fn), dict(reads=list(reads), writes=list(writes), signal=signal)))
            return None
        deps = self._deps(reads, writes, eng)
        self._emit_waits(eng, deps)
        if signal:
            self.cnt[eng] += 1
            tok = (("eng", eng), self.cnt[eng])
            self.prog[eng].append(("op", fn, self.sem[("eng", eng)], 1))
        else:
            tok = (("eng", eng), self.cnt[eng] + 1)
            self.prog[eng].append(("op", fn, None, 0))
        self.nops[eng] += 1
        self._record(tok, reads, writes)
        return tok

    def dma(self, q, out, in_, chan, reads=(), writes=(), **kw):
        if self.rec is not None:
            self.rec.append(("dma", (q, out, in_, chan), dict(reads=list(reads), writes=list(writes), **kw)))
            return None
        c = self.chan(chan)
        deps = self._deps(reads, writes)
        self._emit_waits(q, deps)
        c.count += 16
        tok = (("chan", chan), c.count)
        self.prog[q].append(("op", lambda e, out=out, in_=in_, kw=kw: e.dma_start(out=out, in_=in_, **kw), c.sem, 16))
        self.nops[q] += 1
        self._record(tok, reads, writes)
        return tok

    def barrier(self):
        for e in self.ENGS:
            for other in ("pe", "act", "dve", "pool"):
                if other == e or not self.cnt[other]:
                    continue
                k = ("eng", other)
                if self.waited[e].get(k, 0) < self.cnt[other]:
                    self.waited[e][k] = self.cnt[other]
                    self.prog[e].append(("wait", self.sem[k], self.cnt[other]))
            for name, c in self.chans.items():
                k = ("chan", name)
                if c.count and self.waited[e].get(k, 0) < c.count:
                    self.waited[e][k] = c.count
                    self.prog[e].append(("wait", c.sem, c.count))

    def build(self):
        for name, c in self.chans.items():
            if c.count:
                self.prog["sp"].append(("wait", c.sem, c.count))
        for e in ("pe", "act", "dve", "pool"):
            if self.cnt[e]:
                self.prog["sp"].append(("wait", self.sem[("eng", e)], self.cnt[e]))
        prog = self.prog

        def replay(name):
            def f(e):
                for ent in prog[name]:
                    if ent[0] == "wait":
                        e.wait_ge(ent[1], ent[2])
                    else:
                        ins = ent[1](e)
                        if ent[2] is not None:
                            ins.then_inc(ent[2], ent[3])
            return f

        with self.nc.Block() as block:
            block.tensor(replay("pe"))
            block.scalar(replay("act"))
            block.vector(replay("dve"))
            block.gpsimd(replay("pool"))
            block.sync(replay("sp"))


def build_nc(stop_after=None):
    nc = bass.Bass("TRN2", target_bir_lowering=False)

    def din(name, shape):
        return nc.dram_tensor(name, list(shape), F32, kind="ExternalInput").ap()

    def dout(name, shape):
        return nc.dram_tensor(name, list(shape), F32, kind="ExternalOutput").ap()

    x_in = din("x_in", [T, D])
    mem_in = din("mem_in", [NMEM, D])
    st_conv = din("st_conv", [2, 30, 512])
    st_hgrn = din("st_hgrn", [2, 4, 128, 128])
    ck_in = din("ck_in", [2, NMEM, D])
    cv_in = din("cv_in", [2, NMEM, D])
    w1g = din("w1g", [D, FF]); w1u = din("w1u", [D, FF]); w1d = din("w1d", [FF, D])
    w2g = din("w2g", [D, FF]); w2u = din("w2u", [D, FF]); w2d = din("w2d", [FF, D])
    w_in = din("w_in", [D, DIN]); w_out = din("w_out", [D, D])
    wk_d = din("wk", [D, D]); wv_d = din("wv", [D, D]); wq_d = din("wq", [D, D]); wo_d = din("wo", [D, D])
    n_ffn1 = din("n_ffn1", [D]); n_mix = din("n_mix", [D]); n_mem = din("n_mem", [D])
    n_xattn = din("n_xattn", [D]); n_ffn2 = din("n_ffn2", [D]); n_final = din("n_final", [D])
    cw_d = din("cw", [512, 31])
    cb_d = din("cb", [128, 4]); clg_d = din("clg", [128, 4]); clb_d = din("clb", [128, 4])
    lb2_d = din("lb2", [128, 8]); gn_d = din("gn", [128, 4])
    ident_d = din("ident", [128, 128])
    tri64_d = din("tri64", [128, 128]); tri32_d = din("tri32", [64, 64])
    rst64_d = din("rst64", [128, 512]); rst32_d = din("rst32", [128, 64])

    y_out = dout("y_out", [T, D])
    convp_out = dout("convp_out", [30, 512])
    hgrnp_out = dout("hgrnp_out", [4, 128, 128])
    mk_out = dout("mk_out", [NMEM, D])
    mv_out = dout("mv_out", [NMEM, D])
    convs_out = dout("convs_out", [2, 30, 512])
    hgrns_out = dout("hgrns_out", [2, 4, 128, 128])

    with ExitStack() as st:
        S = Sched(nc, st)

        def sb(ctx, name, shape, dt):
            return ctx.enter_context(nc.sbuf_tensor("sb_" + name, list(shape), dt))

        def ps(ctx, name, shape, dt=F32):
            return ctx.enter_context(nc.psum_tensor("ps_" + name, list(shape), dt))

        def A(fn, r, w, **k):
            return S.op("act", fn, reads=r, writes=w, **k)

        def V(fn, r, w, **k):
            return S.op("dve", fn, reads=r, writes=w, **k)

        def G(fn, r, w, **k):
            return S.op("pool", fn, reads=r, writes=w, **k)

        def P(fn, r, w, signal=True):
            return S.op("pe", fn, reads=r, writes=w, signal=signal)

        xres = sb(st, "xres", [128, NT, D], F32)
        gb = sb(st, "gb", [128, D], F32)
        idf = sb(st, "idf", [128, 128], F32)
        idb = sb(st, "idb", [128, 128], BF16)
        ss = sb(st, "ss", [128, NT], F32)
        lnt = sb(st, "lnt", [128, NT], F32)
        rstd = sb(st, "rstd", [128, NT], F32)
        sqj = sb(st, "sqj", [128, D], BF16)
        xnb = [sb(st, "xnb0", [128, D], BF16)] * 2

        XK = [("x", t) for t in range(NT)]

        S.dma("sp", idf[:], ident_d, "ld_idf", writes=["idf"])
        for g4 in range(4):
            S.dma("sp", xres[:, 4 * g4:4 * g4 + 4, :],
                  x_in[g4 * 512:(g4 + 1) * 512, :].rearrange("(t p) d -> p t d", p=128),
                  f"ldx{g4}", writes=[("x", 4 * g4 + i) for i in range(4)])
        V(lambda e: e.memset(xres[64:128, 16, :], 0.0), [], [("xpad",)])
        S.dma("sp", xres[0:64, 16, :], x_in[2048:2112, :], "ldx4", writes=[("x", 16)], reads=[("xpad",)])
        V(lambda e: e.tensor_copy(out=idb[:], in_=idf[:]), ["idf"], ["idb"])

        tile_rows = lambda t: 64 if t == 16 else 128

        def load_gain(g_dram):
            S.dma("sp", gb[:], g_dram.partition_broadcast(128), "ld_gb", writes=["gb"])

        def norm_stats():
            for t0 in range(0, NT, 4):
                t1 = min(NT, t0 + 4)
                ck = t0 // 4
                for t in range(t0, t1):
                    A(lambda e, t=t: e.activation(out=sqj[:], in_=xres[:, t, :], func=AF.Square, accum_out=ss[:, t:t + 1]),
                      [("x", t), ("xpad",)], ["sqj", ("ss", t)])
                A(lambda e, t0=t0, t1=t1: e.activation(out=lnt[:, t0:t1], in_=ss[:, t0:t1], func=AF.Ln, bias=EPS, scale=1.0 / D),
                  [("ss", t) for t in range(t0, t1)], [("lnt", ck)])
                A(lambda e, t0=t0, t1=t1: e.activation(out=rstd[:, t0:t1], in_=lnt[:, t0:t1], func=AF.Exp, scale=-0.5), [("lnt", ck)], [("rstd", ck)])

        def norm_tile_T(t, pT, pTkey, dst_ap, dst_keys, par, xb1=None, cp_eng="act"):
            if par == 0:
                xb, xk = xnb[0], "xnb"
            elif xb1 is not None:
                xb, xk = xb1, "xnb1"
            else:
                xb, xk = sqj, "sqj"
            V(lambda e: e.scalar_tensor_tensor(out=xb[:], in0=xres[:, t, :], scalar=rstd[:, t:t + 1], in1=gb[:],
                                               op0=ALU.mult, op1=ALU.mult),
              [("x", t), ("xpad",), ("rstd", t // 4), "gb"], [xk])
            for k in range(KC):
                P(lambda e, k=k: e.transpose(out=pT[:, k * 128:(k + 1) * 128], in_=xb[:, k * 128:(k + 1) * 128], identity=idb[:]),
                  [xk, "idb"], [pTkey], signal=(k == KC - 1))
            if cp_eng == "act":
                A(lambda e: e.copy(out=dst_ap, in_=pT[:, 0:1024].rearrange("p (k t) -> p k t", k=KC)), [pTkey], dst_keys)
            else:
                V(lambda e: e.tensor_copy(out=dst_ap, in_=pT[:, 0:1024].rearrange("p (k t) -> p k t", k=KC)), [pTkey], dst_keys)

        def ffn_phase(tag, g_dram, wg_d, wu_d, wd_d, final=False):
            GF = 4
            groups = []
            f0 = 0
            while f0 < FC:
                gf = min(GF, FC - f0)
                groups.append((f0, gf))
                f0 += gf
            NS = 2
            tgs = [(0, 512), (512, 512), (1024, 512), (1536, 512), (2048, 64)]
            with ExitStack() as ph:
                xnT = sb(ph, f"xnT_{tag}", [128, KC, TP], BF16)
                xnb1 = sb(ph, f"xnb1_{tag}", [128, D], BF16)
                wgs = [sb(ph, f"wg_{tag}{s}", [128, KC, GF * 128], BF16) for s in range(NS)]
                wus = [sb(ph, f"wu_{tag}{s}", [128, KC, GF * 128], BF16) for s in range(NS)]
                wds = [sb(ph, f"wd_{tag}{s}", [128, GF, D], BF16) for s in range(NS)]
                sg = [sb(ph, f"sg_{tag}{i}", [128, 512], F32) for i in range(2)]
                hT = [sb(ph, f"hT_{tag}{i}", [128, GF, 512], BF16) for i in range(2)]
                pT = [ps(ph, f"pT_{tag}{i}", [128, 1024], BF16) for i in range(2)]
                pg = [ps(ph, f"pg_{tag}{i}", [128, 512]) for i in range(2)]
                pu = [ps(ph, f"pu_{tag}{i}", [128, 512]) for i in range(2)]
                pd = [ps(ph, f"pd_{tag}{i}", [128, 512]) for i in range(2)]

                wg_v = wg_d.rearrange("(k p) f -> p k f", p=128)
                wu_v = wu_d.rearrange("(k p) f -> p k f", p=128)
                wd_v = wd_d.rearrange("(f p) d -> p f d", p=128)

                def load_group(gi, after=()):
                    f0, gf = groups[gi]
                    s = gi % NS
                    S.dma("pool", wgs[s][:, :, 0:gf * 128], wg_v[:, :, f0 * 128:(f0 + gf) * 128], f"ldwg{s}", writes=[("wg", tag, s)], reads=list(after))
                    S.dma("pool", wus[s][:, :, 0:gf * 128], wu_v[:, :, f0 * 128:(f0 + gf) * 128], f"ldwu{s}", writes=[("wu", tag, s)])
                    S.dma("pool", wds[s][:, 0:gf, :], wd_v[:, f0:f0 + gf, :], f"ldwd{s}", writes=[("wd", tag, s)])

                load_group(0)
                load_group(1, after=[("wg", tag, 0), ("wu", tag, 0), ("wd", tag, 0)])
                load_gain(g_dram)
                norm_stats()
                first_tiles = tgs[0][1] // 128
                for t in range(first_tiles):
                    norm_tile_T(t, pT[t % 2], ("pT", tag, t % 2), xnT[:, :, t * 128:(t + 1) * 128], [("xnT", tag, t)], t % 2, xb1=xnb1, cp_eng=("dve" if t % 2 else "act"))

                if final:
                    gb2 = sb(ph, "gb2", [128, D], F32)
                    ss2 = sb(ph, "ss2", [128, NT], F32)
                    ln2 = sb(ph, "ln2", [128, NT], F32)
                    rs2 = sb(ph, "rs2", [128, NT], F32)
                    yb = [sb(ph, f"yb{i}", [128, D], F32) for i in range(2)]
                    S.dma("sp", gb2[:], n_final.partition_broadcast(128), "ld_gb2", writes=["gb2"])

                def final_tail(ti):
                    c0, n = tgs[ti]
                    t0, t1 = c0 // 128, (c0 + n + 127) // 128
                    for t in range(t0, t1):
                        A(lambda e, t=t: e.activation(out=sqj[:], in_=xres[:, t, :], func=AF.Square, accum_out=ss2[:, t:t + 1]),
                          [("x", t)], ["sqj", ("ss2", t)])
                    A(lambda e: e.activation(out=ln2[:, t0:t1], in_=ss2[:, t0:t1], func=AF.Ln, bias=EPS, scale=1.0 / D),
                      [("ss2", t) for t in range(t0, t1)], [("ln2", ti)])
                    A(lambda e: e.activation(out=rs2[:, t0:t1], in_=ln2[:, t0:t1], func=AF.Exp, scale=-0.5), [("ln2", ti)], [("rs2", ti)])
                    for t in range(t0, t1):
                        par = t % 2
                        rows = tile_rows(t)
                        V(lambda e, t=t, par=par: e.scalar_tensor_tensor(out=yb[par][:], in0=xres[:, t, :], scalar=rs2[:, t:t + 1], in1=gb2[:],
                                                                        op0=ALU.mult, op1=ALU.mult),
                          [("x", t), ("rs2", ti), "gb2"], [("yb", par)])
                        S.dma("sp", y_out[t * 128:t * 128 + rows, :], yb[par][0:rows, :], f"st_y{par}", reads=[("yb", par)])

                items = [(gi, ti) for gi in range(len(groups)) for ti in range(len(tgs))]

                def GU(i):
                    gi, ti = items[i]
                    f0, gf = groups[gi]
                    s = gi % NS
                    c0, n = tgs[ti]
                    tiles = list(range(c0 // 128, (c0 + n + 127) // 128))
                    par = i % 2
                    rkg = [("wg", tag, s)] + [("xnT", tag, t) for t in tiles]
                    rku = [("wu", tag, s)] + [("xnT", tag, t) for t in tiles]
                    for fi in range(gf):
                        b = fi % 2
                        for k in range(KC):
                            P(lambda e, k=k, fi=fi, b=b: e.matmul(pg[b][:, 0:n], lhsT=wgs[s][:, k, fi * 128:(fi + 1) * 128],
                                                                 rhs=xnT[:, k, c0:c0 + n], start=(k == 0), stop=(k == KC - 1)),
                              rkg, [("pg", tag, b)], signal=(k == KC - 1))
                        for k in range(KC):
                            P(lambda e, k=k, fi=fi, b=b: e.matmul(pu[b][:, 0:n], lhsT=wus[s][:, k, fi * 128:(fi + 1) * 128],
                                                                 rhs=xnT[:, k, c0:c0 + n], start=(k == 0), stop=(k == KC - 1)),
                              rku, [("pu", tag, b)], signal=(k == KC - 1))
                        A(lambda e, b=b: e.activation(out=sg[b][:, 0:n], in_=pg[b][:, 0:n], func=AF.Silu),
                          [("pg", tag, b)], [("sg", tag, b)])
                        V(lambda e, b=b, fi=fi: e.tensor_tensor(out=hT[par][:, fi, 0:n], in0=sg[b][:, 0:n], in1=pu[b][:, 0:n], op=ALU.mult),
                          [("sg", tag, b), ("pu", tag, b)], [("hT", tag, par, fi)])

                dcount = [0]

                def DOWN(i):
                    gi, ti = items[i]
                    f0, gf = groups[gi]
                    s = gi % NS
                    c0, n = tgs[ti]
                    tiles = list(range(c0 // 128, (c0 + n + 127) // 128))
                    par = i % 2
                    for t in tiles:
                        lc = t * 128 - c0
                        rows = min(128, c0 + n - t * 128)
                        for dh in range(2):
                            b = dcount[0] % 2
                            dcount[0] += 1
                            for fi in range(gf):
                                P(lambda e, fi=fi, b=b, lc=lc, dh=dh, rows=rows: e.matmul(pd[b][0:rows, :], lhsT=hT[par][:, fi, lc:lc + rows],
                                                                                         rhs=wds[s][:, fi, dh * 512:(dh + 1) * 512],
                                                                                         start=(fi == 0), stop=(fi == gf - 1)),
                                  [("hT", tag, par, fi), ("wd", tag, s)], [("pd", tag, b)], signal=(fi == gf - 1))
                            V(lambda e, b=b, t=t, dh=dh, rows=rows: e.scalar_tensor_tensor(
                                out=xres[0:rows, t, dh * 512:(dh + 1) * 512], in0=pd[b][0:rows, :], scalar=0.5,
                                in1=xres[0:rows, t, dh * 512:(dh + 1) * 512], op0=ALU.mult, op1=ALU.add),
                              [("pd", tag, b), ("x", t)], [("x", t)])
                    if ti == len(tgs) - 1 and gi + NS < len(groups):
                        load_group(gi + NS)

                GU(0)
                for t in range(first_tiles, NT):
                    norm_tile_T(t, pT[t % 2], ("pT", tag, t % 2), xnT[:, :, t * 128:(t + 1) * 128], [("xnT", tag, t)], t % 2, xb1=xnb1, cp_eng=("dve" if t % 2 else "act"))
                for i in range(len(items)):
                    if i + 1 < len(items):
                        GU(i + 1)
                    DOWN(i)
                    if final and items[i][0] == len(groups) - 1:
                        final_tail(items[i][1])
                S.barrier()

        def conv_phase(ymc):
            NG = 512
            with ExitStack() as ph:
                winc = sb(ph, "winc", [128, KC, 1024], BF16)
                dg = sb(ph, "dg", [128, 124, 128], BF16)
                cw = sb(ph, "cw", [128, 4, 31], F32)
                cb = sb(ph, "cb", [128, 4], F32); clg = sb(ph, "clg", [128, 4], F32); clb = sb(ph, "clb", [128, 4], F32)
                onesln = sb(ph, "onesln", [128, 128], F32)
                xg = sb(ph, "c_xg", [128, KC, NG], BF16)
                cxnb1 = sb(ph, "c_xnb1", [128, D], BF16)
                uext = sb(ph, "uext", [128, 4, 30 + NG], F32)
                ubf = sb(ph, "ubf", [128, 4, 30 + NG], BF16)
                uextS = sb(ph, "uextS", [128, 4, 124], F32)
                sig = [sb(ph, f"sig{i}", [128, NG], F32) for i in range(2)]
                acc = sb(ph, "acc", [128, 4, NG], F32)
                ysq = sb(ph, "ysq", [128, 4, NG], F32)
                mean_sb = sb(ph, "mean_sb", [128, NG], F32)
                var_sb = sb(ph, "var_sb", [128, NG], F32)
                rs_sb = sb(ph, "rs_sb", [128, NG], F32)
                cvo = sb(ph, "cvo", [30, 512], F32)
                hist = [acc[0:30, 0, :], acc[0:30, 1, :]]

                pTs = [ps(ph, f"c_pT{i}", [128, 1024], BF16) for i in range(2)]
                pp = [ps(ph, f"c_pp{i}", [128, 512]) for i in range(2)]
                pc = [ps(ph, f"c_pc{i}", [128, 512]) for i in range(4)]
                PB = {0: (pp[0], pp[1], "c_pp0", "c_pp1"), 1: (pc[2], pc[3], "c_pc2", "c_pc3"),
                      2: (pp[0], pp[1], "c_pp0", "c_pp1"), 3: (pc[0], pc[1], "c_pc0", "c_pc1")}

                S.dma("pool", winc[:], w_in.rearrange("(k p) f -> p k f", p=128)[:, :, 0:1024], "ldwinc", writes=["winc"])
                S.dma("sp", cw[:], cw_d.rearrange("(cc p) j -> p cc j", p=128), "ld_cw", writes=["cw"])
                for (tt, dd, nm) in ((cb, cb_d, "cb"), (clg, clg_d, "clg"), (clb, clb_d, "clb")):
                    S.dma("sp", tt[:], dd, "ld_" + nm, writes=[nm])
                V(lambda e: e.memset(onesln[:], 1.0 / 512.0), [], ["onesln"])
                G(lambda e: e.memset(ubf[:, :, 0:30], 0.0), [], [("ubh",)])
                for cc in range(4):
                    V(lambda e, cc=cc: e.tensor_tensor(out=dg[:, cc * 31:(cc + 1) * 31, :],
                                                       in0=idb[:, :].unsqueeze(1).to_broadcast([128, 31, 128]),
                                                       in1=cw[:, cc, :].unsqueeze(2).to_broadcast([128, 31, 128]), op=ALU.mult),
                      ["idb", "cw"], [("dg", cc * 31 + j) for j in range(31)])
                DG = [("dg", i) for i in range(124)]
                load_gain(n_mix)
                norm_stats()

                groups = [("p", g * NG, NG, [4 * g + i for i in range(4)]) for g in range(NPR // NG)] + [("s", NPR, 64, [16])]

                def do_group(gidx, kind, c0, n, tiles, nxt_tiles):
                    if gidx == 0:
                        for ti, t in enumerate(tiles):
                            norm_tile_T(t, pTs[ti % 2], ("c_pT", ti % 2), xg[:, :, ti * 128:(ti + 1) * 128], [("c_xg", ti)], ti % 2, xb1=cxnb1, cp_eng=("dve" if ti % 2 else "act"))
                    XG = [("c_xg", ti) for ti in range(len(tiles))]
                    if kind == "s":
                        HK = [("acc", cc) for cc in range(4)]
                        for i in range(2):
                            S.dma("sp", hist[i], st_conv[i], f"ld_hist{i}", writes=HK if i == 0 else [("hist1",)])
                        for i in range(2):
                            for cc in range(4):
                                P(lambda e, i=i, cc=cc: e.transpose(out=pc[0][:, cc * 32:cc * 32 + 30], in_=hist[i][:, cc * 128:(cc + 1) * 128],
                                                                    identity=idf[0:30, 0:30]),
                                  HK + [("hist1",), "idf"], ["c_pc0"], signal=(cc == 3))
                            V(lambda e, i=i: e.tensor_copy(out=uextS[:, :, i * 62:i * 62 + 30],
                                                           in_=pc[0][:, 0:128].rearrange("p (c j) -> p c j", c=4)[:, :, 0:30]),
                              ["c_pc0"], [("uSh", i)])
                    for cc in range(4):
                        bv, bg, kv_, kg_ = PB[cc]
                        for (bank, bkey, coff) in ((bv, kv_, 0), (bg, kg_, 512)):
                            for k in range(KC):
                                P(lambda e, k=k, cc=cc, bank=bank, coff=coff: e.matmul(
                                    bank[:, 0:n], lhsT=winc[:, k, coff + cc * 128:coff + (cc + 1) * 128], rhs=xg[:, k, 0:n],
                                    start=(k == 0), stop=(k == KC - 1)),
                                  XG + ["winc"], [bkey], signal=(k == KC - 1))
                        sgi = sig[cc % 2]
                        A(lambda e, sgi=sgi, bg=bg: e.activation(out=sgi[:, 0:n], in_=bg[:, 0:n], func=AF.Sigmoid), [kg_], [("sig", cc % 2)])
                        if kind == "p":
                            V(lambda e, cc=cc, sgi=sgi, bv=bv: e.tensor_tensor(out=ubf[:, cc, 30:30 + n], in0=bv[:, 0:n], in1=sgi[:, 0:n], op=ALU.mult),
                              [kv_, ("sig", cc % 2)], [("ub", cc)])
                            if gidx == NPR // NG - 1:
                                V(lambda e, cc=cc, sgi=sgi, bv=bv: e.tensor_tensor(out=uext[:, cc, n:n + 30], in0=bv[:, n - 30:n], in1=sgi[:, n - 30:n], op=ALU.mult),
                                  [kv_, ("sig", cc % 2)], [("u", cc)])
                        else:
                            for i in range(2):
                                V(lambda e, cc=cc, sgi=sgi, i=i, bv=bv: e.tensor_tensor(out=uextS[:, cc, i * 62 + 30:i * 62 + 62], in0=bv[:, i * 32:(i + 1) * 32],
                                                                                       in1=sgi[:, i * 32:(i + 1) * 32], op=ALU.mult),
                                  [kv_, ("sig", cc % 2)], [("uS", cc, i)])
                    if kind == "p":
                        L = n
                        UB = [("ub", cc) for cc in range(4)] + [("ubh",)]
                    else:
                        L = 94
                        G(lambda e: e.tensor_copy(out=ubf[:, :, 0:124], in_=uextS[:, :, :]),
                          [("uS", cc, i) for cc in range(4) for i in range(2)] + [("uSh", 0), ("uSh", 1)], [("ub", cc) for cc in range(4)] + [("ubh",)])
                        UB = [("ub", cc) for cc in range(4)] + [("ubh",)]
                    for cc in range(4):
                        for j in range(31):
                            P(lambda e, cc=cc, j=j: e.matmul(pc[cc][:, 0:L], lhsT=dg[:, cc * 31 + j, :], rhs=ubf[:, cc, j:j + L],
                                                             start=(j == 0), stop=(j == 30)),
                              UB + [("dg", cc * 31 + j)], [f"c_pc{cc}"], signal=(j == 30))
                        if cc < len(nxt_tiles):
                            norm_tile_T(nxt_tiles[cc], pTs[cc % 2], ("c_pT", cc % 2), xg[:, :, cc * 128:(cc + 1) * 128], [("c_xg", cc)], cc % 2, xb1=cxnb1, cp_eng=("dve" if cc % 2 else "act"))
                    if kind == "p" and gidx < NPR // NG - 1:
                        G(lambda e: e.tensor_copy(out=ubf[:, :, 0:30], in_=ubf[:, :, NG:NG + 30]), [("ub", cc) for cc in range(4)], [("ubh",)])
                    for cc in range(4):
                        if kind == "p":
                            pieces = [(0, 0, n)]
                        else:
                            pieces = [(0, 0, 32), (32, 62, 32)]
                        for (d0, s0, ln) in pieces:
                            A(lambda e, cc=cc, d0=d0, s0=s0, ln=ln: e.activation(out=acc[:, cc, d0:d0 + ln], in_=pc[cc][:, s0:s0 + ln], func=AF.Identity,
                                                                               bias=cb[:, cc:cc + 1], scale=1.0),
                              [f"c_pc{cc}", "cb"], [("acc", cc)])
                            V(lambda e, cc=cc, d0=d0, ln=ln: e.tensor_tensor(out=ysq[:, cc, d0:d0 + ln], in0=acc[:, cc, d0:d0 + ln],
                                                                            in1=acc[:, cc, d0:d0 + ln], op=ALU.mult),
                              [("acc", cc)], [("ysq", cc)])
                    ACC = [("acc", cc) for cc in range(4)]
                    YSQ = [("ysq", cc) for cc in range(4)]
                    for cc in range(4):
                        P(lambda e, cc=cc: e.matmul(pp[0][:, 0:n], lhsT=onesln[:], rhs=acc[:, cc, 0:n], start=(cc == 0), stop=(cc == 3)),
                          ACC + ["onesln"], ["c_pp0"], signal=(cc == 3))
                    for cc in range(4):
                        P(lambda e, cc=cc: e.matmul(pp[1][:, 0:n], lhsT=onesln[:], rhs=ysq[:, cc, 0:n], start=(cc == 0), stop=(cc == 3)),
                          YSQ + ["onesln"], ["c_pp1"], signal=(cc == 3))
                    A(lambda e: e.copy(out=mean_sb[:, 0:n], in_=pp[0][:, 0:n]), ["c_pp0"], ["mean_sb"])
                    V(lambda e: e.tensor_tensor(out=var_sb[:, 0:n], in0=mean_sb[:, 0:n], in1=mean_sb[:, 0:n], op=ALU.mult), ["mean_sb"], ["var_sb"])
                    V(lambda e: e.tensor_tensor(out=var_sb[:, 0:n], in0=pp[1][:, 0:n], in1=var_sb[:, 0:n], op=ALU.subtract), ["c_pp1", "var_sb"], ["var_sb"])
                    A(lambda e: e.activation(out=var_sb[:, 0:n], in_=var_sb[:, 0:n], func=AF.Ln, bias=EPS, scale=1.0), ["var_sb"], ["var_sb"])
                    A(lambda e: e.activation(out=rs_sb[:, 0:n], in_=var_sb[:, 0:n], func=AF.Exp, scale=-0.5), ["var_sb"], ["rs_sb"])
                    V(lambda e: e.tensor_tensor(out=ysq[:, :, 0:n], in0=acc[:, :, 0:n], in1=mean_sb[:, 0:n].unsqueeze(1).to_broadcast([128, 4, n]),
                                                op=ALU.subtract), ACC + YSQ + ["mean_sb"], YSQ)
                    V(lambda e: e.tensor_tensor(out=ysq[:, :, 0:n], in0=ysq[:, :, 0:n], in1=rs_sb[:, 0:n].unsqueeze(1).to_broadcast([128, 4, n]),
                                                op=ALU.mult), YSQ + ["rs_sb"], YSQ)
                    for cc in range(4):
                        A(lambda e, cc=cc: e.activation(out=ymc[:, cc, c0:c0 + n], in_=ysq[:, cc, 0:n], func=AF.Silu, bias=clb[:, cc:cc + 1], scale=clg[:, cc:cc + 1]),
                          YSQ + ["clg", "clb"], [("ymc", cc, gidx)])

                    def conv_out(src_ap, dst_ap, srck):
                        for cc in range(4):
                            P(lambda e, cc=cc: e.transpose(out=pp[0][0:30, cc * 128:(cc + 1) * 128], in_=src_ap[:, cc, :], identity=idf[:]),
                              srck + ["idf"], ["c_pp0"], signal=(cc == 3))
                        V(lambda e: e.tensor_copy(out=cvo[:], in_=pp[0][0:30, :]), ["c_pp0"], ["cvo"])
                        S.dma("sp", dst_ap, cvo[:], "st_cvo", reads=["cvo"])
                    if kind == "p" and gidx == NPR // NG - 1:
                        conv_out(uext[:, :, NG:NG + 30], convp_out, [("u", cc) for cc in range(4)])
                    if kind == "s":
                        for i in range(2):
                            conv_out(uextS[:, :, i * 62 + 32:i * 62 + 62], convs_out[i], [("uS", cc, i) for cc in range(4)])

                for gidx, (kind, c0, n, tiles) in enumerate(groups):
                    do_group(gidx, kind, c0, n, tiles, groups[gidx + 1][3] if gidx + 1 < len(groups) else [])
                S.barrier()

        def hgrn_phase(ymc):
            NB = 512
            with ExitStack() as ph:
                win = sb(ph, "win", [128, KC, 2048], BF16)
                wout = sb(ph, "wout", [128, KC, D], BF16)
                lb2 = sb(ph, "lb2", [128, 8], F32); gn = sb(ph, "gn", [128, 4], F32)
                lb = sb(ph, "lb", [128, 4], F32); oml = sb(ph, "oml", [128, 4], F32); lbd = sb(ph, "lbd", [128, 4], F32)
                fc1 = sb(ph, "fc1", [128, 4], F32); fc0 = sb(ph, "fc0", [128, 4], F32)
                tri64 = sb(ph, "tri64", [128, 128], F32); tri32 = sb(ph, "tri32", [64, 64], F32)
                rstP = sb(ph, "rstP", [128, 512], F32); rstS = sb(ph, "rstS", [128, 64], F32)
                onesrm = sb(ph, "onesrm", [128, 128], F32)
                xg = sb(ph, "h_xg", [128, KC, NB], BF16)
                qd = sb(ph, "qd", [128, 4, NB], BF16)
                kd = sb(ph, "kd", [128, 4, NB], BF16)
                vtm = sb(ph, "vtm", [128, 4, 512], BF16)
                sgate = sb(ph, "sgate", [128, 4, NB], F32)
                egl = sb(ph, "egl", [128, 4, 8], F32)
                qs = [sb(ph, "qs0", [128, NB], F32)] * 2
                fgt = [sb(ph, "fgt0", [128, NB], F32)] * 2
                logf = [sb(ph, "logf0", [128, NB], F32)] * 2
                gc = [sb(ph, "gc0", [128, NB], F32)] * 2
                eg = [sb(ph, "eg0", [128, NB], F32)] * 2
                osq = [sb(ph, "osq0", [128, 4, 128], F32)] * 2
                rso = [sb(ph, "rso0", [128, 4, 128], F32)] * 2
                ymh = [sb(ph, f"ymh{i}", [128, 4, 128], BF16) for i in range(2)]
                kdT = [sb(ph, f"kdT{i}", [128, 512], BF16) for i in range(2)]
                ATm = [sb(ph, f"ATm{i}", [128, 4, 128], BF16) for i in range(2)]
                Sst = [sb(ph, f"Sst{i}", [128, 4, 128], F32) for i in range(3)]
                Sbp = [sb(ph, f"Sbp{i}", [128, 4, 128], BF16) for i in range(2)]
                SbS = [sb(ph, f"SbS{i}", [128, 4, 128], BF16) for i in range(2)]
                Rt = sb(ph, "Rt", [128, 4, 128], F32)
                f2v = lambda tns: tns[:, :, :].rearrange("p a b -> p (a b)")
                qs = [qs[0], f2v(Rt)]; qsk = ["qs", "Rt"]
                fgt = [fgt[0], f2v(osq[0])]; fgk = ["fgt", "osq"]
                logf = [logf[0], f2v(rso[0])]; lfk = ["logf", "rso"]
                gc = [gc[0], xnb[0][:, :].bitcast(F32)]; gck = ["gc", "xnb"]
                eg = [eg[0], sqj[:, :].bitcast(F32)]; egk = ["eg", "sqj"]

                bk = [ps(ph, f"hb{i}", [128, 512]) for i in range(8)]
                bkk = [f"hb{i}" for i in range(8)]
                bT = bk[0][:, :].bitcast(BF16)
                bTx = bk[7][:, :].bitcast(BF16)

                win_v = w_in.rearrange("(k p) f -> p k f", p=128)
                for part in (2, 0, 3, 1):
                    S.dma("pool", win[:, :, part * 512:(part + 1) * 512], win_v[:, :, 1024 + part * 512:1024 + (part + 1) * 512],
                          f"ldwin{part}", writes=[("win", part)], reads=([] if part == 2 else [("win", 2)]))
                S.dma("pool", wout[:], w_out.rearrange("(k p) f -> p k f", p=128), "ldwout", writes=["wout"])
                for (tt, dd, nm) in ((lb2, lb2_d, "lb2"), (gn, gn_d, "gn"), (tri64, tri64_d, "tri64"), (tri32, tri32_d, "tri32"),
                                     (rstP, rst64_d, "rstP"), (rstS, rst32_d, "rstS")):
                    S.dma("sp", tt[:], dd, "ld_" + nm, writes=[nm])
                for i in range(2):
                    S.dma("sp", Sst[1 + i][:], st_hgrn[i].rearrange("h k v -> k h v"), f"ld_S{i}", writes=[("Sst", 1 + i)])
                V(lambda e: e.memset(onesrm[:], 1.0 / 128.0), [], ["onesrm"])
                V(lambda e: e.memset(Sst[0][:], 0.0), [], [("Sst", 0)])
                V(lambda e: e.memset(Sbp[0][:], 0.0), [], [("Sbp", 0)])
                for i in range(2):
                    V(lambda e, i=i: e.tensor_copy(out=SbS[i][:], in_=Sst[1 + i][:]), [("Sst", 1 + i)], [("SbS", i)])
                V(lambda e: e.tensor_tensor(out=lbd[:], in0=lb2[:, 0:4], in1=lb2[:, 4:8], op=ALU.subtract), ["lb2"], ["lbd"])
                A(lambda e: e.activation(out=lb[:], in_=lbd[:], func=AF.Sigmoid), ["lbd"], ["lb"])
                V(lambda e: e.tensor_scalar(out=oml[:], in0=lb[:], scalar1=-1.0, scalar2=1.0, op0=ALU.mult, op1=ALU.add), ["lb"], ["oml"])
                V(lambda e: e.tensor_scalar(out=fc1[:], in0=oml[:], scalar1=0.5, scalar2=None, op0=ALU.mult), ["oml"], ["fc1"])
                V(lambda e: e.tensor_tensor(out=fc0[:], in0=lb[:], in1=fc1[:], op=ALU.add), ["lb", "fc1"], ["fc0"])
                gnb = gn[:, :].unsqueeze(2).to_broadcast([128, 4, 128])

                blocks = [("p", b * NB, NB, [4 * b + i for i in range(4)]) for b in range(NPR // NB)] + [("s", NPR, 64, [16])]
                f2 = lambda ap: ap.rearrange("p a b -> p (a b)")
                v3 = lambda bank: bank[:, :].rearrange("p (a b) -> p a b", a=4)
                jgc = [0]

                def pass_x_norm(tiles, ti):
                    norm_tile_T(tiles[ti], bTx, "hb7", xg[:, :, ti * 128:(ti + 1) * 128], [("h_xg", ti)], ti % 2)
                    rows = 128 if tiles[ti] < 16 else 64
                    for k in range(KC):
                        P(lambda e, k=k: e.matmul(bk[7][0:rows, :], lhsT=xg[:, k, ti * 128:ti * 128 + rows], rhs=win[:, k, 1024:1536],
                                                  start=(k == 0), stop=(k == KC - 1)),
                          [("h_xg", ti), ("win", 2)], ["hb7"], signal=(k == KC - 1))
                    A(lambda e: e.copy(out=vtm[0:rows, ti, :], in_=bk[7][0:rows, :]), ["hb7"], [("vtm", ti)])

                def pass_x(kind, c0, n, tiles):
                    rows = 128 if kind == "p" else 64
                    XG = [("h_xg", ti) for ti in range(len(tiles))]
                    rst, rstk = (rstP, "rstP") if kind == "p" else (rstS, "rstS")
                    csz = 64 if kind == "p" else 32
                    nch = n // csz
                    def head_ops(h):
                        hp = h % 2
                        b3 = (2, 3, 4) if hp == 0 else (5, 6, 7)
                        for (bi, coff, part) in ((b3[0], 0, 0), (b3[1], 1536, 3), (b3[2], 512, 1)):
                            for k in range(KC):
                                P(lambda e, k=k, h=h, bi=bi, coff=coff: e.matmul(bk[bi][:, 0:n], lhsT=win[:, k, coff + h * 128:coff + (h + 1) * 128],
                                                                                rhs=xg[:, k, 0:n], start=(k == 0), stop=(k == KC - 1)),
                                  XG + [("win", part)], [bkk[bi]], signal=(k == KC - 1))
                        bq, bg, bf_ = bk[b3[0]], bk[b3[1]], bk[b3[2]]
                        kq, kg, kf = bkk[b3[0]], bkk[b3[1]], bkk[b3[2]]
                        A(lambda e, hp=hp, bq=bq: e.activation(out=qs[hp][:, 0:n], in_=bq[:, 0:n], func=AF.Silu), [kq], [qsk[hp]])
                        A(lambda e, h=h, bg=bg: e.activation(out=sgate[:, h, 0:n], in_=bg[:, 0:n], func=AF.Silu), [kg], [("sgate", h)])
                        A(lambda e, hp=hp, bf_=bf_: e.activation(out=fgt[hp][:, 0:n], in_=bf_[:, 0:n], func=AF.Tanh, scale=0.5), [kf], [fgk[hp]])
                        V(lambda e, hp=hp, h=h: e.tensor_scalar(out=fgt[hp][:, 0:n], in0=fgt[hp][:, 0:n], scalar1=fc1[:, h:h + 1], scalar2=fc0[:, h:h + 1],
                                                              op0=ALU.mult, op1=ALU.add), [fgk[hp], "fc1", "fc0"], [fgk[hp]])
                        A(lambda e, hp=hp: e.activation(out=logf[hp][:, 0:n], in_=fgt[hp][:, 0:n], func=AF.Ln), [fgk[hp]], [lfk[hp]])
                        V(lambda e, hp=hp: e.tensor_tensor_scan(out=gc[hp][:, 0:n], data0=rst[:, 0:n], data1=logf[hp][:, 0:n], initial=0.0,
                                                               op0=ALU.mult, op1=ALU.add), [lfk[hp], rstk], [gck[hp]])
                        V(lambda e, hp=hp: e.tensor_scalar(out=fgt[hp][:, 0:n], in0=fgt[hp][:, 0:n], scalar1=-1.0, scalar2=1.0, op0=ALU.mult, op1=ALU.add),
                          [fgk[hp], lfk[hp]], [fgk[hp]])
                        A(lambda e, hp=hp: e.activation(out=eg[hp][:, 0:n], in_=gc[hp][:, 0:n], func=AF.Exp), [gck[hp]], [egk[hp]])
                        A(lambda e, hp=hp: e.activation(out=gc[hp][:, 0:n], in_=gc[hp][:, 0:n], func=AF.Exp, scale=-1.0), [gck[hp], egk[hp]], [gck[hp]])
                        V(lambda e, hp=hp, h=h: e.tensor_tensor(out=qd[:, h, 0:n], in0=qs[hp][:, 0:n], in1=eg[hp][:, 0:n], op=ALU.mult),
                          [qsk[hp], egk[hp]], [("qd", h)])
                        V(lambda e, hp=hp, h=h: e.tensor_tensor(out=kd[:, h, 0:n], in0=fgt[hp][:, 0:n], in1=gc[hp][:, 0:n], op=ALU.mult),
                          [fgk[hp], gck[hp]], [("kd", h)])
                        G(lambda e, hp=hp, h=h: e.tensor_copy(out=egl[:, h, 0:nch], in_=eg[hp][:, 0:n].rearrange("p (c j) -> p c j", j=csz)[:, :, csz - 1]),
                          [egk[hp]], [("egl", h)])

                    hl = [S.record(lambda h=h: head_ops(h)) for h in range(4)]
                    for l_ in hl:
                        assert len(l_) == 36, len(l_)
                    for p0 in (0, 2):
                        la, lb = hl[p0], hl[p0 + 1]
                        for part in (la[:28], lb[:28], la[28:29], lb[28:29]):
                            S.emit_interleaved(part)
                        S.emit_interleaved(la[29:], lb[29:])

                def pass_y1(kind, c0, ti, t, yp):
                    rows = 128 if kind == "p" else 64
                    lo = ti * 128
                    pAT, po, pPc = bk[1 + yp], bk[3 + yp], bk[5 + yp]
                    kAT, ko, kPc = bkk[1 + yp], bkk[3 + yp], bkk[5 + yp]
                    KD = [("kd", h) for h in range(4)]
                    QD = [("qd", h) for h in range(4)]
                    for h in range(4):
                        P(lambda e, h=h: e.transpose(out=bT[0:rows, h * 128:(h + 1) * 128], in_=kd[:, h, lo:lo + rows], identity=idb[:]),
                          KD + ["idb"], ["hb0"], signal=(h == 3))
                    A(lambda e: e.copy(out=kdT[yp][0:rows, :], in_=bT[0:rows, 0:512]), ["hb0"], [("kdT", yp)])
                    if kind == "p":
                        chunks = [(0, 64, 0), (64, 64, 0)]
                        tri, trik = tri64, "tri64"
                    else:
                        chunks = [(0, 32, 1), (32, 32, 2)]
                        tri, trik = tri32, "tri32"
                    for h in range(4):
                        P(lambda e, h=h: e.matmul(pAT[0:rows, h * 128:h * 128 + rows], lhsT=kd[:, h, lo:lo + rows], rhs=qd[:, h, lo:lo + rows],
                                                  start=True, stop=True),
                          KD + QD, [kAT], signal=(h == 3))
                    V(lambda e: e.tensor_tensor(out=ATm[yp][0:rows, :, 0:rows], in0=v3(pAT)[0:rows, :, 0:rows],
                                                in1=tri[0:rows, 0:rows].unsqueeze(1).to_broadcast([rows, 4, rows]), op=ALU.mult),
                      [kAT, trik], [("ATm", yp)])
                    for ci, (cc0, cl, sidx) in enumerate(chunks):
                        if kind == "p":
                            jg = jgc[0]
                            jgc[0] += 1
                            Sb_cur, Sbk_cur = Sbp[jg % 2], ("Sbp", jg % 2)
                            Sb_nxt, Sbk_nxt = Sbp[(jg + 1) % 2], ("Sbp", (jg + 1) % 2)
                        else:
                            Sb_cur, Sbk_cur = SbS[sidx - 1], ("SbS", sidx - 1)
                            Sb_nxt, Sbk_nxt = None, None
                        for h in range(4):
                            P(lambda e, h=h, cc0=cc0, cl=cl: e.matmul(pPc[:, h * 128:(h + 1) * 128], lhsT=kdT[yp][cc0:cc0 + cl, h * 128:(h + 1) * 128],
                                                                     rhs=vtm[cc0:cc0 + cl, ti, h * 128:(h + 1) * 128], start=True, stop=True),
                              [("kdT", yp), ("vtm", ti)], [kPc], signal=(h == 3))
                        for h in range(4):
                            P(lambda e, h=h, cc0=cc0, cl=cl: e.matmul(po[:, h * 128 + cc0:h * 128 + cc0 + cl], lhsT=vtm[0:rows, ti, h * 128:(h + 1) * 128],
                                                                     rhs=ATm[yp][0:rows, h, cc0:cc0 + cl], start=True, stop=False),
                              [("vtm", ti), ("ATm", yp)], [ko], signal=False)
                            P(lambda e, h=h, cc0=cc0, cl=cl, Sb_cur=Sb_cur: e.matmul(po[:, h * 128 + cc0:h * 128 + cc0 + cl], lhsT=Sb_cur[:, h, :],
                                                                                    rhs=qd[:, h, lo + cc0:lo + cc0 + cl], start=False, stop=True),
                              [Sbk_cur] + QD, [ko], signal=(h == 3))
                        V(lambda e, sidx=sidx: e.tensor_tensor(out=Rt[:], in0=v3(pPc), in1=Sst[sidx][:], op=ALU.add), [kPc, ("Sst", sidx)], ["Rt"])
                        cidx = (lo + cc0) // cl
                        V(lambda e, sidx=sidx, cidx=cidx: e.tensor_tensor(out=Sst[sidx][:], in0=Rt[:],
                                                                         in1=egl[:, :, cidx:cidx + 1].to_broadcast([128, 4, 128]), op=ALU.mult),
                          ["Rt"] + [("egl", h) for h in range(4)], [("Sst", sidx)])
                        if Sb_nxt is not None:
                            A(lambda e, sidx=sidx, Sb_nxt=Sb_nxt: e.copy(out=Sb_nxt[:], in_=Sst[sidx][:]), [("Sst", sidx)], [Sbk_nxt])

                def pass_y2(kind, c0, ti, t, yp, last):
                    rows = 128 if kind == "p" else 64
                    lo = ti * 128
                    pAT, po, pPc = bk[1 + yp], bk[3 + yp], bk[5 + yp]
                    kAT, ko, kPc = bkk[1 + yp], bkk[3 + yp], bkk[5 + yp]
                    SG = [("sgate", h) for h in range(4)]
                    A(lambda e: e.activation(out=osq[yp][:], in_=v3(po), func=AF.Square), [ko], ["osq"])
                    P(lambda e: e.matmul(pAT[:, :], lhsT=onesrm[:], rhs=f2(osq[yp][:]), start=True, stop=True), ["osq", "onesrm"], [kAT])
                    A(lambda e: e.activation(out=rso[yp][:], in_=v3(pAT), func=AF.Ln, bias=EPS, scale=1.0), [kAT], ["rso"])
                    A(lambda e: e.activation(out=rso[yp][:], in_=rso[yp][:], func=AF.Exp, scale=-0.5), ["rso"], ["rso"])
                    V(lambda e: e.tensor_tensor(out=osq[yp][:], in0=v3(po), in1=rso[yp][:], op=ALU.mult), [ko, "rso", "osq"], ["osq"])
                    V(lambda e: e.tensor_tensor(out=osq[yp][:], in0=osq[yp][:], in1=gnb, op=ALU.mult), ["osq", "gn"], ["osq"])
                    V(lambda e: e.tensor_tensor(out=ymh[yp][:], in0=osq[yp][:], in1=sgate[:, :, lo:lo + 128], op=ALU.mult),
                      ["osq"] + SG, [("ymh", yp)])
                    for dh, (pb, pk) in enumerate(((pAT, kAT), (pPc, kPc))):
                        for c in range(8):
                            lhs = ymc[:, c, c0 + lo:c0 + lo + rows] if c < 4 else ymh[yp][:, c - 4, 0:rows]
                            P(lambda e, c=c, lhs=lhs, dh=dh, pb=pb: e.matmul(pb[0:rows, :], lhsT=lhs, rhs=wout[:, c, dh * 512:(dh + 1) * 512],
                                                                            start=(c == 0), stop=(c == 7)),
                              [("ymh", yp), "wout"], [pk], signal=(c == 7))
                        V(lambda e, dh=dh, pb=pb: e.tensor_tensor(out=xres[0:rows, t, dh * 512:(dh + 1) * 512], in0=pb[0:rows, :],
                                                                 in1=xres[0:rows, t, dh * 512:(dh + 1) * 512], op=ALU.add),
                          [pk, ("x", t)], [("x", t)])
                    if kind == "p" and t == 15:
                        S.dma("sp", hgrnp_out.rearrange("h k v -> k h v"), Sst[0][:], "st_S0", reads=[("Sst", 0)])
                    if kind == "s":
                        for i in range(2):
                            S.dma("sp", hgrns_out[i].rearrange("h k v -> k h v"), Sst[1 + i][:], f"st_S{1 + i}", reads=[("Sst", 1 + i)])

                ytile = [0]
                for ti in range(len(blocks[0][3])):
                    pass_x_norm(blocks[0][3], ti)
                for bi, (kind, c0, n, tiles) in enumerate(blocks):
                    pass_x(kind, c0, n, tiles)
                    nxt = blocks[bi + 1][3] if bi + 1 < len(blocks) else []
                    yps = [(ytile[0] + i) % 2 for i in range(len(tiles))]
                    ytile[0] += len(tiles)
                    pass_y1(kind, c0, 0, tiles[0], yps[0])
                    for ti, t in enumerate(tiles):
                        l2 = S.record(lambda: pass_y2(kind, c0, ti, t, yps[ti], ti == len(tiles) - 1))
                        l1 = S.record(lambda: pass_y1(kind, c0, ti + 1, tiles[ti + 1], yps[ti + 1])) if ti + 1 < len(tiles) else []
                        ln_ = S.record(lambda: pass_x_norm(nxt, ti)) if ti < len(nxt) else []
                        S.emit_interleaved(l1, l2, ln_)
                S.barrier()

        def mixer_phase():
            with ExitStack() as mph:
                ymc = sb(mph, "ymc", [128, 4, TP], BF16)
                conv_phase(ymc)
                hgrn_phase(ymc)

        def attn_phase():
            with ExitStack() as ph:
                wq = sb(ph, "wq", [128, KC, D], BF16)
                wo = sb(ph, "wo", [128, KC, D], BF16)
                wkv = sb(ph, "wkv", [128, KC, D], BF16)
                memt = sb(ph, "memt", [128, 2, D], F32)
                mnT = sb(ph, "mnT", [128, KC, NMEM], BF16)
                mss = sb(ph, "mss", [128, 2], F32)
                mrs = sb(ph, "mrs", [128, 2], F32)
                KT = [sb(ph, f"KT{i}", [128, KC, NMEM], BF16) for i in range(3)]
                Vt = [sb(ph, f"Vt{i}", [128, 2, D], BF16) for i in range(3)]
                kvo = sb(ph, "kvo", [128, D], F32)
                ktm = sb(ph, "ktm", [128, 2, D], BF16)
                xg = sb(ph, "a_xg", [128, KC, 512], BF16)
                qT = sb(ph, "qT", [128, KC, 512], BF16)
                mx = sb(ph, "mx", [128, 12], F32)
                nmx = sb(ph, "nmx", [128, 12], F32)
                rsum = sb(ph, "rsum", [128, 8], F32)
                rrec = sb(ph, "rrec", [128, 8], F32)
                Pf = sb(ph, "Pf", [128, 4, NMEM], F32)
                Pn = sb(ph, "Pn", [128, 4, NMEM], BF16)
                PnT = sb(ph, "PnT", [128, 8, 512], BF16)
                oT = sb(ph, "oT", [128, KC, 512], BF16)

                pTs = [ps(ph, f"a_pT{i}", [128, 1024], BF16) for i in range(2)]
                pA = [ps(ph, f"a_pA{i}", [128, 512]) for i in range(2)]
                pS = [ps(ph, f"a_pS{i}", [128, 512]) for i in range(2)]
                pO = [ps(ph, f"a_pO{i}", [128, 512]) for i in range(2)]

                _astop = int(os.environ.get("MK_ATTN_STOP", "99"))
                wvA = PnT[:, :, :].rearrange("p a b -> p (a b)").rearrange("p (k f) -> p k f", k=4)
                wvB = oT[:, :, :].rearrange("p a b -> p (a b)").rearrange("p (k f) -> p k f", k=4)
                wv_v = wv_d.rearrange("(k p) f -> p k f", p=128)
                S.dma("sp", memt[:], mem_in.rearrange("(t p) d -> p t d", p=128), "ld_mem", writes=["memt"])
                ktmB = [ktm[:, :, :], kvo[:, :].bitcast(BF16).rearrange("p (t d) -> p t d", t=2)]
                KTK = [["ktm"], [("kvo", 0), ("kvo", 1)]]
                for i in range(2):
                    S.dma("pool", ktmB[i], ck_in[i].rearrange("(t p) d -> p t d", p=128), f"ld_ck{i}", writes=KTK[i])
                S.dma("pool", wkv[:], wk_d.rearrange("(k p) f -> p k f", p=128), "ld_wkv", writes=["wkv"])
                S.dma("pool", wvA, wv_v[:, 0:4, :], "ld_wvA", writes=["wvA"], reads=["wkv"])
                S.dma("pool", wvB, wv_v[:, 4:8, :], "ld_wvB", writes=["wvB"])
                for i in range(2):
                    S.dma("pool", Vt[1 + i][:], cv_in[i].rearrange("(t p) d -> p t d", p=128), f"ld_cv{i}", writes=[("Vt", 1 + i)])
                S.dma("pool", wq[:], wq_d.rearrange("(k p) f -> p k f", p=128), "ld_wq", writes=["wq"])
                S.dma("pool", wo[:], wo_d.rearrange("(k p) f -> p k f", p=128), "ld_wo", writes=["wo"])
                norm_stats()
                for i in range(2):
                    for mt in range(2):
                        for k in range(KC):
                            P(lambda e, k=k, mt=mt, i=i: e.transpose(out=pTs[mt % 2][:, k * 128:(k + 1) * 128], in_=ktmB[i][:, mt, k * 128:(k + 1) * 128], identity=idb[:]),
                              KTK[i] + ["idb"], [("a_pT", mt % 2)], signal=(k == KC - 1))
                        A(lambda e, mt=mt, i=i: e.copy(out=KT[1 + i][:, :, mt * 128:(mt + 1) * 128], in_=pTs[mt % 2][:, 0:1024].rearrange("p (k t) -> p k t", k=KC)),
                          [("a_pT", mt % 2)], [("KT", 1 + i)])

                if _astop <= 0:
                    S.barrier(); return
                load_gain(n_mem)
                for mt in range(2):
                    A(lambda e, mt=mt: e.activation(out=sqj[:], in_=memt[:, mt, :], func=AF.Square, accum_out=mss[:, mt:mt + 1]), ["memt"], ["sqj", "mss"])
                A(lambda e: e.activation(out=mrs[:], in_=mss[:], func=AF.Ln, bias=EPS, scale=1.0 / D), ["mss"], ["mrs"])
                A(lambda e: e.activation(out=mrs[:], in_=mrs[:], func=AF.Exp, scale=-0.5), ["mrs"], ["mrs"])
                for mt in range(2):
                    V(lambda e, mt=mt: e.scalar_tensor_tensor(out=xnb[0][:], in0=memt[:, mt, :], scalar=mrs[:, mt:mt + 1], in1=gb[:],
                                                             op0=ALU.mult, op1=ALU.mult), ["memt", "mrs", "gb"], ["xnb"])
                    for k in range(KC):
                        P(lambda e, k=k, mt=mt: e.transpose(out=pTs[mt % 2][:, k * 128:(k + 1) * 128], in_=xnb[0][:, k * 128:(k + 1) * 128], identity=idb[:]),
                          ["xnb", "idb"], [("a_pT", mt % 2)], signal=(k == KC - 1))
                    A(lambda e, mt=mt: e.copy(out=mnT[:, :, mt * 128:(mt + 1) * 128], in_=pTs[mt % 2][:, 0:1024].rearrange("p (k t) -> p k t", k=KC)),
                      [("a_pT", mt % 2)], [("mnT", mt)])
                MNT = [("mnT", 0), ("mnT", 1)]

                def kv_token_major(out_dram, dst_bf, dst_key, wsel, wkeys):
                    cnt = 0
                    for mt in range(2):
                        for dh in range(2):
                            pb = pA[cnt % 2]
                            pk = ("a_pA", cnt % 2)
                            cnt += 1
                            for k in range(KC):
                                P(lambda e, k=k, mt=mt, dh=dh, pb=pb: e.matmul(pb[:, :], lhsT=mnT[:, k, mt * 128:(mt + 1) * 128],
                                                                              rhs=wsel(k, dh), start=(k == 0), stop=(k == KC - 1)),
                                  MNT + wkeys, [pk], signal=(k == KC - 1))
                            A(lambda e, dh=dh, pb=pb: e.copy(out=kvo[:, dh * 512:(dh + 1) * 512], in_=pb[:, :]), [pk], [("kvo", dh)])
                            if dst_bf is not None:
                                V(lambda e, mt=mt, dh=dh: e.tensor_copy(out=dst_bf[:, mt, dh * 512:(dh + 1) * 512], in_=kvo[:, dh * 512:(dh + 1) * 512]),
                                  [("kvo", dh)], [dst_key])
                        S.dma("sp", out_dram[mt * 128:(mt + 1) * 128, :], kvo[:], "st_kvo", reads=[("kvo", 0), ("kvo", 1)])

                if _astop <= 1:
                    S.barrier(); return
                kv_token_major(mk_out, None, None, lambda k, dh: wkv[:, k, dh * 512:(dh + 1) * 512], ["wkv"])
                for c in range(KC):
                    pb = pA[c % 2]
                    pk = ("a_pA", c % 2)
                    for k in range(KC):
                        P(lambda e, k=k, c=c, pb=pb: e.matmul(pb[:, 0:NMEM], lhsT=wkv[:, k, c * 128:(c + 1) * 128], rhs=mnT[:, k, :],
                                                             start=(k == 0), stop=(k == KC - 1)),
                          MNT + ["wkv"], [pk], signal=(k == KC - 1))
                    A(lambda e, c=c, pb=pb: e.copy(out=KT[0][:, c, :], in_=pb[:, 0:NMEM]), [pk], [("KT", 0)])
                if _astop <= 2:
                    S.barrier(); return
                kv_token_major(mv_out, Vt[0], ("Vt", 0),
                               lambda k, dh: (wvA if k < 4 else wvB)[:, k % 4, dh * 512:(dh + 1) * 512], ["wvA", "wvB"])
                if _astop <= 3:
                    S.barrier(); return
                if _astop <= 4:
                    S.barrier(); return
                load_gain(n_xattn)
                tgs = [(0, 512), (512, 512), (1024, 512), (1536, 512), (2048, 64)]

                def tg_norm(c0, n):
                    tiles = list(range(c0 // 128, (c0 + n + 127) // 128))
                    for ti, t in enumerate(tiles):
                        norm_tile_T(t, pTs[ti % 2], ("a_pT", ti % 2), xg[:, :, ti * 128:(ti + 1) * 128], [("a_xg", ti)], ti % 2,
                                    cp_eng=("dve" if ti % 2 else "act"))

                QTB = [qT, wkv[:, :, 0:512]]

                def tg_qproj(gi, c0, n):
                    tiles = list(range(c0 // 128, (c0 + n + 127) // 128))
                    XG = [("a_xg", ti) for ti in range(len(tiles))]
                    qb = QTB[gi % 2]
                    xk = ["wkv"] if gi % 2 == 1 else []
                    for c in range(KC):
                        pb = pA[c % 2]
                        pk = ("a_pA", c % 2)
                        for k in range(KC):
                            P(lambda e, k=k, c=c, pb=pb: e.matmul(pb[:, 0:n], lhsT=wq[:, k, c * 128:(c + 1) * 128], rhs=xg[:, k, 0:n],
                                                                 start=(k == 0), stop=(k == KC - 1)),
                              XG + ["wq"], [pk], signal=(k == KC - 1))
                        A(lambda e, c=c, pb=pb, qb=qb: e.copy(out=qb[:, c, 0:n], in_=pb[:, 0:n]), [pk], [("qT", gi % 2, c)] + xk)

                def do_tg(gi, c0, n, nxt):
                    tiles = list(range(c0 // 128, (c0 + n + 127) // 128))
                    qT = QTB[gi % 2]
                    xkq = ["wkv"] if gi % 2 == 1 else []
                    if c0 < NPR:
                        segs = [(ti * 128, 128, 0) for ti in range(len(tiles))]
                    else:
                        segs = [(0, 32, 1), (32, 32, 2)]
                    SBANK = [(pS, "a_pS"), (pO, "a_pO")]
                    PFB = [Pf, memt[:, 0, :].rearrange("p (a b) -> p a b", a=4)]
                    PNB = [Pn, ktm[:, 0, :].rearrange("p (a b) -> p a b", a=4)]
                    PFX = [[], ["memt"]]
                    PNX = [[], ["ktm"]]

                    def seg_scores(si):
                        lc, ln, kvi = segs[si]
                        bank, bkey = SBANK[si % 2]
                        for h in range(4):
                            for cc in range(2):
                                c = 2 * h + cc
                                P(lambda e, c=c, cc=cc, h=h, lc=lc, ln=ln, kvi=kvi, bank=bank: e.matmul(
                                    bank[h // 2][0:ln, (h % 2) * 256:(h % 2) * 256 + 256], lhsT=qT[:, c, lc:lc + ln], rhs=KT[kvi][:, c, :],
                                    start=(cc == 0), stop=(cc == 1)),
                                  [("qT", gi % 2, c), ("KT", kvi)] + xkq, [(bkey, h // 2)], signal=(cc == 1))

                    def seg_max(si):
                        lc, ln, kvi = segs[si]
                        rows = ln
                        bank, bkey = SBANK[si % 2]
                        par = si % 2
                        st0 = 4 * (si % 3)
                        for hb in range(2):
                            V(lambda e, hb=hb, rows=rows, bank=bank, st0=st0: e.tensor_reduce(
                                out=mx[0:rows, st0 + 2 * hb:st0 + 2 * hb + 2], in_=bank[hb][0:rows, :].rearrange("p (a b) -> p a b", a=2),
                                axis=AX.X, op=ALU.max),
                              [(bkey, hb)], [("mx", si % 3, hb)])
                        V(lambda e, rows=rows, st0=st0: e.tensor_scalar(out=nmx[0:rows, st0:st0 + 4], in0=mx[0:rows, st0:st0 + 4], scalar1=-1.0 / 16.0,
                                                                       scalar2=None, op0=ALU.mult),
                          [("mx", si % 3, 0), ("mx", si % 3, 1)], [("nmx", si % 3)])

                    def seg_exp(si):
                        lc, ln, kvi = segs[si]
                        rows = ln
                        bank, bkey = SBANK[si % 2]
                        par = si % 2
                        pf_ = PFB[par]
                        st0 = 4 * par
                        for h in range(4):
                            A(lambda e, h=h, rows=rows, bank=bank, st0=st0, pf_=pf_: e.activation(
                                out=pf_[0:rows, h, :], in_=bank[h // 2][0:rows, (h % 2) * 256:(h % 2) * 256 + 256],
                                func=AF.Exp, bias=nmx[0:rows, 4 * (si % 3) + h:4 * (si % 3) + h + 1], scale=1.0 / 16.0, accum_out=rsum[0:rows, st0 + h:st0 + h + 1]),
                              [(bkey, h // 2), ("nmx", si % 3)], [("Pf", par, h), ("rsum", par, h)] + PFX[par])

                    def seg_norm(si):
                        lc, ln, kvi = segs[si]
                        rows = ln
                        par = si % 2
                        pf_, pn_ = PFB[par], PNB[par]
                        st0 = 4 * par
                        RS = [("rsum", par, h) for h in range(4)]
                        V(lambda e, rows=rows, st0=st0: e.reciprocal(out=rrec[0:rows, st0:st0 + 4], in_=rsum[0:rows, st0:st0 + 4]), RS, [("rrec", par)])
                        V(lambda e, rows=rows, st0=st0, pf_=pf_, pn_=pn_: e.tensor_tensor(
                            out=pn_[0:rows, :, :], in0=pf_[0:rows, :, :],
                            in1=rrec[0:rows, st0:st0 + 4].unsqueeze(2).to_broadcast([rows, 4, NMEM]), op=ALU.mult),
                          [("Pf", par, h) for h in range(4)] + [("rrec", par)] + PFX[par], [("Pn", par)] + PNX[par])

                    def seg_transpose(si):
                        lc, ln, kvi = segs[si]
                        rows = ln
                        par = si % 2
                        pn_ = PNB[par]
                        for j in range(8):
                            P(lambda e, j=j, rows=rows, pn_=pn_, par=par: e.transpose(out=pTs[par][:, j * 128:j * 128 + rows],
                                                                                     in_=pn_[0:rows, j // 2, (j % 2) * 128:(j % 2) * 128 + 128],
                                                                                     identity=idb[0:rows, 0:rows]),
                              [("Pn", par), "idb"] + PNX[par], [("a_pT", par)], signal=(j == 7))
                        A(lambda e, lc=lc, rows=rows, par=par: e.copy(out=PnT[:, :, lc:lc + rows],
                                                                     in_=pTs[par][:, 0:1024].rearrange("p (j t) -> p j t", j=8)[:, :, 0:rows]),
                          [("a_pT", par)], [("PnT", si), "wvA"])

                    ns = len(segs)

                    def softmax_all():
                        for si in range(min(2, ns)):
                            seg_scores(si)
                            seg_max(si)
                        seg_exp(0)
                        for si in range(ns):
                            if si + 2 < ns:
                                seg_scores(si + 2)
                                seg_max(si + 2)
                            if si + 1 < ns:
                                seg_exp(si + 1)
                            seg_norm(si)
                            seg_transpose(si)

                    if nxt is not None:
                        tg_norm(*nxt)
                        lq = S.record(lambda: tg_qproj(gi + 1, *nxt))
                        lsm = S.record(softmax_all)
                        S.emit_interleaved(lq, lsm)
                    else:
                        softmax_all()
                    tiles_pnt = len(segs)
                    PNT = [("PnT", si) for si in range(tiles_pnt)]
                    if c0 < NPR:
                        osegs = [(0, n, 0)]
                    else:
                        osegs = [(0, 32, 1), (32, 32, 2)]
                    for c in range(KC):
                        h = c // 2
                        pb = pO[c % 2]
                        pk = ("a_pO", c % 2)
                        for si, (lc, ln, kvi) in enumerate(osegs):
                            for mh in range(2):
                                P(lambda e, c=c, h=h, mh=mh, lc=lc, ln=ln, kvi=kvi, pb=pb: e.matmul(
                                    pb[:, lc:lc + ln], lhsT=Vt[kvi][:, mh, c * 128:(c + 1) * 128], rhs=PnT[:, 2 * h + mh, lc:lc + ln],
                                    start=(mh == 0), stop=(mh == 1)),
                                  PNT + [("Vt", kvi)], [pk], signal=(mh == 1))
                        A(lambda e, c=c, pb=pb: e.copy(out=oT[:, c, 0:n], in_=pb[:, 0:n]), [pk], [("oT", c), "wvB"])
                    OT = [("oT", c) for c in range(KC)]
                    cnt = 0
                    for ti, t in enumerate(tiles):
                        rows = 128 if t < 16 else 64
                        for dh in range(2):
                            pb = pA[cnt % 2]
                            pk = ("a_pA", cnt % 2)
                            cnt += 1
                            for c in range(KC):
                                P(lambda e, c=c, ti=ti, rows=rows, dh=dh, pb=pb: e.matmul(pb[0:rows, :], lhsT=oT[:, c, ti * 128:ti * 128 + rows],
                                                                                         rhs=wo[:, c, dh * 512:(dh + 1) * 512], start=(c == 0), stop=(c == KC - 1)),
                                  OT + ["wo"], [pk], signal=(c == KC - 1))
                            V(lambda e, t=t, rows=rows, dh=dh, pb=pb: e.tensor_tensor(out=xres[0:rows, t, dh * 512:(dh + 1) * 512], in0=pb[0:rows, :],
                                                                                     in1=xres[0:rows, t, dh * 512:(dh + 1) * 512], op=ALU.add),
                              [pk, ("x", t)], [("x", t)])

                tg_norm(*tgs[0])
                tg_qproj(0, *tgs[0])
                for gi_, (c0, n) in enumerate(tgs):
                    do_tg(gi_, c0, n, tgs[gi_ + 1] if gi_ + 1 < len(tgs) else None)
                S.barrier()

        if os.environ.get("MK_SKIP_FFN1") != "1":
            ffn_phase("f1", n_ffn1, w1g, w1u, w1d)
        if stop_after != "ffn1":
            if os.environ.get("MK_SKIP_MIX") != "1":
                mixer_phase()
            if stop_after != "mix":
                attn_phase()
                if stop_after != "attn":
                    ffn_phase("f2", n_ffn2, w2g, w2u, w2d, final=(stop_after not in ("ffn1", "mix", "attn", "ffn2")))

        def final_phase():
            with ExitStack() as ph:
                yb = [sb(ph, f"yb{i}", [128, D], F32) for i in range(2)]
                load_gain(n_final)
                norm_stats()
                for t in range(NT):
                    par = t % 2
                    rows = tile_rows(t)
                    V(lambda e, t=t, par=par: e.scalar_tensor_tensor(out=yb[par][:], in0=xres[:, t, :], scalar=rstd[:, t:t + 1], in1=gb[:],
                                                                    op0=ALU.mult, op1=ALU.mult),
                      [("x", t), ("xpad",), ("rstd", t // 4), "gb"], [("yb", par)])
                    S.dma("sp", y_out[t * 128:t * 128 + rows, :], yb[par][0:rows, :], f"st_y{par}", reads=[("yb", par)])

        def dump_x():
            for t in range(NT):
                rows = tile_rows(t)
                S.dma("sp", y_out[t * 128:t * 128 + rows, :], xres[0:rows, t, :], "st_dbg", reads=[("x", t)])

        if stop_after in ("ffn1", "mix", "attn", "ffn2"):
            dump_x()
        S.build()
    return nc


def _consts():
    ident = np.eye(128, dtype=np.float32)
    s = np.arange(128)[:, None]
    t = np.arange(128)[None, :]
    tri64 = ((s // 64 == t // 64) & (s <= t)).astype(np.float32)
    s2 = np.arange(64)[:, None]
    t2 = np.arange(64)[None, :]
    tri32 = ((s2 // 32 == t2 // 32) & (s2 <= t2)).astype(np.float32)
    rst64 = np.ones((128, 512), np.float32)
    rst64[:, ::64] = 0.0
    rst32 = np.ones((128, 64), np.float32)
    rst32[:, ::32] = 0.0
    return dict(ident=ident, tri64=tri64, tri32=tri32, rst64=rst64, rst32=rst32)


def _fm4(v):
    return np.ascontiguousarray(np.asarray(v, np.float32).reshape(4, 128).T)


def make_in_maps(inp):
    c = _consts()
    f = lambda a: np.ascontiguousarray(np.asarray(a, dtype=np.float32))
    shared = dict(
        w1g=f(inp["ffn1_w_gate"][0]), w1u=f(inp["ffn1_w_up"][0]), w1d=f(inp["ffn1_w_down"][0]),
        w2g=f(inp["ffn2_w_gate"][0]), w2u=f(inp["ffn2_w_up"][0]), w2d=f(inp["ffn2_w_down"][0]),
        w_in=f(inp["w_in"][0]), w_out=f(inp["w_out"][0]),
        wk=f(inp["mem_wk"][0]), wv=f(inp["mem_wv"][0]), wq=f(inp["xattn_wq"][0]), wo=f(inp["xattn_wo"][0]),
        n_ffn1=f(inp["ffn1_norm"][0]), n_mix=f(inp["mix_norm"][0]), n_mem=f(inp["mem_norm"][0]),
        n_xattn=f(inp["xattn_norm"][0]), n_ffn2=f(inp["ffn2_norm"][0]), n_final=f(inp["final_norm"]),
        cw=f(np.asarray(inp["conv_dw"][0]).T),
        cb=_fm4(inp["conv_dw_b"][0]), clg=_fm4(inp["conv_ln_g"][0]), clb=_fm4(inp["conv_ln_b"][0]),
        lb2=np.ascontiguousarray(np.concatenate([_fm4(inp["hgrn_lb"][0]), _fm4(inp["hgrn_lb"][1])], axis=1)),
        gn=np.ascontiguousarray(np.asarray(inp["hgrn_norm"][0], np.float32).T),
        **c,
    )
    maps = []
    for b in range(N_CORES):
        m = dict(shared)
        m["x_in"] = np.ascontiguousarray(np.concatenate(
            [inp["x_prompt"][b], inp["x_sample"][2 * b], inp["x_sample"][2 * b + 1]], axis=0).astype(np.float32))
        m["mem_in"] = f(inp["mem_prompt"][b])
        m["st_conv"] = f(inp["state_conv"][0, 2 * b:2 * b + 2])
        m["st_hgrn"] = f(inp["state_hgrn"][0, 2 * b:2 * b + 2])
        m["ck_in"] = f(np.asarray(inp["cache_mem_k"][0, 2 * b:2 * b + 2]).reshape(2, NMEM, D))
        m["cv_in"] = f(np.asarray(inp["cache_mem_v"][0, 2 * b:2 * b + 2]).reshape(2, NMEM, D))
        maps.append(m)
    return maps


def assemble(results):
    y_p = np.stack([r["y_out"][0:NPR] for r in results])
    y_s = np.stack([r["y_out"][NPR + i * NSM:NPR + (i + 1) * NSM] for r in results for i in range(2)])
    conv_p = np.stack([r["convp_out"] for r in results])[None]
    hgrn_p = np.stack([r["hgrnp_out"] for r in results])[None]
    mk_p = np.stack([r["mk_out"].reshape(NMEM, 4, 256) for r in results])[None]
    mv_p = np.stack([r["mv_out"].reshape(NMEM, 4, 256) for r in results])[None]
    conv_s = np.concatenate([r["convs_out"] for r in results], axis=0)[None]
    hgrn_s = np.concatenate([r["hgrns_out"] for r in results], axis=0)[None]
    outs = (y_p, y_s, conv_p, hgrn_p, mk_p, mv_p, conv_s, hgrn_s)
    return tuple(np.ascontiguousarray(o.astype(np.float32)) for o in outs)


def kernel(**inputs):
    nc = build_nc(stop_after=os.environ.get("MK_STOP_AFTER"))
    in_maps = make_in_maps(inputs)
    res = run_bass_kernel_spmd(nc, in_maps, core_ids=list(range(N_CORES)))
    return assemble(res.results)
```
